# Optimizing a Trainium2 kernel written in Bass

```python
import math
import jax, jax.numpy as jnp
from jax import lax
import numpy as np

D_MODEL = 1024
BATCH = 2
SEQ = 8192
DEPTH = 1

GRID_W = 64
HEAD_DIM = 64
A_Q_HEADS = 8
A_KV_HEADS = 2
A_GROUP = A_Q_HEADS // A_KV_HEADS
A_WIDTH = A_Q_HEADS * HEAD_DIM
B_HEADS = 4
B_V_DIM = 2 * HEAD_DIM
B_WIDTH = B_HEADS * B_V_DIM
ROPE_THETA = 10000.0
AXIAL_THETA = 10000.0
Q_BLOCK = 128
NORM_EPS = 1e-6
D_FF = -(-8 * D_MODEL // (3 * 256)) * 256

SPLIT_SIZES = (
    A_Q_HEADS * HEAD_DIM,
    A_KV_HEADS * HEAD_DIM,
    A_KV_HEADS * HEAD_DIM,
    B_HEADS * 2 * HEAD_DIM,
    B_HEADS * 2 * HEAD_DIM,
    B_HEADS * B_V_DIM,
    D_MODEL,
    D_MODEL,
)
IN_WIDTH = sum(SPLIT_SIZES)

kernel_name = "hybrid_gqa_axial_diffattn_gated_encoder"


def rms_norm(x, g):
    xf = x.astype(jnp.float32)
    y = xf * lax.rsqrt(jnp.mean(xf * xf, axis=-1, keepdims=True) + NORM_EPS)
    return (y * g.astype(jnp.float32)).astype(x.dtype)


def rope_angles(pos, dim, theta):
    inv_freq = theta ** (-jnp.arange(0, dim, 2, dtype=jnp.float32) / dim)
    return pos[:, None] * inv_freq[None, :]


def axial_angles(seq_len):
    rows = seq_len // GRID_W
    row = jnp.broadcast_to(jnp.arange(rows, dtype=jnp.float32)[:, None], (rows, GRID_W)).reshape(-1)
    col = jnp.broadcast_to(jnp.arange(GRID_W, dtype=jnp.float32)[None, :], (rows, GRID_W)).reshape(-1)
    axis_dim = HEAD_DIM // 2
    return jnp.concatenate([rope_angles(row, axis_dim, AXIAL_THETA),
                            rope_angles(col, axis_dim, AXIAL_THETA)], axis=-1)


def apply_rope(x, ang):
    cos = jnp.cos(ang).astype(x.dtype)
    sin = jnp.sin(ang).astype(x.dtype)
    half = x.shape[-1] // 2
    x1, x2 = x[..., :half], x[..., half:]
    return jnp.concatenate([x1 * cos - x2 * sin, x2 * cos + x1 * sin], axis=-1)


def split_cols(z):
    idx, acc = [], 0
    for s in SPLIT_SIZES[:-1]:
        acc += s
        idx.append(acc)
    return jnp.split(z, idx, axis=-1)


def gqa_attention(q, k, v):
    b, kvh, g, s, d = q.shape
    nb = s // Q_BLOCK
    scale = 1.0 / math.sqrt(d)
    qb = jnp.moveaxis(q.reshape(b, kvh, g, nb, Q_BLOCK, d), 3, 0)

    def block(qi):
        sc = jnp.einsum('bkgqd,bksd->bkgqs', qi, k).astype(jnp.float32) * scale
        p = jax.nn.softmax(sc, axis=-1).astype(v.dtype)
        return jnp.einsum('bkgqs,bksd->bkgqd', p, v)

    out = lax.map(block, qb)
    out = jnp.moveaxis(out, 0, 3).reshape(b, kvh * g, s, d)
    return out


def diff_attention(q, k, v, lam):
    b, h, _, s, d = q.shape
    nb = s // Q_BLOCK
    scale = 1.0 / math.sqrt(d)
    qb = jnp.moveaxis(q.reshape(b, h, 2, nb, Q_BLOCK, d), 3, 0)

    def block(qi):
        sc = jnp.einsum('bhcqd,bhcsd->bhcqs', qi, k).astype(jnp.float32) * scale
        p = jax.nn.softmax(sc, axis=-1)
        diff = (p[:, :, 0] - lam * p[:, :, 1]).astype(v.dtype)
        return jnp.einsum('bhqs,bhse->bhqe', diff, v)

    out = lax.map(block, qb)
    return jnp.moveaxis(out, 0, 2).reshape(b, h, s, v.shape[-1])


def setup_inputs(seed: int = 0) -> dict:
    key = jax.random.key(seed)
    ks = jax.random.split(key, 20)

    def nrm(k, shape, scale):
        return jax.random.normal(k, shape, jnp.float32) * scale

    def gain(k, shape):
        return 1.0 + 0.02 * jax.random.normal(k, shape, jnp.float32)

    return {
        "x": nrm(ks[0], (BATCH, SEQ, D_MODEL), 1.0),
        "norm_mix": gain(ks[1], (DEPTH, D_MODEL)),
        "w_in": nrm(ks[2], (DEPTH, D_MODEL, IN_WIDTH), D_MODEL ** -0.5),
        "q_norm_a": gain(ks[3], (DEPTH, HEAD_DIM)),
        "k_norm_a": gain(ks[4], (DEPTH, HEAD_DIM)),
        "lambda_q1": nrm(ks[5], (DEPTH, HEAD_DIM), 0.1),
        "lambda_k1": nrm(ks[6], (DEPTH, HEAD_DIM), 0.1),
        "lambda_q2": nrm(ks[7], (DEPTH, HEAD_DIM), 0.1),
        "lambda_k2": nrm(ks[8], (DEPTH, HEAD_DIM), 0.1),
        "subln_b": gain(ks[9], (DEPTH, B_V_DIM)),
        "w_proj_a": nrm(ks[10], (DEPTH, A_WIDTH, D_MODEL), A_WIDTH ** -0.5),
        "w_proj_b": nrm(ks[11], (DEPTH, B_WIDTH, D_MODEL), B_WIDTH ** -0.5),
        "w_out": nrm(ks[12], (DEPTH, D_MODEL, D_MODEL), D_MODEL ** -0.5),
        "norm_ffn": gain(ks[13], (DEPTH, D_MODEL)),
        "w_gate_ffn": nrm(ks[14], (DEPTH, D_MODEL, D_FF), D_MODEL ** -0.5),
        "w_up_ffn": nrm(ks[15], (DEPTH, D_MODEL, D_FF), D_MODEL ** -0.5),
        "w_down_ffn": nrm(ks[16], (DEPTH, D_FF, D_MODEL), D_FF ** -0.5),
        "norm_final": gain(ks[17], (D_MODEL,)),
    }


def reference(x, norm_mix, w_in, q_norm_a, k_norm_a, lambda_q1, lambda_k1, lambda_q2, lambda_k2,
              subln_b, w_proj_a, w_proj_b, w_out, norm_ffn, w_gate_ffn, w_up_ffn, w_down_ffn,
              norm_final):
    b, s, _ = x.shape
    ang_axial = axial_angles(s)
    ang_1d = rope_angles(jnp.arange(s, dtype=jnp.float32), HEAD_DIM, ROPE_THETA)

    for l in range(DEPTH):
        lambda_init = 0.8 - 0.6 * math.exp(-0.3 * l)
        h = rms_norm(x, norm_mix[l])
        z = h @ w_in[l]
        aq, ak, av, bq, bk, bv, ga, gb = split_cols(z)

        aq = aq.reshape(b, s, A_Q_HEADS, HEAD_DIM).transpose(0, 2, 1, 3)
        ak = ak.reshape(b, s, A_KV_HEADS, HEAD_DIM).transpose(0, 2, 1, 3)
        av = av.reshape(b, s, A_KV_HEADS, HEAD_DIM).transpose(0, 2, 1, 3)
        aq = apply_rope(rms_norm(aq, q_norm_a[l]), ang_axial)
        ak = apply_rope(rms_norm(ak, k_norm_a[l]), ang_axial)
        aq = aq.reshape(b, A_KV_HEADS, A_GROUP, s, HEAD_DIM)
        oa = gqa_attention(aq, ak, av)
        oa = oa.transpose(0, 2, 1, 3).reshape(b, s, A_WIDTH)

        bq = apply_rope(bq.reshape(b, s, B_HEADS, 2, HEAD_DIM).transpose(0, 2, 3, 1, 4), ang_1d)
        bk = apply_rope(bk.reshape(b, s, B_HEADS, 2, HEAD_DIM).transpose(0, 2, 3, 1, 4), ang_1d)
        bv = bv.reshape(b, s, B_HEADS, B_V_DIM).transpose(0, 2, 1, 3)
        lam = (jnp.exp(jnp.sum(lambda_q1[l].astype(jnp.float32) * lambda_k1[l].astype(jnp.float32)))
               - jnp.exp(jnp.sum(lambda_q2[l].astype(jnp.float32) * lambda_k2[l].astype(jnp.float32)))
               + lambda_init)
        ob = diff_attention(bq, bk, bv, lam)
        ob = rms_norm(ob, subln_b[l]) * (1.0 - lambda_init)
        ob = ob.transpose(0, 2, 1, 3).reshape(b, s, B_WIDTH)

        y = jax.nn.sigmoid(ga) * (oa @ w_proj_a[l]) + jax.nn.sigmoid(gb) * (ob @ w_proj_b[l])
        x = x + y @ w_out[l]

        h = rms_norm(x, norm_ffn[l])
        x = x + (jax.nn.silu(h @ w_gate_ffn[l]) * (h @ w_up_ffn[l])) @ w_down_ffn[l]

    return rms_norm(x, norm_final)
```

```python
import math
from contextlib import ExitStack

import numpy as np
import concourse.bass as bass
import concourse.mybir as mybir
from concourse.bass_utils import run_bass_kernel_spmd

F32 = mybir.dt.float32
BF16 = mybir.dt.bfloat16
AF = mybir.ActivationFunctionType
ALU = mybir.AluOpType
AX = mybir.AxisListType

S = 8192
D = 1024
NOWN = 2048
DFF = 2816
EPS = 1e-6
LAMBDA_INIT = 0.8 - 0.6 * math.exp(-0.3 * 0)
NVEC = 3 * 1024 + 2 * 64 + 256


class SemC:
    def __init__(self, nc, es, name):
        self.sem = es.enter_context(nc.semaphore(name))
        self.cnt = 0


class Eng(SemC):
    def __init__(self, nc, es, name, eng):
        super().__init__(nc, es, name)
        self.e = eng
        self.seen = {}

    def wait(self, sc, v):
        if self.seen.get(sc, 0) >= v:
            return
        self.e.wait_ge(sc.sem, v)
        self.seen[sc] = v


class Buf:
    def __init__(self, nc=None, es=None, name=None):
        self.w = {}
        self.r = {}
        self.dsem = SemC(nc, es, "d_" + name) if nc is not None else None


STRICT = True


def _pre(E, reads, writes):
    for b in reads:
        for sc, v in b.w.items():
            E.wait(sc, v)
    for b in writes:
        for sc, v in b.w.items():
            if STRICT or sc is not E:
                E.wait(sc, v)
        for sc, v in b.r.items():
            if STRICT or sc is not E:
                E.wait(sc, v)


def emit(E, fn, reads=(), writes=()):
    _pre(E, reads, writes)
    ins = fn(E.e)
    E.cnt += 1
    ins.then_inc(E.sem, 1)
    for b in reads:
        b.r[E] = E.cnt
    for b in writes:
        b.w[E] = E.cnt


def emit_group(E, fns, reads=(), writes=()):
    _pre(E, reads, writes)
    ins = None
    for fn in fns:
        ins = fn(E.e)
    E.cnt += 1
    ins.then_inc(E.sem, 1)
    for b in reads:
        b.r[E] = E.cnt
    for b in writes:
        b.w[E] = E.cnt


def emit_dma(Q, dsem, out, in_, reads=(), writes=()):
    _pre(Q, reads, writes)
    Q.e.dma_start(out=out, in_=in_).then_inc(dsem.sem, 16)
    dsem.cnt += 16
    for b in reads:
        b.r[dsem] = dsem.cnt
    for b in writes:
        b.w[dsem] = dsem.cnt


def emit_dma_batch(Q, dsem, pairs, reads=(), writes=()):
    _pre(Q, reads, writes)
    for out, in_ in pairs:
        Q.e.dma_start(out=out, in_=in_).then_inc(dsem.sem, 16)
        dsem.cnt += 16
    for b in reads:
        b.r[dsem] = dsem.cnt
    for b in writes:
        b.w[dsem] = dsem.cnt


def bc_mid(ap2d, h):
    (ps, pn), (s, n) = ap2d.ap
    return bass.AP(ap2d.tensor, ap2d.offset, [[ps, pn], [0, h], [s, n]])


def bc_last(ap2d, n):
    (ps, pn), (s, h) = ap2d.ap
    return bass.AP(ap2d.tensor, ap2d.offset, [[ps, pn], [s, h], [0, n]])


def build_program(debug=False):
    nc = bass.Bass("TRN2", target_bir_lowering=False)

    def din(name, shape, dt=F32):
        return nc.dram_tensor(name, shape, dt, kind="ExternalInput")

    x_all = din("x_all", [S, D])
    x_own = din("x_own", [NOWN, D])
    tab_all = din("tab_all", [S, 256])
    tab_own = din("tab_own", [NOWN, 256])
    w_kv = din("w_kv", [D, 1280])
    w_q = din("w_q", [D, 1024])
    w_g = din("w_g", [D, 2048])
    w_pa = din("w_pa", [512, D])
    w_pb = din("w_pb", [512, D])
    w_out = din("w_out", [D, D])
    w_gate = din("w_gate", [D, DFF])
    w_up = din("w_up", [D, DFF])
    w_down = din("w_down", [DFF, D])
    vecs = din("vecs", [1, NVEC])
    subln_col = din("subln_col", [128, 1])
    ident_in = din("ident", [128, 128])
    out = nc.dram_tensor("out", [NOWN, D], F32, kind="ExternalOutput")
    kT_scr = [nc.dram_tensor(f"kT_scr{i}", [128, S], BF16, kind="Internal") for i in range(5)]
    vA_scr = nc.dram_tensor("vA_scr", [128, 64 * 130], BF16, kind="Internal")
    vB_scr = [nc.dram_tensor(f"vB_scr{i}", [128, 64 * 128], BF16, kind="Internal") for i in range(4)]
    g_scr = nc.dram_tensor("g_scr", [NOWN, 2048], F32, kind="Internal")
    x2_scr = nc.dram_tensor("x2_scr", [NOWN, D], F32, kind="Internal")
    dbg = {}
    if debug:
        dbg["qT"] = nc.dram_tensor("dbg_qT", [128, 8 * NOWN], F32, kind="ExternalOutput")
        dbg["oaT"] = nc.dram_tensor("dbg_oaT", [64, 8 * NOWN], F32, kind="ExternalOutput")
        dbg["obT"] = nc.dram_tensor("dbg_obT", [128, 4 * NOWN], F32, kind="ExternalOutput")

    with ExitStack() as es:
        es.enter_context(nc.allow_low_precision("bf16 matmul operands, fp32 accumulation"))
        PE = Eng(nc, es, "s_pe", nc.tensor)
        ACT = Eng(nc, es, "s_act", nc.scalar)
        DVE = Eng(nc, es, "s_dve", nc.vector)
        POOL = Eng(nc, es, "s_pool", nc.gpsimd)
        SP = Eng(nc, es, "s_sp", nc.sync)

        def sb(name, shape, dt):
            return es.enter_context(nc.sbuf_tensor(name, shape, dt))

        def mkbuf(name):
            return Buf(nc, es, name)

        PS = [es.enter_context(nc.psum_tensor(f"ps{i}", [128, 1024], F32)) for i in range(4)]
        PSB = [[Buf(), Buf()] for _ in range(4)]

        vec_b = sb("vec_b", [128, NVEC], F32)
        ident_f = sb("ident_f", [128, 128], F32)
        ident = sb("ident_b", [128, 128], BF16)
        ones_f = sb("ones_f", [128, 128], F32)
        ones_b = sb("ones_b", [128, 128], BF16)
        eps_t = sb("eps_t", [128, 1], F32)
        cst = sb("cst", [128, 16], F32)
        U1 = sb("U1", [128, 8, NOWN], BF16)
        U2 = sb("U2", [128, 12 * NOWN], BF16)
        B_const = mkbuf("const")
        B_U1 = Buf()
        B_oaT = Buf()
        B_obT = Buf()
        gmix_b = vec_b[:, 0:1024]
        gffn_b = vec_b[:, 1024:2048]
        gfin_b = vec_b[:, 2048:3072]
        gq_b = vec_b[:, 3072:3136]
        gk_b = vec_b[:, 3136:3200]
        oaT = U2[0:64, 0:8 * NOWN].rearrange("p (h n) -> p h n", h=8)
        obT = U2[:, 8 * NOWN:12 * NOWN].rearrange("p (h n) -> p h n", h=4)

        vb_src = bass.AP(vecs, 0, [[0, 128], [1, NVEC]])
        emit_dma(SP, B_const.dsem, vec_b[:], vb_src, writes=[B_const])
        emit_dma(SP, B_const.dsem, ident_f[:], ident_in.ap(), writes=[B_const])
        emit_dma(SP, B_const.dsem, cst[:, 5:6], subln_col.ap(), writes=[B_const])
        B_c2 = Buf()
        emit(DVE, lambda e: e.memset(ones_f[:], 1.0), writes=[B_c2])
        emit(DVE, lambda e: e.memset(ones_b[:], 1.0), writes=[B_c2])
        emit(DVE, lambda e: e.memset(eps_t[:], EPS), writes=[B_c2])
        emit(DVE, lambda e: e.tensor_copy(ident[:], ident_f[:]), reads=[B_const], writes=[B_c2])
        lam_w = sb("lam_w", [128, 128], F32)
        B_lam = Buf()
        emit(DVE, lambda e: e.tensor_tensor(out=lam_w[:, 0:64], in0=vec_b[:, 3200:3264], in1=vec_b[:, 3264:3328], op=ALU.mult), reads=[B_const], writes=[B_lam])
        emit(DVE, lambda e: e.tensor_tensor(out=lam_w[:, 64:128], in0=vec_b[:, 3328:3392], in1=vec_b[:, 3392:3456], op=ALU.mult), reads=[B_const], writes=[B_lam])
        emit(DVE, lambda e: e.tensor_reduce(out=cst[:, 0:2], in_=lam_w[:].rearrange("p (a n) -> p a n", a=2), axis=AX.X, op=ALU.add), reads=[B_lam], writes=[B_lam])
        emit(ACT, lambda e: e.activation(out=cst[:, 2:4], in_=cst[:, 0:2], func=AF.Exp), reads=[B_lam], writes=[B_lam])
        emit(DVE, lambda e: e.tensor_tensor(out=cst[:, 4:5], in0=cst[:, 3:4], in1=cst[:, 2:3], op=ALU.subtract), reads=[B_lam], writes=[B_lam])
        emit(DVE, lambda e: e.tensor_scalar(out=cst[:, 4:5], in0=cst[:, 4:5], scalar1=-LAMBDA_INIT, scalar2=None, op0=ALU.add), reads=[B_lam], writes=[B_lam])
        emit(DVE, lambda e: e.tensor_scalar(out=cst[:, 5:6], in0=cst[:, 5:6], scalar1=1.0 - LAMBDA_INIT, scalar2=None, op0=ALU.mult), reads=[B_const, B_lam], writes=[B_lam])
        neglam_col = cst[:, 4:5]
        gcol = cst[:, 5:6]
        B_cst_all = [B_const, B_c2, B_lam]

        def wview(w, p=128):
            return w.ap().rearrange("(c p) n -> p c n", p=p)

        def pipeline(n, stages, skews):
            for s_ in range(n + max(skews)):
                for fn, sk in zip(stages, skews):
                    t_ = s_ - sk
                    if 0 <= t_ < n:
                        fn(t_)

        NXT, NXNT, NTAB = 4, 4, 8

        def phase_front(es2, pfx):
            r = {}
            r["xt"] = [es2.enter_context(nc.sbuf_tensor(pfx + f"xt{i}", [128, D], F32)) for i in range(NXT)]
            r["B_xt"] = [mkbuf(pfx + f"xt{i}") for i in range(NXT)]
            r["junk"] = es2.enter_context(nc.sbuf_tensor(pfx + "junk", [128, D], BF16))
            r["B_junk"] = Buf()
            r["st"] = [es2.enter_context(nc.sbuf_tensor(pfx + f"st{i}", [128, 4], F32)) for i in range(2)]
            r["B_st"] = [Buf(), Buf()]
            r["xn"] = [es2.enter_context(nc.sbuf_tensor(pfx + f"xn{i}", [128, D], BF16)) for i in range(2)]
            r["B_xn"] = [Buf(), Buf()]
            r["xnT"] = [es2.enter_context(nc.sbuf_tensor(pfx + f"xnT{i}", [128, 8, 128], BF16)) for i in range(NXNT)]
            r["B_xnT"] = [Buf() for _ in range(NXNT)]
            return r

        def rmsnorm_1(F, t, gain_b):
            xt, st = F["xt"][t % NXT], F["st"][t % 2]
            B_xt, B_st = F["B_xt"][t % NXT], F["B_st"][t % 2]
            xn, B_xn = F["xn"][t % 2], F["B_xn"][t % 2]
            emit(DVE, lambda e: e.memset(st[:, 0:1], 0.0), writes=[B_st])
            emit(ACT, lambda e: e.activation(out=F["junk"][:], in_=xt[:], func=AF.Square, accum_out=st[:, 0:1]),
                 reads=[B_xt, B_st], writes=[F["B_junk"], B_st])
            emit(ACT, lambda e: e.activation(out=st[:, 1:2], in_=st[:, 0:1], func=AF.Ln, scale=1.0 / D, bias=eps_t[:, 0:1]),
                 reads=[B_st, B_c2], writes=[B_st])
            emit(ACT, lambda e: e.activation(out=st[:, 2:3], in_=st[:, 1:2], func=AF.Exp, scale=-0.5),
                 reads=[B_st], writes=[B_st])
            emit(DVE, lambda e: e.scalar_tensor_tensor(out=xn[:], in0=xt[:], scalar=st[:, 2:3], in1=gain_b,
                                                       op0=ALU.mult, op1=ALU.mult),
                 reads=[B_xt, B_st, B_const], writes=[B_xn])

        def rmsnorm_2(F, t, pT_ap, B_pT):
            xn, B_xn = F["xn"][t % 2], F["B_xn"][t % 2]
            xnT, B_xnT = F["xnT"][t % NXNT], F["B_xnT"][t % NXNT]
            emit_group(PE, [(lambda e, c=c: e.transpose(pT_ap[:, c * 128:(c + 1) * 128], xn[:, c * 128:(c + 1) * 128], ident[:]))
                            for c in range(8)], reads=[B_xn, B_c2], writes=[B_pT])
            emit(ACT, lambda e: e.activation(out=xnT[:].rearrange("p c n -> p (c n)"), in_=pT_ap, func=AF.Copy),
                 reads=[B_pT], writes=[B_xnT])

        def rope(z_ap, H, tc, ts, t1, t2, B_z, B_tab, B_t, out_ap, B_out, rs=None, B_rs=None):
            z3 = z_ap.rearrange("p (h d) -> p h d", h=H)
            t13 = t1[:, 0:H * 64].rearrange("p (h d) -> p h d", h=H)
            t23 = t2[:, 0:H * 64].rearrange("p (h d) -> p h d", h=H)
            o3 = out_ap.rearrange("p (h d) -> p h d", h=H)
            B_t1, B_t2 = B_t
            emit(DVE, lambda e: e.tensor_tensor(out=t13, in0=z3, in1=bc_mid(tc, H), op=ALU.mult),
                 reads=B_z + [B_tab], writes=[B_t1])
            emit(DVE, lambda e: e.tensor_tensor(out=t23[:, :, 0:32], in0=z3[:, :, 32:64], in1=bc_mid(ts[:, 0:32], H), op=ALU.mult),
                 reads=B_z + [B_tab], writes=[B_t2[0]])
            emit(DVE, lambda e: e.tensor_tensor(out=t23[:, :, 32:64], in0=z3[:, :, 0:32], in1=bc_mid(ts[:, 32:64], H), op=ALU.mult),
                 reads=B_z + [B_tab], writes=[B_t2[1]])
            if rs is None:
                emit(DVE, lambda e: e.tensor_tensor(out=o3, in0=t13, in1=t23, op=ALU.add), reads=[B_t1] + B_t2, writes=[B_out])
            else:
                emit(DVE, lambda e: e.tensor_tensor(out=t23, in0=t13, in1=t23, op=ALU.add), reads=[B_t1] + B_t2, writes=B_t2)
                emit(DVE, lambda e: e.tensor_tensor(out=o3, in0=t23, in1=bc_last(rs, 64), op=ALU.mult),
                     reads=B_t2 + [B_rs], writes=[B_out])

        def head_rs(z_ap, H, sq, hs, B_z, B_sq, B_hs):
            emit(ACT, lambda e: e.activation(out=sq[:, 0:H * 64], in_=z_ap, func=AF.Square), reads=B_z, writes=[B_sq])
            emit(DVE, lambda e: e.tensor_reduce(out=hs[:, 0:H], in_=sq[:, 0:H * 64].rearrange("p (h d) -> p h d", h=H), axis=AX.X, op=ALU.add),
                 reads=[B_sq], writes=[B_hs])
            emit(ACT, lambda e: e.activation(out=hs[:, H:2 * H], in_=hs[:, 0:H], func=AF.Ln, scale=1.0 / 64, bias=eps_t[:, 0:1]),
                 reads=[B_hs, B_c2], writes=[B_hs])
            emit(ACT, lambda e: e.activation(out=hs[:, 2 * H:3 * H], in_=hs[:, H:2 * H], func=AF.Exp, scale=-0.5),
                 reads=[B_hs], writes=[B_hs])
            return hs[:, 2 * H:3 * H]

        pT_ap = PS[1][:, 512:1024].bitcast(BF16)
        B_pT = PSB[1][1]
        pKT_ap = PS[3][:, 512:1024].bitcast(BF16)
        B_pKT = PSB[3][1]

        def barrier(bufs, engines):
            for E in engines:
                for b in bufs:
                    for sc, v in list(b.w.items()) + list(b.r.items()):
                        if sc is not E:
                            E.wait(sc, v)

        ALL = [PE, ACT, DVE, POOL, SP]
        Wq = U2[:, 0:8192].rearrange("p (c n) -> p c n", c=8)
        Wg = U2[:, 8192:24576].rearrange("p (c n) -> p c n", c=8)
        B_Wq, B_Wg = mkbuf("Wq"), mkbuf("Wg")

        with ExitStack() as es1:
            F = phase_front(es1, "p1")
            Wkv = es1.enter_context(nc.sbuf_tensor("Wkv", [128, 8, 1280], BF16))
            B_Wkv = mkbuf("Wkv")
            emit_dma(POOL, B_Wkv.dsem, Wkv[:], wview(w_kv), writes=[B_Wkv])
            emit_dma(POOL, B_Wq.dsem, Wq, wview(w_q), writes=[B_Wq])
            emit_dma(POOL, B_Wg.dsem, Wg, wview(w_g), writes=[B_Wg])
            tabt = [es1.enter_context(nc.sbuf_tensor(f"tabt{i}", [128, 256], F32)) for i in range(NTAB)]
            B_tab = [mkbuf(f"tabt{i}") for i in range(NTAB)]
            t1 = es1.enter_context(nc.sbuf_tensor("t1", [128, 512], F32))
            t2 = es1.enter_context(nc.sbuf_tensor("t2", [128, 512], F32))
            sqs = es1.enter_context(nc.sbuf_tensor("sqs", [128, 512], F32))
            kg = es1.enter_context(nc.sbuf_tensor("kg", [128, 128], F32))
            hs = es1.enter_context(nc.sbuf_tensor("hs", [128, 24], F32))
            B_t = (Buf(), [Buf(), Buf()])
            B_kg, B_hs, B_sqs = Buf(), Buf(), Buf()
            kpost = [es1.enter_context(nc.sbuf_tensor(f"kpost{i}", [128, 640], BF16)) for i in range(2)]
            B_kpost = [[Buf(), Buf()] for _ in range(2)]
            kst = [es1.enter_context(nc.sbuf_tensor(f"kst{i}", [128, 5, 512], BF16)) for i in range(2)]
            B_kst = [mkbuf(f"kst{i}") for i in range(2)]
            vst = [es1.enter_context(nc.sbuf_tensor(f"vst{i}", [128, 4, 642], BF16)) for i in range(2)]
            B_vst = [mkbuf(f"vst{i}") for i in range(2)]
            B_kscr = [Buf(), Buf()]
            B_vscr = [Buf(), Buf()]
            for i in range(2):
                emit(DVE, lambda e, i=i: e.memset(vst[i][:], 1.0), writes=[B_vst[i]])

            def load1(t):
                emit_dma(SP, F["B_xt"][t % NXT].dsem, F["xt"][t % NXT][:], x_all.ap()[t * 128:(t + 1) * 128, :], writes=[F["B_xt"][t % NXT]])
                emit_dma(SP, B_tab[t % NTAB].dsem, tabt[t % NTAB][:], tab_all.ap()[t * 128:(t + 1) * 128, :], writes=[B_tab[t % NTAB]])

            def p1_A1(t):
                rmsnorm_1(F, t, gmix_b)

            def p1_A2(t):
                rmsnorm_2(F, t, pT_ap, B_pT)

            def p1_zs(t):
                zs = t % 2
                return PS[2 * zs], PS[2 * zs + 1], [PSB[2 * zs][0], PSB[2 * zs][1], PSB[2 * zs + 1][0]]

            def p1_B(t):
                Z0, Z1, B_z = p1_zs(t)
                xnT = F["xnT"][t % NXNT]
                fns = []
                for c in range(8):
                    fns.append(lambda e, c=c: e.matmul(Z0[:, 0:512], lhsT=xnT[:, c, :], rhs=Wkv[:, c, 0:512], start=(c == 0), stop=(c == 7)))
                    fns.append(lambda e, c=c: e.matmul(Z0[:, 512:1024], lhsT=xnT[:, c, :], rhs=Wkv[:, c, 512:1024], start=(c == 0), stop=(c == 7)))
                    fns.append(lambda e, c=c: e.matmul(Z1[:, 0:256], lhsT=xnT[:, c, :], rhs=Wkv[:, c, 1024:1280], start=(c == 0), stop=(c == 7)))
                emit_group(PE, fns, reads=[F["B_xnT"][t % NXNT], B_Wkv], writes=B_z)

            def p1_C1(t):
                Z0, Z1, B_z = p1_zs(t)
                i = t % 2
                tb_, B_tb_ = tabt[t % NTAB], B_tab[t % NTAB]
                g4, s4 = t // 4, (t // 4) % 2
                j4 = t % 4
                vrow = vst[s4][:, j4, :]
                emit(ACT, lambda e: e.activation(out=vrow[:, 0:130].rearrange("p (h d) -> p h d", h=2)[:, :, 0:64],
                                                 in_=Z0[:, 128:256].rearrange("p (h d) -> p h d", h=2), func=AF.Copy),
                     reads=[B_z[0]], writes=[B_vst[s4]])
                emit(ACT, lambda e: e.activation(out=vrow[:, 130:386], in_=Z0[:, 768:1024], func=AF.Copy), reads=[B_z[1]], writes=[B_vst[s4]])
                emit(ACT, lambda e: e.activation(out=vrow[:, 386:642], in_=Z1[:, 0:256], func=AF.Copy), reads=[B_z[2]], writes=[B_vst[s4]])
                rk = head_rs(Z0[:, 0:128], 2, sqs, hs, [B_z[0]], B_sqs, B_hs)
                emit(DVE, lambda e: e.tensor_tensor(out=kg[:].rearrange("p (h d) -> p h d", h=2), in0=Z0[:, 0:128].rearrange("p (h d) -> p h d", h=2),
                                                    in1=bc_mid(gk_b, 2), op=ALU.mult), reads=[B_z[0], B_const], writes=[B_kg])
                rope(Z0[:, 256:768], 8, tb_[:, 128:192], tb_[:, 192:256], t1, t2, [B_z[0], B_z[1]], B_tb_, B_t, kpost[i][:, 128:640], B_kpost[i][1])
                rope(kg[:], 2, tb_[:, 0:64], tb_[:, 64:128], t1, t2, [B_kg], B_tb_, B_t, kpost[i][:, 0:128], B_kpost[i][0], rs=rk, B_rs=B_hs)

            def p1_C2(t):
                i = t % 2
                g4, s4 = t // 4, (t // 4) % 2
                j4 = t % 4
                emit_group(PE, [(lambda e, b=b: e.transpose(pKT_ap[:, b * 128:(b + 1) * 128], kpost[i][:, b * 128:(b + 1) * 128], ident[:]))
                                for b in range(5)], reads=B_kpost[i] + [B_c2], writes=[B_pKT])
                emit(ACT, lambda e: e.activation(out=kst[s4][:, :, j4 * 128:(j4 + 1) * 128], in_=pKT_ap[:, 0:640].rearrange("p (b n) -> p b n", b=5), func=AF.Copy),
                     reads=[B_pKT], writes=[B_kst[s4]])
                if j4 == 3:
                    emit_dma_batch(SP, B_kst[s4].dsem,
                                   [(kT_scr[b].ap()[:, g4 * 512:(g4 + 1) * 512], kst[s4][:, b, :]) for b in range(5)],
                                   reads=[B_kst[s4]], writes=[B_kscr[g4 % 2]])
                    emit_dma_batch(SP, B_vst[s4].dsem,
                                   [(vA_scr.ap().rearrange("p (t c) -> p t c", c=130)[:, g4 * 4:(g4 + 1) * 4, :], vst[s4][:, :, 0:130])] +
                                   [(vB_scr[h].ap().rearrange("p (t c) -> p t c", c=128)[:, g4 * 4:(g4 + 1) * 4, :],
                                     vst[s4][:, :, 130 + h * 128:130 + (h + 1) * 128]) for h in range(4)],
                                   reads=[B_vst[s4]], writes=[B_vscr[g4 % 2]])

            qT_all = U1
            qg = es1.enter_context(nc.sbuf_tensor("qg", [128, 512], F32))
            B_qg = Buf()
            qpost = [es1.enter_context(nc.sbuf_tensor(f"qpost{i}", [128, 1024], BF16)) for i in range(2)]
            B_qpost = [[Buf(), Buf()] for _ in range(2)]
            gst = [es1.enter_context(nc.sbuf_tensor(f"gst{i}", [128, 2048], F32)) for i in range(2)]
            B_gst = [mkbuf(f"gst{i}") for i in range(2)]
            B_gscr = Buf()
            Zs = [PS[0], PS[2]]
            B_Zs = [PSB[0], PSB[2]]
            Gs = [PS[1][:, 0:512], PS[3][:, 0:512]]
            B_Gs = [PSB[1][0], PSB[3][0]]

            def load1b(T):
                t = T - 64
                emit_dma(SP, F["B_xt"][T % NXT].dsem, F["xt"][T % NXT][:], x_own.ap()[t * 128:(t + 1) * 128, :], writes=[F["B_xt"][T % NXT]])
                emit_dma(SP, B_tab[T % NTAB].dsem, tabt[T % NTAB][:], tab_own.ap()[t * 128:(t + 1) * 128, :], writes=[B_tab[T % NTAB]])

            def p1b_A1(T):
                t = T - 64
                rmsnorm_1(F, T, gmix_b)

            def p1b_A2(T):
                t = T - 64
                rmsnorm_2(F, T, pT_ap, B_pT)

            def p1b_B(T):
                t = T - 64
                i = T % 2
                xnT = F["xnT"][T % NXNT]
                B_xnT = F["B_xnT"][T % NXNT]
                Z, B_z = Zs[i], B_Zs[i]
                fns = []
                for c in range(8):
                    fns.append(lambda e, c=c: e.matmul(Z[:, 0:512], lhsT=xnT[:, c, :], rhs=Wq[:, c, 0:512], start=(c == 0), stop=(c == 7)))
                    fns.append(lambda e, c=c: e.matmul(Z[:, 512:1024], lhsT=xnT[:, c, :], rhs=Wq[:, c, 512:1024], start=(c == 0), stop=(c == 7)))
                emit_group(PE, fns, reads=[B_xnT, B_Wq], writes=B_z)

            def p1b_Bg(T):
                t = T - 64
                i = T % 2
                xnT = F["xnT"][T % NXNT]
                B_xnT = F["B_xnT"][T % NXNT]
                for q in range(4):
                    G, B_g = Gs[q % 2], B_Gs[q % 2]
                    emit_group(PE, [(lambda e, c=c, q=q, G=G: e.matmul(G, lhsT=xnT[:, c, :], rhs=Wg[:, c, q * 512:(q + 1) * 512],
                                                                      start=(c == 0), stop=(c == 7))) for c in range(8)],
                               reads=[B_xnT, B_Wg], writes=[B_g])
                    emit(ACT, lambda e, G=G, q=q: e.activation(out=gst[i][:, q * 512:(q + 1) * 512], in_=G, func=AF.Copy),
                         reads=[B_g], writes=[B_gst[i]])
                emit_dma(SP, B_gst[i].dsem, g_scr.ap()[t * 128:(t + 1) * 128, :], gst[i][:], reads=[B_gst[i]], writes=[B_gscr])

            def p1b_C1(T):
                t = T - 64
                i = T % 2
                Z, B_z = Zs[i], B_Zs[i]
                tb_, B_tb_ = tabt[T % NTAB], B_tab[T % NTAB]
                rq = head_rs(Z[:, 0:512], 8, sqs, hs, [B_z[0]], B_sqs, B_hs)
                emit(DVE, lambda e: e.tensor_tensor(out=qg[:].rearrange("p (h d) -> p h d", h=8), in0=Z[:, 0:512].rearrange("p (h d) -> p h d", h=8),
                                                    in1=bc_mid(gq_b, 8), op=ALU.mult), reads=[B_z[0], B_const], writes=[B_qg])
                rope(Z[:, 512:1024], 8, tb_[:, 128:192], tb_[:, 192:256], t1, t2, [B_z[1]], B_tb_, B_t, qpost[i][:, 512:1024], B_qpost[i][1])
                rope(qg[:], 8, tb_[:, 0:64], tb_[:, 64:128], t1, t2, [B_qg], B_tb_, B_t, qpost[i][:, 0:512], B_qpost[i][0], rs=rq, B_rs=B_hs)

            def p1b_C2(T):
                t = T - 64
                i = T % 2
                emit_group(PE, [(lambda e, b=b: e.transpose(pKT_ap[:, b * 128:(b + 1) * 128], qpost[i][:, b * 128:(b + 1) * 128], ident[:]))
                                for b in range(8)], reads=B_qpost[i] + [B_c2], writes=[B_pKT])
                emit(DVE, lambda e: e.tensor_copy(qT_all[:, :, t * 128:(t + 1) * 128], pKT_ap.rearrange("p (b n) -> p b n", b=8)),
                     reads=[B_pKT], writes=[B_U1])

            def both(f_kv, f_q):
                return lambda T: (f_kv(T) if T < 64 else (f_q(T) if f_q is not None else None)) if (f_kv is not None or T >= 64) else None

            pipeline(80, [both(load1, load1b), both(p1_A1, p1b_A1), both(p1_C1, p1b_C1), both(p1_A2, p1b_A2), both(p1_C2, p1b_C2),
                          both(p1_B, p1b_B), both(None, p1b_Bg)], [0, 2, 6, 3, 7, 5, 5])
            B_p1_all = ([F["B_junk"], B_t[0], B_kg, B_hs, B_sqs, B_Wkv, B_qg, B_Wq, B_Wg] + B_t[1] + F["B_xt"] + F["B_st"] + F["B_xn"] + F["B_xnT"] + B_tab
                        + B_kpost[0] + B_kpost[1] + B_kst + B_vst + B_qpost[0] + B_qpost[1] + B_gst)
        barrier(B_p1_all, ALL)
        if debug:
            B_dbg = mkbuf("dbg")

        with ExitStack() as es2:
            kTs = [es2.enter_context(nc.sbuf_tensor(f"kTs{i}", [128, S], BF16)) for i in range(2)]
            Vs = [es2.enter_context(nc.sbuf_tensor(f"Vs{i}", [128, 64 * 130], BF16)) for i in range(2)]
            B_kTs = [mkbuf(f"kTs{i}") for i in range(2)]
            B_Vs = [mkbuf(f"Vs{i}") for i in range(2)]
            NPB = 4
            p_sb = [es2.enter_context(nc.sbuf_tensor(f"p_sb{i}", [128, 1024], BF16)) for i in range(NPB)]
            B_p = [Buf() for _ in range(NPB)]
            o_f = es2.enter_context(nc.sbuf_tensor("o_f", [128, 1024], F32))
            l_f = es2.enter_context(nc.sbuf_tensor("l_f", [128, 1024], F32))
            dd = es2.enter_context(nc.sbuf_tensor("dd", [128, 1024], F32))
            sq = es2.enter_context(nc.sbuf_tensor("sq", [128, 512], F32))
            lnv = es2.enter_context(nc.sbuf_tensor("lnv", [128, 512], F32))
            rs_f = es2.enter_context(nc.sbuf_tensor("rs_f", [128, 512], F32))
            B_of, B_lf, B_dd, B_sq, B_lnv, B_rs = Buf(), Buf(), Buf(), Buf(), Buf(), Buf()
            accL1 = es2.enter_context(nc.sbuf_tensor("accL1", [128, 512], F32))
            lnl = es2.enter_context(nc.sbuf_tensor("lnl", [128, 1024], F32))
            B_lnl = Buf()
            B_acc = Buf()

            units = [("A", j) for j in range(4)] + [("B", h) for h in range(4)]
            slot_of = {0: 0, 1: 0, 2: 0, 3: 0, 4: 1, 5: 0, 6: 1, 7: 0}

            def load_unit(u):
                kind, j = units[u]
                s = slot_of[u]
                if kind == "A":
                    if j != 0:
                        return
                    emit_dma(SP, B_kTs[s].dsem, kTs[s][:], kT_scr[0].ap(), reads=B_kscr, writes=[B_kTs[s]])
                    emit_dma(SP, B_Vs[s].dsem, Vs[s][:], vA_scr.ap(), reads=B_vscr, writes=[B_Vs[s]])
                else:
                    emit_dma(SP, B_kTs[s].dsem, kTs[s][:], kT_scr[1 + j].ap(), reads=B_kscr, writes=[B_kTs[s]])
                    emit_dma(SP, B_Vs[s].dsem, Vs[s][:, 0:64 * 128], vB_scr[j].ap(), reads=B_vscr, writes=[B_Vs[s]])

            steps = [(u, qc, kt) for u in range(8) for qc in range(4) for kt in range(64)]
            NS = len(steps)
            Sbuf = [PS[0], PS[1]]
            B_S = [PSB[0], PSB[1]]
            O_ap = PS[2]
            B_O = PSB[2]
            L_ap = PS[3]
            B_L = PSB[3]

            def emit_PV(g):
                u, qc, kt = steps[g]
                kind, j = units[u]
                s = slot_of[u]
                P = p_sb[g % NPB]
                st, sp = (kt == 0), (kt == 63)
                if kind == "A":
                    V3 = Vs[s][:].rearrange("p (t c) -> p t c", c=130)
                    emit_group(PE, [
                        lambda e: e.matmul(O_ap[0:65, 0:512], lhsT=V3[:, kt, 0:65], rhs=P[:, 0:512], start=st, stop=sp),
                        lambda e: e.matmul(O_ap[0:65, 512:1024], lhsT=V3[:, kt, 65:130], rhs=P[:, 512:1024], start=st, stop=sp),
                    ], reads=[B_Vs[s], B_p[g % NPB]], writes=B_O)
                else:
                    V3 = Vs[s][:, 0:64 * 128].rearrange("p (t c) -> p t c", c=128)
                    emit_group(PE, [
                        lambda e: e.matmul(O_ap[:, 0:512], lhsT=V3[:, kt, :], rhs=P[:, 0:512], start=st, stop=sp),
                        lambda e: e.matmul(O_ap[:, 512:1024], lhsT=V3[:, kt, :], rhs=P[:, 512:1024], start=st, stop=sp),
                        lambda e: e.matmul(L_ap[:, 512:1024], lhsT=ones_b[:], rhs=P[:, 512:1024], start=st, stop=sp),
                    ], reads=[B_Vs[s], B_p[g % NPB], B_c2], writes=B_O + [B_L[1]])
                    if st:
                        emit(DVE, lambda e: e.tensor_copy(accL1[:], P[:, 0:512]), reads=[B_p[g % NPB]], writes=[B_acc])
                    else:
                        emit(DVE, lambda e: e.tensor_tensor(out=accL1[:], in0=accL1[:], in1=P[:, 0:512], op=ALU.add),
                             reads=[B_p[g % NPB], B_acc], writes=[B_acc])
                    if sp:
                        emit_group(PE, [lambda e: e.matmul(L_ap[:, 0:512], lhsT=ones_f[:], rhs=accL1[:], start=True, stop=True)],
                                   reads=[B_acc, B_c2], writes=[B_L[0]])

            def fin_evac(u, qc):
                kind, j = units[u]
                if kind == "A":
                    emit(DVE, lambda e: e.tensor_copy(o_f[0:65, :], O_ap[0:65, :]), reads=B_O, writes=[B_of])
                    emit(DVE, lambda e: e.reciprocal(out=l_f[64:65, :], in_=o_f[64:65, :]), reads=[B_of], writes=[B_lf])
                else:
                    emit(DVE, lambda e: e.tensor_copy(o_f[:], O_ap[:]), reads=B_O, writes=[B_of])
                    emit(DVE, lambda e: e.tensor_copy(l_f[:], L_ap[:]), reads=B_L, writes=[B_lf])

            def fin_B2():
                emit(ACT, lambda e: e.activation(out=lnl[:], in_=l_f[:], func=AF.Ln), reads=[B_lf], writes=[B_lnl])
                emit(ACT, lambda e: e.activation(out=l_f[:], in_=lnl[:], func=AF.Exp, scale=-1.0), reads=[B_lnl], writes=[B_lf])

            def fin_B3():
                emit(DVE, lambda e: e.tensor_tensor(out=dd[:, 0:512], in0=o_f[:, 0:512], in1=l_f[:, 0:512], op=ALU.mult),
                     reads=[B_of, B_lf], writes=[B_dd])
                emit(DVE, lambda e: e.scalar_tensor_tensor(out=dd[:, 512:1024], in0=o_f[:, 512:1024], scalar=neglam_col, in1=l_f[:, 512:1024],
                                                           op0=ALU.mult, op1=ALU.mult), reads=[B_of, B_lf, B_lam], writes=[B_dd])
                emit(DVE, lambda e: e.tensor_tensor(out=dd[:, 0:512], in0=dd[:, 0:512], in1=dd[:, 512:1024], op=ALU.add), reads=[B_dd], writes=[B_dd])
                emit(DVE, lambda e: e.tensor_tensor(out=sq[:], in0=dd[:, 0:512], in1=dd[:, 0:512], op=ALU.mult), reads=[B_dd], writes=[B_sq])

            def fin_pe(u, qc, gslot):
                kind, j = units[u]
                if kind == "A":
                    Sg = Sbuf[gslot] if gslot is not None else L_ap
                    B_Sg = B_S[gslot] if gslot is not None else B_L
                    emit_group(PE, [
                        lambda e: e.matmul(Sg[0:64, 0:512], lhsT=ones_f[64:65, 0:64], rhs=l_f[64:65, 0:512], start=True, stop=True),
                        lambda e: e.matmul(Sg[0:64, 512:1024], lhsT=ones_f[64:65, 0:64], rhs=l_f[64:65, 512:1024], start=True, stop=True),
                    ], reads=[B_lf, B_c2], writes=B_Sg)
                    for hh, col in ((j, 0), (j + 4, 512)):
                        emit(DVE, lambda e, hh=hh, col=col: e.tensor_tensor(out=oaT[:, hh, qc * 512:(qc + 1) * 512], in0=o_f[0:64, col:col + 512],
                                                                           in1=Sg[0:64, col:col + 512], op=ALU.mult),
                             reads=[B_of] + B_Sg, writes=[B_oaT])
                else:
                    Sg = Sbuf[gslot]
                    emit_group(PE, [lambda e: e.matmul(Sg[:, 0:512], lhsT=ones_f[:], rhs=sq[:], start=True, stop=True)],
                               reads=[B_sq, B_c2], writes=B_S[gslot])
                    emit(ACT, lambda e: e.activation(out=lnv[:], in_=Sg[:, 0:512], func=AF.Ln, scale=1.0 / 128, bias=eps_t[:, 0:1]),
                         reads=B_S[gslot] + [B_c2], writes=[B_lnv])
                    emit(ACT, lambda e: e.activation(out=rs_f[:], in_=lnv[:], func=AF.Exp, scale=-0.5), reads=[B_lnv], writes=[B_rs])
                    emit(DVE, lambda e: e.scalar_tensor_tensor(out=obT[:, j, qc * 512:(qc + 1) * 512], in0=dd[:, 0:512], scalar=gcol, in1=rs_f[:],
                                                               op0=ALU.mult, op1=ALU.mult), reads=[B_dd, B_rs, B_lam], writes=[B_obT])

            load_unit(0)
            load_unit(4)
            DEFER = 14
            free_S = [0, 1]
            Sphys = {}
            pending = []
            next_S = 0

            def issue_S(g):
                sl = free_S.pop(0)
                Sphys[g] = sl
                u, qc, kt = steps[g]
                kind, j = units[u]
                s = slot_of[u]
                blk = j if kind == "A" else 4 + j
                Sg = Sbuf[sl]
                q0 = qT_all[0:64, blk, qc * 512:(qc + 1) * 512]
                q1 = qT_all[64:128, blk, qc * 512:(qc + 1) * 512]
                emit_group(PE, [
                    lambda e: e.matmul(Sg[:, 0:512], lhsT=kTs[s][0:64, kt * 128:(kt + 1) * 128], rhs=q0, start=True, stop=True),
                    lambda e: e.matmul(Sg[:, 512:1024], lhsT=kTs[s][64:128, kt * 128:(kt + 1) * 128], rhs=q1, start=True, stop=True),
                ], reads=[B_kTs[s], B_U1], writes=B_S[sl])

            def _emit_exp(g):
                sl = Sphys[g]
                Sg = Sbuf[sl]
                emit(ACT, lambda e: e.activation(out=p_sb[g % NPB][:], in_=Sg[:], func=AF.Exp, scale=0.125),
                     reads=B_S[sl], writes=[B_p[g % NPB]])
                free_S.append(sl)

            def run_pending(g):
                pending.sort(key=lambda x: x[0])
                while pending and pending[0][0] <= g:
                    _, what, pu, pqc = pending.pop(0)
                    if what == "B2":
                        fin_B2()
                    elif what == "B3":
                        fin_B3()
                    elif units[pu][0] == "A" and not (pu == 3 and pqc == 3):
                        fin_pe(pu, pqc, None)
                    else:
                        sl = free_S.pop(0)
                        fin_pe(pu, pqc, sl)
                        free_S.append(sl)

            issue_S(0)
            issue_S(1)
            next_S = 2
            for g in range(NS):
                u, qc, kt = steps[g]
                _emit_exp(g)
                run_pending(g)
                while next_S < NS and next_S <= g + 2 and free_S:
                    issue_S(next_S)
                    next_S += 1
                emit_PV(g)
                if kt == 63:
                    fin_evac(u, qc)
                    if units[u][0] == "B":
                        pending.append((g + 3, "B2", u, qc))
                        pending.append((g + 5, "B3", u, qc))
                    pending.append((g + DEFER, "PE", u, qc))
                    if qc == 3:
                        if u == 3:
                            load_unit(5)
                        elif u == 4:
                            load_unit(6)
                        elif u == 5:
                            load_unit(7)
            run_pending(NS + DEFER + 1)
            B_p2_all = B_kTs + B_Vs + B_p + [B_of, B_lf, B_dd, B_sq, B_lnv, B_rs, B_acc, B_lnl]
        barrier(B_p2_all, ALL)

        if debug:
            dq = sb("dq", [128, 2048], F32)
            for b in range(8):
                emit(DVE, lambda e, b=b: e.tensor_copy(dq[:], qT_all[:, b, :]), reads=[B_U1], writes=[B_dbg])
                emit_dma(SP, B_dbg.dsem, dbg["qT"].ap()[:, b * NOWN:(b + 1) * NOWN], dq[:], reads=[B_dbg])
                emit(DVE, lambda e, b=b: e.tensor_copy(dq[0:64, :], oaT[:, b, :]), reads=[B_oaT], writes=[B_dbg])
                emit_dma(SP, B_dbg.dsem, dbg["oaT"].ap()[:, b * NOWN:(b + 1) * NOWN], dq[0:64, :], reads=[B_dbg])
            for b in range(4):
                emit(DVE, lambda e, b=b: e.tensor_copy(dq[:], obT[:, b, :]), reads=[B_obT], writes=[B_dbg])
                emit_dma(SP, B_dbg.dsem, dbg["obT"].ap()[:, b * NOWN:(b + 1) * NOWN], dq[:], reads=[B_dbg])

        h2T = U1
        pT3_ap = PS[3][:, 0:512].bitcast(BF16)
        B_pT3 = PSB[3][0]
        with ExitStack() as es3:
            Wpa = es3.enter_context(nc.sbuf_tensor("Wpa", [64, 8, D], BF16))
            Wpb = es3.enter_context(nc.sbuf_tensor("Wpb", [128, 4, D], BF16))
            Wout = es3.enter_context(nc.sbuf_tensor("Wout", [128, 8, D], BF16))
            B_Wpa, B_Wpb, B_Wout = mkbuf("Wpa"), mkbuf("Wpb"), mkbuf("Wout")
            emit_dma(POOL, B_Wpa.dsem, Wpa[:], wview(w_pa, 64), writes=[B_Wpa])
            emit_dma(POOL, B_Wpb.dsem, Wpb[:], wview(w_pb), writes=[B_Wpb])
            emit_dma(POOL, B_Wout.dsem, Wout[:], wview(w_out), writes=[B_Wout])
            N3 = 4
            xt = [es3.enter_context(nc.sbuf_tensor(f"p3xt{i}", [128, D], F32)) for i in range(N3)]
            B_xt = [mkbuf(f"p3xt{i}") for i in range(N3)]
            NG = 3
            gt = [es3.enter_context(nc.sbuf_tensor(f"p3gt{i}", [128, 2048], F32)) for i in range(NG)]
            B_gt = [mkbuf(f"p3gt{i}") for i in range(NG)]
            ta = es3.enter_context(nc.sbuf_tensor("p3ta", [128, D], F32))
            tb = es3.enter_context(nc.sbuf_tensor("p3tb", [128, D], F32))
            yb16 = [es3.enter_context(nc.sbuf_tensor(f"p3yb{i}", [128, D], BF16)) for i in range(2)]
            ypT = [es3.enter_context(nc.sbuf_tensor(f"p3ypT{i}", [128, 8, 128], BF16)) for i in range(2)]
            x2t = [es3.enter_context(nc.sbuf_tensor(f"p3x2{i}", [128, D], F32)) for i in range(2)]
            B_x2t = [mkbuf(f"p3x2{i}") for i in range(2)]
            junk = es3.enter_context(nc.sbuf_tensor("p3junk", [128, D], BF16))
            st3 = [es3.enter_context(nc.sbuf_tensor(f"p3st{i}", [128, 4], F32)) for i in range(2)]
            h2b = [es3.enter_context(nc.sbuf_tensor(f"p3h2b{i}", [128, D], BF16)) for i in range(2)]
            B_ta, B_tb, B_junk = Buf(), Buf(), Buf()
            B_yb, B_ypT, B_st3, B_h2b = [Buf(), Buf()], [Buf(), Buf()], [Buf(), Buf()], [Buf(), Buf()]
            B_x2scr = Buf()

            def load3(t):
                i = t % N3
                emit_dma(SP, B_xt[i].dsem, xt[i][:], x_own.ap()[t * 128:(t + 1) * 128, :], writes=[B_xt[i]])
                emit_dma(SP, B_gt[t % NG].dsem, gt[t % NG][:], g_scr.ap()[t * 128:(t + 1) * 128, :], reads=[B_gscr], writes=[B_gt[t % NG]])

            def p3_G(t):
                g, B_g = gt[t % NG], B_gt[t % NG]
                emit(ACT, lambda e: e.activation(out=g[:], in_=g[:], func=AF.Exp, scale=-1.0), reads=[B_g], writes=[B_g])
                emit(ACT, lambda e: e.activation(out=g[:], in_=g[:], func=AF.Ln, bias=ones_f[:, 0:1]), reads=[B_g, B_c2], writes=[B_g])
                emit(ACT, lambda e: e.activation(out=g[:], in_=g[:], func=AF.Exp, scale=-1.0), reads=[B_g], writes=[B_g])

            def p3_A(t):
                i, i4 = t % 2, t % N3
                tok = slice(t * 128, (t + 1) * 128)
                YA, YB = PS[0], PS[1]
                fa, fb = [], []
                for h in range(8):
                    for hf in range(2):
                        fa.append(lambda e, h=h, hf=hf: e.matmul(YA[:, hf * 512:(hf + 1) * 512], lhsT=oaT[:, h, tok], rhs=Wpa[:, h, hf * 512:(hf + 1) * 512],
                                                                 start=(h == 0), stop=(h == 7)))
                for h in range(4):
                    for hf in range(2):
                        fb.append(lambda e, h=h, hf=hf: e.matmul(YB[:, hf * 512:(hf + 1) * 512], lhsT=obT[:, h, tok], rhs=Wpb[:, h, hf * 512:(hf + 1) * 512],
                                                                 start=(h == 0), stop=(h == 3)))
                emit_group(PE, fa, reads=[B_oaT, B_Wpa], writes=PSB[0])
                emit_group(PE, fb, reads=[B_obT, B_Wpb], writes=PSB[1])
                emit(DVE, lambda e: e.tensor_tensor(out=ta[:], in0=YA[:], in1=gt[t % NG][:, 0:1024], op=ALU.mult), reads=PSB[0] + [B_gt[t % NG]], writes=[B_ta])
                emit(DVE, lambda e: e.tensor_tensor(out=tb[:], in0=YB[:], in1=gt[t % NG][:, 1024:2048], op=ALU.mult), reads=PSB[1] + [B_gt[t % NG]], writes=[B_tb])
                emit(DVE, lambda e: e.tensor_tensor(out=yb16[i][:], in0=ta[:], in1=tb[:], op=ALU.add), reads=[B_ta, B_tb], writes=[B_yb[i]])

            def p3_A2(t):
                i = t % 2
                emit_group(PE, [(lambda e, c=c: e.transpose(pT3_ap[:, c * 128:(c + 1) * 128], yb16[i][:, c * 128:(c + 1) * 128], ident[:]))
                                for c in range(8)], reads=[B_yb[i], B_c2], writes=[B_pT3])
                emit(ACT, lambda e: e.activation(out=ypT[i][:].rearrange("p c n -> p (c n)"), in_=pT3_ap, func=AF.Copy), reads=[B_pT3], writes=[B_ypT[i]])

            def p3_B(t):
                i, i4 = t % 2, t % N3
                tok = slice(t * 128, (t + 1) * 128)
                XO = PS[2]
                fo = []
                for c in range(8):
                    for hf in range(2):
                        fo.append(lambda e, c=c, hf=hf: e.matmul(XO[:, hf * 512:(hf + 1) * 512], lhsT=ypT[i][:, c, :], rhs=Wout[:, c, hf * 512:(hf + 1) * 512],
                                                                 start=(c == 0), stop=(c == 7)))
                emit_group(PE, fo, reads=[B_ypT[i], B_Wout], writes=PSB[2])
                emit(DVE, lambda e: e.tensor_tensor(out=x2t[i][:], in0=XO[:], in1=xt[i4][:], op=ALU.add), reads=PSB[2] + [B_xt[i4]], writes=[B_x2t[i]])
                emit_dma(SP, B_x2t[i].dsem, x2_scr.ap()[tok, :], x2t[i][:], reads=[B_x2t[i]], writes=[B_x2scr])
                st = st3[i]
                emit(DVE, lambda e: e.memset(st[:, 0:1], 0.0), writes=[B_st3[i]])
                emit(ACT, lambda e: e.activation(out=junk[:], in_=x2t[i][:], func=AF.Square, accum_out=st[:, 0:1]), reads=[B_x2t[i], B_st3[i]], writes=[B_junk, B_st3[i]])
                emit(ACT, lambda e: e.activation(out=st[:, 1:2], in_=st[:, 0:1], func=AF.Ln, scale=1.0 / D, bias=eps_t[:, 0:1]), reads=[B_st3[i], B_c2], writes=[B_st3[i]])
                emit(ACT, lambda e: e.activation(out=st[:, 2:3], in_=st[:, 1:2], func=AF.Exp, scale=-0.5), reads=[B_st3[i]], writes=[B_st3[i]])
                emit(DVE, lambda e: e.scalar_tensor_tensor(out=h2b[i][:], in0=x2t[i][:], scalar=st[:, 2:3], in1=gffn_b, op0=ALU.mult, op1=ALU.mult),
                     reads=[B_x2t[i], B_st3[i], B_const], writes=[B_h2b[i]])

            def p3_B2(t):
                i = t % 2
                tok = slice(t * 128, (t + 1) * 128)
                emit_group(PE, [(lambda e, c=c: e.transpose(pKT_ap[:, c * 128:(c + 1) * 128], h2b[i][:, c * 128:(c + 1) * 128], ident[:]))
                                for c in range(8)], reads=[B_h2b[i], B_c2], writes=[B_pKT])
                emit(ACT, lambda e: e.activation(out=h2T[:, :, tok], in_=pKT_ap.rearrange("p (c n) -> p c n", c=8), func=AF.Copy), reads=[B_pKT], writes=[B_U1])

            pipeline(16, [p3_A, p3_B, p3_A2, p3_B2, p3_G, load3], [2, 4, 3, 5, 1, 0])
            B_p3_all = B_xt + B_gt + B_x2t + B_yb + B_ypT + B_st3 + B_h2b + [B_ta, B_tb, B_junk, B_Wpa, B_Wpb, B_Wout, B_oaT, B_obT]
        barrier(B_p3_all, ALL)

        aT = U2[:, 0:22 * 1024].rearrange("p (k n) -> p k n", k=22)
        B_aT = Buf()
        SCW = [512, 512, 512, 512, 512, 256]
        with ExitStack() as es4:
            Wd = es4.enter_context(nc.sbuf_tensor("Wd", [128, 22, D], BF16))
            B_Wd = mkbuf("Wd")
            Wgs = [es4.enter_context(nc.sbuf_tensor(f"Wgs{i}", [128, 8, 512], BF16)) for i in range(2)]
            Wus = [es4.enter_context(nc.sbuf_tensor(f"Wus{i}", [128, 8, 512], BF16)) for i in range(2)]
            B_Wgs = [mkbuf(f"Wgs{i}") for i in range(2)]
            B_Wus = [mkbuf(f"Wus{i}") for i in range(2)]
            ef = [es4.enter_context(nc.sbuf_tensor(f"p4e{i}", [128, 512], F32)) for i in range(2)]
            B_ef = [Buf(), Buf()]
            x2t = [es4.enter_context(nc.sbuf_tensor(f"p4x2{i}", [128, D], F32)) for i in range(2)]
            B_x2t = [mkbuf(f"p4x2{i}") for i in range(2)]
            x3 = es4.enter_context(nc.sbuf_tensor("p4x3", [128, D], F32))
            outt = [es4.enter_context(nc.sbuf_tensor(f"p4o{i}", [128, D], F32)) for i in range(2)]
            B_outt = [mkbuf(f"p4o{i}") for i in range(2)]
            junk = es4.enter_context(nc.sbuf_tensor("p4junk", [128, D], BF16))
            st4 = es4.enter_context(nc.sbuf_tensor("p4st", [128, 4], F32))
            B_x3, B_junk, B_st4 = Buf(), Buf(), Buf()
            B_out = Buf()

            sc_list = [(grp, sc) for grp in range(2) for sc in range(6)]

            def load_sc(idx):
                grp, sc = sc_list[idx]
                i = idx % 2
                w = SCW[sc]
                c0 = sc * 512
                emit_dma(POOL, B_Wgs[i].dsem, Wgs[i][:, :, 0:w], wview(w_gate)[:, :, c0:c0 + w], writes=[B_Wgs[i]])
                emit_dma(POOL, B_Wus[i].dsem, Wus[i][:, :, 0:w], wview(w_up)[:, :, c0:c0 + w], writes=[B_Wus[i]])

            load_sc(0)
            emit_dma(POOL, B_Wd.dsem, Wd[:], wview(w_down), writes=[B_Wd])
            pp = 0
            tile_ctr = 0
            for idx, (grp, sc) in enumerate(sc_list):
                i = idx % 2
                if idx + 1 < len(sc_list):
                    load_sc(idx + 1)
                for kk in range(SCW[sc] // 128):
                    k = sc * 4 + kk
                    for hf in range(2):
                        GU = PS[pp % 2]
                        B_gu = PSB[pp % 2]
                        e_t, B_e = ef[pp % 2], B_ef[pp % 2]
                        pp += 1
                        tk = slice(grp * 1024 + hf * 512, grp * 1024 + (hf + 1) * 512)
                        fg = []
                        for c in range(8):
                            fg.append(lambda e, c=c, GU=GU, kk=kk, i=i, tk=tk: e.matmul(GU[:, 0:512], lhsT=Wgs[i][:, c, kk * 128:(kk + 1) * 128], rhs=h2T[:, c, tk],
                                                                                      start=(c == 0), stop=(c == 7)))
                        for c in range(8):
                            fg.append(lambda e, c=c, GU=GU, kk=kk, i=i, tk=tk: e.matmul(GU[:, 512:1024], lhsT=Wus[i][:, c, kk * 128:(kk + 1) * 128], rhs=h2T[:, c, tk],
                                                                                      start=(c == 0), stop=(c == 7)))
                        emit_group(PE, fg, reads=[B_Wgs[i], B_Wus[i], B_U1], writes=B_gu)
                        emit(ACT, lambda e, GU=GU, e_t=e_t: e.activation(out=e_t[:], in_=GU[:, 0:512], func=AF.Silu), reads=B_gu, writes=[B_e])
                        emit(DVE, lambda e, e_t=e_t, GU=GU, k=k, hf=hf: e.tensor_tensor(out=aT[:, k, hf * 512:(hf + 1) * 512], in0=e_t[:], in1=GU[:, 512:1024], op=ALU.mult),
                             reads=[B_e] + B_gu, writes=[B_aT])
                if sc == 5:
                    for tt in range(8):
                        t = grp * 8 + tt
                        i2 = tile_ctr % 2
                        tile_ctr += 1
                        tok = slice(t * 128, (t + 1) * 128)
                        emit_dma(SP, B_x2t[i2].dsem, x2t[i2][:], x2_scr.ap()[tok, :], reads=[B_x2scr], writes=[B_x2t[i2]])
                        Y = PS[2 + tt % 2]
                        B_y = PSB[2 + tt % 2]
                        fd = []
                        for k in range(22):
                            for hf in range(2):
                                fd.append(lambda e, k=k, hf=hf, Y=Y, tt=tt: e.matmul(Y[:, hf * 512:(hf + 1) * 512], lhsT=aT[:, k, tt * 128:(tt + 1) * 128],
                                                                                   rhs=Wd[:, k, hf * 512:(hf + 1) * 512], start=(k == 0), stop=(k == 21)))
                        emit_group(PE, fd, reads=[B_aT, B_Wd], writes=B_y)
                        emit(DVE, lambda e, Y=Y, i2=i2: e.tensor_tensor(out=x3[:], in0=Y[:], in1=x2t[i2][:], op=ALU.add), reads=B_y + [B_x2t[i2]], writes=[B_x3])
                        emit(DVE, lambda e: e.memset(st4[:, 0:1], 0.0), writes=[B_st4])
                        emit(ACT, lambda e: e.activation(out=junk[:], in_=x3[:], func=AF.Square, accum_out=st4[:, 0:1]), reads=[B_x3, B_st4], writes=[B_junk, B_st4])
                        emit(ACT, lambda e: e.activation(out=st4[:, 1:2], in_=st4[:, 0:1], func=AF.Ln, scale=1.0 / D, bias=eps_t[:, 0:1]), reads=[B_st4, B_c2], writes=[B_st4])
                        emit(ACT, lambda e: e.activation(out=st4[:, 2:3], in_=st4[:, 1:2], func=AF.Exp, scale=-0.5), reads=[B_st4], writes=[B_st4])
                        emit(DVE, lambda e, i2=i2: e.scalar_tensor_tensor(out=outt[i2][:], in0=x3[:], scalar=st4[:, 2:3], in1=gfin_b, op0=ALU.mult, op1=ALU.mult),
                             reads=[B_x3, B_st4, B_const], writes=[B_outt[i2]])
                        emit_dma(SP, B_outt[i2].dsem, out.ap()[tok, :], outt[i2][:], reads=[B_outt[i2]], writes=[B_out])
            for sc_, v in B_out.w.items():
                SP.wait(sc_, v)
            if debug:
                for sc_, v in B_dbg.r.items():
                    SP.wait(sc_, v)
    return nc


_NC_CACHE = {}


def _rope_tab(ang_cs):
    c, s = ang_cs
    return np.concatenate([c, c], axis=1), np.concatenate([-s, s], axis=1)


def _rope_angles_like_reference():
    try:
        import jax
        import jax.numpy as jnp
        with jax.default_device(jax.devices("cpu")[0]):
            def rope_angles(pos, dim, theta):
                inv_freq = theta ** (-jnp.arange(0, dim, 2, dtype=jnp.float32) / dim)
                return pos[:, None] * inv_freq[None, :]
            rows = S // 64
            row = jnp.broadcast_to(jnp.arange(rows, dtype=jnp.float32)[:, None], (rows, 64)).reshape(-1)
            col = jnp.broadcast_to(jnp.arange(64, dtype=jnp.float32)[None, :], (rows, 64)).reshape(-1)
            ang_a = jnp.concatenate([rope_angles(row, 32, 10000.0), rope_angles(col, 32, 10000.0)], axis=-1)
            ang_1 = rope_angles(jnp.arange(S, dtype=jnp.float32), 64, 10000.0)
            out = [(np.asarray(jnp.cos(a), dtype=np.float32), np.asarray(jnp.sin(a), dtype=np.float32)) for a in (ang_a, ang_1)]
        return out[0], out[1]
    except Exception:
        pos = np.arange(S, dtype=np.float32)
        inv16 = (np.float32(10000.0) ** (-np.arange(0, 32, 2, dtype=np.float32) / np.float32(32))).astype(np.float32)
        inv32 = (np.float32(10000.0) ** (-np.arange(0, 64, 2, dtype=np.float32) / np.float32(64))).astype(np.float32)
        row = np.floor(pos / 64).astype(np.float32)
        col = (pos - row * 64).astype(np.float32)
        ang_a = np.concatenate([row[:, None] * inv16[None, :], col[:, None] * inv16[None, :]], axis=1).astype(np.float32)
        ang_1 = (pos[:, None] * inv32[None, :]).astype(np.float32)
        return (np.cos(ang_a), np.sin(ang_a)), (np.cos(ang_1), np.sin(ang_1))


def _host_inputs(inputs):
    f = lambda a: np.ascontiguousarray(np.asarray(a, dtype=np.float32))
    x = f(inputs["x"])
    w_in = f(inputs["w_in"])[0]
    w_kv = np.concatenate([w_in[:, 512:640], w_in[:, 640:768], w_in[:, 1280:1792], w_in[:, 1792:2304]], axis=1)
    order = [0, 4, 1, 5, 2, 6, 3, 7]
    aq = np.concatenate([w_in[:, h * 64:(h + 1) * 64] for h in order], axis=1)
    w_q = np.concatenate([aq, w_in[:, 768:1280]], axis=1)
    w_g = w_in[:, 2304:4352]
    vecs = np.concatenate([f(inputs["norm_mix"])[0], f(inputs["norm_ffn"])[0], f(inputs["norm_final"]),
                           f(inputs["q_norm_a"])[0], f(inputs["k_norm_a"])[0],
                           f(inputs["lambda_q1"])[0], f(inputs["lambda_k1"])[0], f(inputs["lambda_q2"])[0], f(inputs["lambda_k2"])[0]])[None, :]
    angA, ang1 = _rope_angles_like_reference()
    tcA, tsA = _rope_tab(angA)
    tc1, ts1 = _rope_tab(ang1)
    tab = np.ascontiguousarray(np.concatenate([tcA, tsA, tc1, ts1], axis=1).astype(np.float32))
    common = {
        "w_kv": np.ascontiguousarray(w_kv), "w_q": np.ascontiguousarray(w_q), "w_g": np.ascontiguousarray(w_g),
        "w_pa": f(inputs["w_proj_a"])[0], "w_pb": f(inputs["w_proj_b"])[0], "w_out": f(inputs["w_out"])[0],
        "w_gate": f(inputs["w_gate_ffn"])[0], "w_up": f(inputs["w_up_ffn"])[0], "w_down": f(inputs["w_down_ffn"])[0],
        "vecs": np.ascontiguousarray(vecs), "subln_col": np.ascontiguousarray(f(inputs["subln_b"])[0][:, None]),
        "ident": np.eye(128, dtype=np.float32), "tab_all": tab,
    }
    in_maps = []
    for c in range(8):
        b, qb = c // 4, c % 4
        m = dict(common)
        m["x_all"] = x[b]
        m["x_own"] = np.ascontiguousarray(x[b, qb * NOWN:(qb + 1) * NOWN])
        m["tab_own"] = np.ascontiguousarray(tab[qb * NOWN:(qb + 1) * NOWN])
        in_maps.append(m)
    return in_maps


def kernel(**inputs):
    in_maps = _host_inputs(inputs)
    if "nc" not in _NC_CACHE:
        _NC_CACHE["nc"] = build_program()
    nc = _NC_CACHE["nc"]
    res = run_bass_kernel_spmd(nc, in_maps, core_ids=list(range(8)))
    outp = np.empty((2, S, D), dtype=np.float32)
    for c in range(8):
        b, qb = c // 4, c % 4
        outp[b, qb * NOWN:(qb + 1) * NOWN] = res.results[c]["out"]
    return outp
```

```python
import math
from contextlib import ExitStack

import numpy as np
import concourse.bass as bass
import concourse.mybir as mybir
from concourse.bass_utils import run_bass_kernel_spmd

F32 = mybir.dt.float32
BF16 = mybir.dt.bfloat16
AF = mybir.ActivationFunctionType
ALU = mybir.AluOpType
AX = mybir.AxisListType

S = 8192
D = 1024
NOWN = 2048
DFF = 2816
EPS = 1e-6
LAMBDA_INIT = 0.8 - 0.6 * math.exp(-0.3 * 0)
NVEC = 3 * 1024 + 2 * 64 + 256


class SemC:
    def __init__(self, nc, es, name):
        self.sem = es.enter_context(nc.semaphore(name))
        self.cnt = 0


class Eng(SemC):
    def __init__(self, nc, es, name, eng):
        super().__init__(nc, es, name)
        self.e = eng
        self.seen = {}

    def wait(self, sc, v):
        if self.seen.get(sc, 0) >= v:
            return
        self.e.wait_ge(sc.sem, v)
        self.seen[sc] = v


class Buf:
    def __init__(self, nc=None, es=None, name=None):
        self.w = {}
        self.r = {}
        self.dsem = SemC(nc, es, "d_" + name) if nc is not None else None


STRICT = True


def _pre(E, reads, writes):
    for b in reads:
        for sc, v in b.w.items():
            E.wait(sc, v)
    for b in writes:
        for sc, v in b.w.items():
            if STRICT or sc is not E:
                E.wait(sc, v)
        for sc, v in b.r.items():
            if STRICT or sc is not E:
                E.wait(sc, v)


def emit(E, fn, reads=(), writes=()):
    _pre(E, reads, writes)
    ins = fn(E.e)
    E.cnt += 1
    ins.then_inc(E.sem, 1)
    for b in reads:
        b.r[E] = E.cnt
    for b in writes:
        b.w[E] = E.cnt


def emit_group(E, fns, reads=(), writes=()):
    _pre(E, reads, writes)
    ins = None
    for fn in fns:
        ins = fn(E.e)
    E.cnt += 1
    ins.then_inc(E.sem, 1)
    for b in reads:
        b.r[E] = E.cnt
    for b in writes:
        b.w[E] = E.cnt


def emit_dma(Q, dsem, out, in_, reads=(), writes=()):
    _pre(Q, reads, writes)
    Q.e.dma_start(out=out, in_=in_).then_inc(dsem.sem, 16)
    dsem.cnt += 16
    for b in reads:
        b.r[dsem] = dsem.cnt
    for b in writes:
        b.w[dsem] = dsem.cnt


def emit_dma_batch(Q, dsem, pairs, reads=(), writes=()):
    _pre(Q, reads, writes)
    for out, in_ in pairs:
        Q.e.dma_start(out=out, in_=in_).then_inc(dsem.sem, 16)
        dsem.cnt += 16
    for b in reads:
        b.r[dsem] = dsem.cnt
    for b in writes:
        b.w[dsem] = dsem.cnt


def bc_mid(ap2d, h):
    (ps, pn), (s, n) = ap2d.ap
    return bass.AP(ap2d.tensor, ap2d.offset, [[ps, pn], [0, h], [s, n]])


def bc_last(ap2d, n):
    (ps, pn), (s, h) = ap2d.ap
    return bass.AP(ap2d.tensor, ap2d.offset, [[ps, pn], [s, h], [0, n]])


def build_program(debug=False):
    nc = bass.Bass("TRN2", target_bir_lowering=False)

    def din(name, shape, dt=F32):
        return nc.dram_tensor(name, shape, dt, kind="ExternalInput")

    x_all = din("x_all", [S, D])
    x_own = din("x_own", [NOWN, D])
    tab_all = din("tab_all", [S, 256])
    tab_own = din("tab_own", [NOWN, 256])
    w_kv = din("w_kv", [D, 1280])
    w_q = din("w_q", [D, 1024])
    w_g = din("w_g", [D, 2048])
    w_pa = din("w_pa", [512, D])
    w_pb = din("w_pb", [512, D])
    w_out = din("w_out", [D, D])
    w_gate = din("w_gate", [D, DFF])
    w_up = din("w_up", [D, DFF])
    w_down = din("w_down", [DFF, D])
    vecs = din("vecs", [1, NVEC])
    subln_col = din("subln_col", [128, 1])
    ident_in = din("ident", [128, 128])
    out = nc.dram_tensor("out", [NOWN, D], F32, kind="ExternalOutput")
    kT_scr = [nc.dram_tensor(f"kT_scr{i}", [128, S], BF16, kind="Internal") for i in range(5)]
    vA_scr = nc.dram_tensor("vA_scr", [128, 64 * 130], BF16, kind="Internal")
    vB_scr = [nc.dram_tensor(f"vB_scr{i}", [128, 64 * 128], BF16, kind="Internal") for i in range(4)]
    g_scr = nc.dram_tensor("g_scr", [NOWN, 2048], BF16, kind="Internal")
    x2_scr = nc.dram_tensor("x2_scr", [NOWN, D], F32, kind="Internal")
    dbg = {}
    if debug:
        dbg["qT"] = nc.dram_tensor("dbg_qT", [128, 8 * NOWN], F32, kind="ExternalOutput")
        dbg["oaT"] = nc.dram_tensor("dbg_oaT", [64, 8 * NOWN], F32, kind="ExternalOutput")
        dbg["obT"] = nc.dram_tensor("dbg_obT", [128, 4 * NOWN], F32, kind="ExternalOutput")

    with ExitStack() as es:
        es.enter_context(nc.allow_low_precision("bf16 matmul operands, fp32 accumulation"))
        PE = Eng(nc, es, "s_pe", nc.tensor)
        ACT = Eng(nc, es, "s_act", nc.scalar)
        DVE = Eng(nc, es, "s_dve", nc.vector)
        POOL = Eng(nc, es, "s_pool", nc.gpsimd)
        SP = Eng(nc, es, "s_sp", nc.sync)

        def sb(name, shape, dt):
            return es.enter_context(nc.sbuf_tensor(name, shape, dt))

        def mkbuf(name):
            return Buf(nc, es, name)

        PS = [es.enter_context(nc.psum_tensor(f"ps{i}", [128, 1024], F32)) for i in range(4)]
        PSB = [[Buf(), Buf()] for _ in range(4)]

        vec_b = sb("vec_b", [128, NVEC], F32)
        ident_f = sb("ident_f", [128, 128], F32)
        ident = sb("ident_b", [128, 128], BF16)
        ones_f = sb("ones_f", [128, 128], F32)
        ones_b = sb("ones_b", [128, 128], BF16)
        eps_t = sb("eps_t", [128, 1], F32)
        cst = sb("cst", [128, 16], F32)
        U1 = sb("U1", [128, 8, NOWN], BF16)
        U2 = sb("U2", [128, 12 * NOWN], BF16)
        B_const = mkbuf("const")
        B_U1 = Buf()
        B_oaT = Buf()
        B_obT = Buf()
        gmix_b = vec_b[:, 0:1024]
        gffn_b = vec_b[:, 1024:2048]
        gfin_b = vec_b[:, 2048:3072]
        gq_b = vec_b[:, 3072:3136]
        gk_b = vec_b[:, 3136:3200]
        oaT = U2[0:64, 0:8 * NOWN].rearrange("p (h n) -> p h n", h=8)
        obT = U2[:, 8 * NOWN:12 * NOWN].rearrange("p (h n) -> p h n", h=4)

        vb_src = bass.AP(vecs, 0, [[0, 128], [1, NVEC]])
        emit_dma(SP, B_const.dsem, vec_b[:], vb_src, writes=[B_const])
        emit_dma(SP, B_const.dsem, ident_f[:], ident_in.ap(), writes=[B_const])
        emit_dma(SP, B_const.dsem, cst[:, 5:6], subln_col.ap(), writes=[B_const])
        B_c2 = Buf()
        emit(DVE, lambda e: e.memset(ones_f[:], 1.0), writes=[B_c2])
        emit(DVE, lambda e: e.memset(ones_b[:], 1.0), writes=[B_c2])
        emit(DVE, lambda e: e.memset(eps_t[:], EPS), writes=[B_c2])
        emit(DVE, lambda e: e.tensor_copy(ident[:], ident_f[:]), reads=[B_const], writes=[B_c2])
        lam_w = sb("lam_w", [128, 128], F32)
        B_lam = Buf()
        emit(DVE, lambda e: e.tensor_tensor(out=lam_w[:, 0:64], in0=vec_b[:, 3200:3264], in1=vec_b[:, 3264:3328], op=ALU.mult), reads=[B_const], writes=[B_lam])
        emit(DVE, lambda e: e.tensor_tensor(out=lam_w[:, 64:128], in0=vec_b[:, 3328:3392], in1=vec_b[:, 3392:3456], op=ALU.mult), reads=[B_const], writes=[B_lam])
        emit(DVE, lambda e: e.tensor_reduce(out=cst[:, 0:2], in_=lam_w[:].rearrange("p (a n) -> p a n", a=2), axis=AX.X, op=ALU.add), reads=[B_lam], writes=[B_lam])
        emit(ACT, lambda e: e.activation(out=cst[:, 2:4], in_=cst[:, 0:2], func=AF.Exp), reads=[B_lam], writes=[B_lam])
        emit(DVE, lambda e: e.tensor_tensor(out=cst[:, 4:5], in0=cst[:, 3:4], in1=cst[:, 2:3], op=ALU.subtract), reads=[B_lam], writes=[B_lam])
        emit(DVE, lambda e: e.tensor_scalar(out=cst[:, 4:5], in0=cst[:, 4:5], scalar1=-LAMBDA_INIT, scalar2=None, op0=ALU.add), reads=[B_lam], writes=[B_lam])
        emit(DVE, lambda e: e.tensor_scalar(out=cst[:, 5:6], in0=cst[:, 5:6], scalar1=1.0 - LAMBDA_INIT, scalar2=None, op0=ALU.mult), reads=[B_const, B_lam], writes=[B_lam])
        neglam_col = cst[:, 4:5]
        gcol = cst[:, 5:6]
        B_cst_all = [B_const, B_c2, B_lam]

        def wview(w, p=128):
            return w.ap().rearrange("(c p) n -> p c n", p=p)

        def pipeline(n, stages, skews):
            for s_ in range(n + max(skews)):
                for fn, sk in zip(stages, skews):
                    t_ = s_ - sk
                    if 0 <= t_ < n:
                        fn(t_)

        NXT, NXNT, NTAB = 4, 4, 8

        def phase_front(es2, pfx):
            r = {}
            r["xt"] = [es2.enter_context(nc.sbuf_tensor(pfx + f"xt{i}", [128, D], F32)) for i in range(NXT)]
            r["B_xt"] = [mkbuf(pfx + f"xt{i}") for i in range(NXT)]
            r["junk"] = es2.enter_context(nc.sbuf_tensor(pfx + "junk", [128, D], BF16))
            r["B_junk"] = Buf()
            r["st"] = [es2.enter_context(nc.sbuf_tensor(pfx + f"st{i}", [128, 4], F32)) for i in range(2)]
            r["B_st"] = [Buf(), Buf()]
            r["xn"] = [es2.enter_context(nc.sbuf_tensor(pfx + f"xn{i}", [128, D], BF16)) for i in range(2)]
            r["B_xn"] = [Buf(), Buf()]
            r["xnT"] = [es2.enter_context(nc.sbuf_tensor(pfx + f"xnT{i}", [128, 8, 128], BF16)) for i in range(NXNT)]
            r["B_xnT"] = [Buf() for _ in range(NXNT)]
            return r

        def rmsnorm_1(F, t, gain_b):
            xt, st = F["xt"][t % NXT], F["st"][t % 2]
            B_xt, B_st = F["B_xt"][t % NXT], F["B_st"][t % 2]
            xn, B_xn = F["xn"][t % 2], F["B_xn"][t % 2]
            emit(DVE, lambda e: e.memset(st[:, 0:1], 0.0), writes=[B_st])
            emit(ACT, lambda e: e.activation(out=F["junk"][:], in_=xt[:], func=AF.Square, accum_out=st[:, 0:1]),
                 reads=[B_xt, B_st], writes=[F["B_junk"], B_st])
            emit(ACT, lambda e: e.activation(out=st[:, 1:2], in_=st[:, 0:1], func=AF.Ln, scale=1.0 / D, bias=eps_t[:, 0:1]),
                 reads=[B_st, B_c2], writes=[B_st])
            emit(ACT, lambda e: e.activation(out=st[:, 2:3], in_=st[:, 1:2], func=AF.Exp, scale=-0.5),
                 reads=[B_st], writes=[B_st])
            emit(DVE, lambda e: e.scalar_tensor_tensor(out=xn[:], in0=xt[:], scalar=st[:, 2:3], in1=gain_b,
                                                       op0=ALU.mult, op1=ALU.mult),
                 reads=[B_xt, B_st, B_const], writes=[B_xn])

        def rmsnorm_2(F, t, pT_ap, B_pT):
            xn, B_xn = F["xn"][t % 2], F["B_xn"][t % 2]
            xnT, B_xnT = F["xnT"][t % NXNT], F["B_xnT"][t % NXNT]
            emit_group(PE, [(lambda e, c=c: e.transpose(pT_ap[:, c * 128:(c + 1) * 128], xn[:, c * 128:(c + 1) * 128], ident[:]))
                            for c in range(8)], reads=[B_xn, B_c2], writes=[B_pT])
            emit(ACT, lambda e: e.activation(out=xnT[:].rearrange("p c n -> p (c n)"), in_=pT_ap, func=AF.Copy),
                 reads=[B_pT], writes=[B_xnT])

        def rope(z_ap, H, tc, ts, t1, t2, B_z, B_tab, B_t, out_ap, B_out, rs=None, B_rs=None):
            z3 = z_ap.rearrange("p (h d) -> p h d", h=H)
            t13 = t1[:, 0:H * 64].rearrange("p (h d) -> p h d", h=H)
            t23 = t2[:, 0:H * 64].rearrange("p (h d) -> p h d", h=H)
            o3 = out_ap.rearrange("p (h d) -> p h d", h=H)
            B_t1, B_t2 = B_t
            emit(DVE, lambda e: e.tensor_tensor(out=t13, in0=z3, in1=bc_mid(tc, H), op=ALU.mult),
                 reads=B_z + [B_tab], writes=[B_t1])
            emit(DVE, lambda e: e.tensor_tensor(out=t23[:, :, 0:32], in0=z3[:, :, 32:64], in1=bc_mid(ts[:, 0:32], H), op=ALU.mult),
                 reads=B_z + [B_tab], writes=[B_t2[0]])
            emit(DVE, lambda e: e.tensor_tensor(out=t23[:, :, 32:64], in0=z3[:, :, 0:32], in1=bc_mid(ts[:, 32:64], H), op=ALU.mult),
                 reads=B_z + [B_tab], writes=[B_t2[1]])
            if rs is None:
                emit(DVE, lambda e: e.tensor_tensor(out=o3, in0=t13, in1=t23, op=ALU.add), reads=[B_t1] + B_t2, writes=[B_out])
            else:
                emit(DVE, lambda e: e.tensor_tensor(out=t23, in0=t13, in1=t23, op=ALU.add), reads=[B_t1] + B_t2, writes=B_t2)
                emit(DVE, lambda e: e.tensor_tensor(out=o3, in0=t23, in1=bc_last(rs, 64), op=ALU.mult),
                     reads=B_t2 + [B_rs], writes=[B_out])

        def head_rs(z_ap, H, sq, hs, B_z, B_sq, B_hs):
            emit(ACT, lambda e: e.activation(out=sq[:, 0:H * 64], in_=z_ap, func=AF.Square), reads=B_z, writes=[B_sq])
            emit(DVE, lambda e: e.tensor_reduce(out=hs[:, 0:H], in_=sq[:, 0:H * 64].rearrange("p (h d) -> p h d", h=H), axis=AX.X, op=ALU.add),
                 reads=[B_sq], writes=[B_hs])
            emit(ACT, lambda e: e.activation(out=hs[:, H:2 * H], in_=hs[:, 0:H], func=AF.Ln, scale=1.0 / 64, bias=eps_t[:, 0:1]),
                 reads=[B_hs, B_c2], writes=[B_hs])
            emit(ACT, lambda e: e.activation(out=hs[:, 2 * H:3 * H], in_=hs[:, H:2 * H], func=AF.Exp, scale=-0.5),
                 reads=[B_hs], writes=[B_hs])
            return hs[:, 2 * H:3 * H]

        pT_ap = PS[1][:, 512:1024].bitcast(BF16)
        B_pT = PSB[1][1]
        pKT_ap = PS[3][:, 512:1024].bitcast(BF16)
        B_pKT = PSB[3][1]

        def barrier(bufs, engines):
            for E in engines:
                for b in bufs:
                    for sc, v in list(b.w.items()) + list(b.r.items()):
                        if sc is not E:
                            E.wait(sc, v)

        ALL = [PE, ACT, DVE, POOL, SP]
        Wq = U2[:, 0:8192].rearrange("p (c n) -> p c n", c=8)
        Wg = U2[:, 8192:24576].rearrange("p (c n) -> p c n", c=8)
        B_Wq, B_Wg = mkbuf("Wq"), mkbuf("Wg")

        with ExitStack() as es1:
            F = phase_front(es1, "p1")
            Wkv = es1.enter_context(nc.sbuf_tensor("Wkv", [128, 8, 1280], BF16))
            B_Wkv = mkbuf("Wkv")
            emit_dma(POOL, B_Wkv.dsem, Wkv[:], wview(w_kv), writes=[B_Wkv])
            emit_dma(POOL, B_Wq.dsem, Wq, wview(w_q), writes=[B_Wq])
            emit_dma(POOL, B_Wg.dsem, Wg, wview(w_g), writes=[B_Wg])
            tabt = [es1.enter_context(nc.sbuf_tensor(f"tabt{i}", [128, 256], F32)) for i in range(NTAB)]
            B_tab = [mkbuf(f"tabt{i}") for i in range(NTAB)]
            t1 = es1.enter_context(nc.sbuf_tensor("t1", [128, 512], F32))
            t2 = es1.enter_context(nc.sbuf_tensor("t2", [128, 512], F32))
            sqs = es1.enter_context(nc.sbuf_tensor("sqs", [128, 512], F32))
            kg = es1.enter_context(nc.sbuf_tensor("kg", [128, 128], F32))
            hs = es1.enter_context(nc.sbuf_tensor("hs", [128, 24], F32))
            B_t = (Buf(), [Buf(), Buf()])
            B_kg, B_hs, B_sqs = Buf(), Buf(), Buf()
            kpost = [es1.enter_context(nc.sbuf_tensor(f"kpost{i}", [128, 640], BF16)) for i in range(2)]
            B_kpost = [[Buf(), Buf()] for _ in range(2)]
            kst = [es1.enter_context(nc.sbuf_tensor(f"kst{i}", [128, 5, 512], BF16)) for i in range(2)]
            B_kst = [mkbuf(f"kst{i}") for i in range(2)]
            vst = [es1.enter_context(nc.sbuf_tensor(f"vst{i}", [128, 4, 642], BF16)) for i in range(2)]
            B_vst = [mkbuf(f"vst{i}") for i in range(2)]
            B_kscr = [Buf(), Buf()]
            B_vscr = [Buf(), Buf()]
            for i in range(2):
                emit(DVE, lambda e, i=i: e.memset(vst[i][:], 1.0), writes=[B_vst[i]])

            def load1(t):
                emit_dma(SP, F["B_xt"][t % NXT].dsem, F["xt"][t % NXT][:], x_all.ap()[t * 128:(t + 1) * 128, :], writes=[F["B_xt"][t % NXT]])
                emit_dma(SP, B_tab[t % NTAB].dsem, tabt[t % NTAB][:], tab_all.ap()[t * 128:(t + 1) * 128, :], writes=[B_tab[t % NTAB]])

            def p1_A1(t):
                rmsnorm_1(F, t, gmix_b)

            def p1_A2(t):
                rmsnorm_2(F, t, pT_ap, B_pT)

            def p1_zs(t):
                zs = t % 2
                return PS[2 * zs], PS[2 * zs + 1], [PSB[2 * zs][0], PSB[2 * zs][1], PSB[2 * zs + 1][0]]

            def p1_B(t):
                Z0, Z1, B_z = p1_zs(t)
                xnT = F["xnT"][t % NXNT]
                fns = []
                for c in range(8):
                    fns.append(lambda e, c=c: e.matmul(Z0[:, 0:512], lhsT=xnT[:, c, :], rhs=Wkv[:, c, 0:512], start=(c == 0), stop=(c == 7)))
                    fns.append(lambda e, c=c: e.matmul(Z0[:, 512:1024], lhsT=xnT[:, c, :], rhs=Wkv[:, c, 512:1024], start=(c == 0), stop=(c == 7)))
                    fns.append(lambda e, c=c: e.matmul(Z1[:, 0:256], lhsT=xnT[:, c, :], rhs=Wkv[:, c, 1024:1280], start=(c == 0), stop=(c == 7)))
                emit_group(PE, fns, reads=[F["B_xnT"][t % NXNT], B_Wkv], writes=B_z)

            def p1_C1(t):
                Z0, Z1, B_z = p1_zs(t)
                i = t % 2
                tb_, B_tb_ = tabt[t % NTAB], B_tab[t % NTAB]
                g4, s4 = t // 4, (t // 4) % 2
                j4 = t % 4
                vrow = vst[s4][:, j4, :]
                emit(ACT, lambda e: e.activation(out=vrow[:, 0:130].rearrange("p (h d) -> p h d", h=2)[:, :, 0:64],
                                                 in_=Z0[:, 128:256].rearrange("p (h d) -> p h d", h=2), func=AF.Copy),
                     reads=[B_z[0]], writes=[B_vst[s4]])
                emit(ACT, lambda e: e.activation(out=vrow[:, 130:386], in_=Z0[:, 768:1024], func=AF.Copy), reads=[B_z[1]], writes=[B_vst[s4]])
                emit(ACT, lambda e: e.activation(out=vrow[:, 386:642], in_=Z1[:, 0:256], func=AF.Copy), reads=[B_z[2]], writes=[B_vst[s4]])
                rk = head_rs(Z0[:, 0:128], 2, sqs, hs, [B_z[0]], B_sqs, B_hs)
                emit(DVE, lambda e: e.tensor_tensor(out=kg[:].rearrange("p (h d) -> p h d", h=2), in0=Z0[:, 0:128].rearrange("p (h d) -> p h d", h=2),
                                                    in1=bc_mid(gk_b, 2), op=ALU.mult), reads=[B_z[0], B_const], writes=[B_kg])
                rope(Z0[:, 256:768], 8, tb_[:, 128:192], tb_[:, 192:256], t1, t2, [B_z[0], B_z[1]], B_tb_, B_t, kpost[i][:, 128:640], B_kpost[i][1])
                rope(kg[:], 2, tb_[:, 0:64], tb_[:, 64:128], t1, t2, [B_kg], B_tb_, B_t, kpost[i][:, 0:128], B_kpost[i][0], rs=rk, B_rs=B_hs)

            def p1_C2(t):
                i = t % 2
                g4, s4 = t // 4, (t // 4) % 2
                j4 = t % 4
                emit_group(PE, [(lambda e, b=b: e.transpose(pKT_ap[:, b * 128:(b + 1) * 128], kpost[i][:, b * 128:(b + 1) * 128], ident[:]))
                                for b in range(5)], reads=B_kpost[i] + [B_c2], writes=[B_pKT])
                emit(ACT, lambda e: e.activation(out=kst[s4][:, :, j4 * 128:(j4 + 1) * 128], in_=pKT_ap[:, 0:640].rearrange("p (b n) -> p b n", b=5), func=AF.Copy),
                     reads=[B_pKT], writes=[B_kst[s4]])
                if j4 == 3:
                    emit_dma_batch(SP, B_kst[s4].dsem,
                                   [(kT_scr[b].ap()[:, g4 * 512:(g4 + 1) * 512], kst[s4][:, b, :]) for b in range(5)],
                                   reads=[B_kst[s4]], writes=[B_kscr[g4 % 2]])
                    emit_dma_batch(SP, B_vst[s4].dsem,
                                   [(vA_scr.ap().rearrange("p (t c) -> p t c", c=130)[:, g4 * 4:(g4 + 1) * 4, :], vst[s4][:, :, 0:130])] +
                                   [(vB_scr[h].ap().rearrange("p (t c) -> p t c", c=128)[:, g4 * 4:(g4 + 1) * 4, :],
                                     vst[s4][:, :, 130 + h * 128:130 + (h + 1) * 128]) for h in range(4)],
                                   reads=[B_vst[s4]], writes=[B_vscr[g4 % 2]])

            qT_all = U1
            qg = es1.enter_context(nc.sbuf_tensor("qg", [128, 512], F32))
            ee = [es1.enter_context(nc.sbuf_tensor(f"ee{i}", [128, 512], F32)) for i in range(2)]
            B_ee = [Buf(), Buf()]
            B_qg = Buf()
            qpost = [es1.enter_context(nc.sbuf_tensor(f"qpost{i}", [128, 1024], BF16)) for i in range(2)]
            B_qpost = [[Buf(), Buf()] for _ in range(2)]
            gst = [es1.enter_context(nc.sbuf_tensor(f"gst{i}", [128, 2048], BF16)) for i in range(2)]
            B_gst = [mkbuf(f"gst{i}") for i in range(2)]
            B_gscr = Buf()
            Zs = [PS[0], PS[2]]
            B_Zs = [PSB[0], PSB[2]]
            Gs = [PS[1][:, 0:512], PS[3][:, 0:512]]
            B_Gs = [PSB[1][0], PSB[3][0]]

            def load1b(T):
                t = T - 64
                emit_dma(SP, F["B_xt"][T % NXT].dsem, F["xt"][T % NXT][:], x_own.ap()[t * 128:(t + 1) * 128, :], writes=[F["B_xt"][T % NXT]])
                emit_dma(SP, B_tab[T % NTAB].dsem, tabt[T % NTAB][:], tab_own.ap()[t * 128:(t + 1) * 128, :], writes=[B_tab[T % NTAB]])

            def p1b_A1(T):
                t = T - 64
                rmsnorm_1(F, T, gmix_b)

            def p1b_A2(T):
                t = T - 64
                rmsnorm_2(F, T, pT_ap, B_pT)

            def p1b_B(T):
                t = T - 64
                i = T % 2
                xnT = F["xnT"][T % NXNT]
                B_xnT = F["B_xnT"][T % NXNT]
                Z, B_z = Zs[i], B_Zs[i]
                fns = []
                for c in range(8):
                    fns.append(lambda e, c=c: e.matmul(Z[:, 0:512], lhsT=xnT[:, c, :], rhs=Wq[:, c, 0:512], start=(c == 0), stop=(c == 7)))
                    fns.append(lambda e, c=c: e.matmul(Z[:, 512:1024], lhsT=xnT[:, c, :], rhs=Wq[:, c, 512:1024], start=(c == 0), stop=(c == 7)))
                emit_group(PE, fns, reads=[B_xnT, B_Wq], writes=B_z)

            def p1b_Bg(T):
                t = T - 64
                i = T % 2
                xnT = F["xnT"][T % NXNT]
                B_xnT = F["B_xnT"][T % NXNT]
                for q in range(4):
                    G, B_g = Gs[q % 2], B_Gs[q % 2]
                    emit_group(PE, [(lambda e, c=c, q=q, G=G: e.matmul(G, lhsT=xnT[:, c, :], rhs=Wg[:, c, q * 512:(q + 1) * 512],
                                                                      start=(c == 0), stop=(c == 7))) for c in range(8)],
                               reads=[B_xnT, B_Wg], writes=[B_g])
                    e_t, B_e = ee[q % 2], B_ee[q % 2]
                    emit(ACT, lambda e, G=G, e_t=e_t: e.activation(out=e_t[:], in_=G, func=AF.Exp, scale=-1.0), reads=[B_g], writes=[B_e])
                    emit(ACT, lambda e, e_t=e_t: e.activation(out=e_t[:], in_=e_t[:], func=AF.Ln, bias=ones_f[:, 0:1]), reads=[B_e, B_c2], writes=[B_e])
                    emit(ACT, lambda e, e_t=e_t, q=q: e.activation(out=gst[i][:, q * 512:(q + 1) * 512], in_=e_t[:], func=AF.Exp, scale=-1.0),
                         reads=[B_e], writes=[B_gst[i]])
                emit_dma(SP, B_gst[i].dsem, g_scr.ap()[t * 128:(t + 1) * 128, :], gst[i][:], reads=[B_gst[i]], writes=[B_gscr])

            def p1b_C1(T):
                t = T - 64
                i = T % 2
                Z, B_z = Zs[i], B_Zs[i]
                tb_, B_tb_ = tabt[T % NTAB], B_tab[T % NTAB]
                rq = head_rs(Z[:, 0:512], 8, sqs, hs, [B_z[0]], B_sqs, B_hs)
                emit(DVE, lambda e: e.tensor_tensor(out=qg[:].rearrange("p (h d) -> p h d", h=8), in0=Z[:, 0:512].rearrange("p (h d) -> p h d", h=8),
                                                    in1=bc_mid(gq_b, 8), op=ALU.mult), reads=[B_z[0], B_const], writes=[B_qg])
                rope(Z[:, 512:1024], 8, tb_[:, 128:192], tb_[:, 192:256], t1, t2, [B_z[1]], B_tb_, B_t, qpost[i][:, 512:1024], B_qpost[i][1])
                rope(qg[:], 8, tb_[:, 0:64], tb_[:, 64:128], t1, t2, [B_qg], B_tb_, B_t, qpost[i][:, 0:512], B_qpost[i][0], rs=rq, B_rs=B_hs)

            def p1b_C2(T):
                t = T - 64
                i = T % 2
                emit_group(PE, [(lambda e, b=b: e.transpose(pKT_ap[:, b * 128:(b + 1) * 128], qpost[i][:, b * 128:(b + 1) * 128], ident[:]))
                                for b in range(8)], reads=B_qpost[i] + [B_c2], writes=[B_pKT])
                emit(DVE, lambda e: e.tensor_copy(qT_all[:, :, t * 128:(t + 1) * 128], pKT_ap.rearrange("p (b n) -> p b n", b=8)),
                     reads=[B_pKT], writes=[B_U1])

            def both(f_kv, f_q):
                return lambda T: (f_kv(T) if T < 64 else (f_q(T) if f_q is not None else None)) if (f_kv is not None or T >= 64) else None

            pipeline(80, [both(load1, load1b), both(p1_A1, p1b_A1), both(p1_C1, p1b_C1), both(p1_A2, p1b_A2), both(p1_C2, p1b_C2),
                          both(p1_B, p1b_B), both(None, p1b_Bg)], [0, 2, 6, 3, 7, 5, 5])
            B_p1_all = ([F["B_junk"], B_t[0], B_kg, B_hs, B_sqs, B_Wkv, B_qg, B_Wq, B_Wg] + B_t[1] + F["B_xt"] + F["B_st"] + F["B_xn"] + F["B_xnT"] + B_tab
                        + B_kpost[0] + B_kpost[1] + B_kst + B_vst + B_qpost[0] + B_qpost[1] + B_gst + B_ee)
        barrier(B_p1_all, ALL)
        if debug:
            B_dbg = mkbuf("dbg")

        with ExitStack() as es2:
            kTs = [es2.enter_context(nc.sbuf_tensor(f"kTs{i}", [128, S], BF16)) for i in range(2)]
            Vs = [es2.enter_context(nc.sbuf_tensor(f"Vs{i}", [128, 64 * 130], BF16)) for i in range(2)]
            B_kTs = [mkbuf(f"kTs{i}") for i in range(2)]
            B_Vs = [mkbuf(f"Vs{i}") for i in range(2)]
            NPB = 4
            p_sb = [es2.enter_context(nc.sbuf_tensor(f"p_sb{i}", [128, 1024], BF16)) for i in range(NPB)]
            B_p = [Buf() for _ in range(NPB)]
            o_f = es2.enter_context(nc.sbuf_tensor("o_f", [128, 1024], F32))
            l_f = es2.enter_context(nc.sbuf_tensor("l_f", [128, 1024], F32))
            dd = es2.enter_context(nc.sbuf_tensor("dd", [128, 1024], F32))
            sq = es2.enter_context(nc.sbuf_tensor("sq", [128, 512], F32))
            lnv = es2.enter_context(nc.sbuf_tensor("lnv", [128, 512], F32))
            rs_f = es2.enter_context(nc.sbuf_tensor("rs_f", [128, 512], F32))
            B_of, B_lf, B_dd, B_sq, B_lnv, B_rs = Buf(), Buf(), Buf(), Buf(), Buf(), Buf()
            DX = 128
            accL1 = es2.enter_context(nc.sbuf_tensor("accL1", [128, 512 + DX], F32))
            lnl = es2.enter_context(nc.sbuf_tensor("lnl", [128, 1024], F32))
            B_lnl = Buf()
            B_acc = Buf()

            units = [("A", j) for j in range(4)] + [("B", h) for h in range(4)]
            slot_of = {0: 0, 1: 0, 2: 0, 3: 0, 4: 1, 5: 0, 6: 1, 7: 0}

            def load_unit(u):
                kind, j = units[u]
                s = slot_of[u]
                if kind == "A":
                    if j != 0:
                        return
                    emit_dma(SP, B_kTs[s].dsem, kTs[s][:], kT_scr[0].ap(), reads=B_kscr, writes=[B_kTs[s]])
                    emit_dma(SP, B_Vs[s].dsem, Vs[s][:], vA_scr.ap(), reads=B_vscr, writes=[B_Vs[s]])
                else:
                    emit_dma(SP, B_kTs[s].dsem, kTs[s][:], kT_scr[1 + j].ap(), reads=B_kscr, writes=[B_kTs[s]])
                    emit_dma(SP, B_Vs[s].dsem, Vs[s][:, 0:64 * 128], vB_scr[j].ap(), reads=B_vscr, writes=[B_Vs[s]])

            steps = [(u, qc, kt) for u in range(8) for qc in range(4) for kt in range(64)]
            NS = len(steps)
            Sbuf = [PS[0], PS[1]]
            B_S = [PSB[0], PSB[1]]
            O_ap = PS[2]
            B_O = PSB[2]
            L_ap = PS[3]
            B_L = PSB[3]

            def emit_PV(g):
                u, qc, kt = steps[g]
                kind, j = units[u]
                s = slot_of[u]
                P = p_sb[g % NPB]
                st, sp = (kt == 0), (kt == 63)
                if kind == "A":
                    V3 = Vs[s][:].rearrange("p (t c) -> p t c", c=130)
                    emit_group(PE, [
                        lambda e: e.matmul(O_ap[0:65, 0:512], lhsT=V3[:, kt, 0:65], rhs=P[:, 0:512], start=st, stop=sp),
                        lambda e: e.matmul(O_ap[0:65, 512:1024], lhsT=V3[:, kt, 65:130], rhs=P[:, 512:1024], start=st, stop=sp),
                    ], reads=[B_Vs[s], B_p[g % NPB]], writes=B_O)
                else:
                    V3 = Vs[s][:, 0:64 * 128].rearrange("p (t c) -> p t c", c=128)
                    emit_group(PE, [
                        lambda e: e.matmul(O_ap[:, 0:512], lhsT=V3[:, kt, :], rhs=P[:, 0:512], start=st, stop=sp),
                        lambda e: e.matmul(O_ap[:, 512:1024], lhsT=V3[:, kt, :], rhs=P[:, 512:1024], start=st, stop=sp),
                        lambda e: e.matmul(L_ap[:, 512 + DX:1024], lhsT=ones_b[:], rhs=P[:, 512 + DX:1024], start=st, stop=sp),
                    ], reads=[B_Vs[s], B_p[g % NPB], B_c2], writes=B_O + [B_L[1]])
                    if st:
                        emit(DVE, lambda e: e.tensor_copy(accL1[:], P[:, 0:512 + DX]), reads=[B_p[g % NPB]], writes=[B_acc])
                    else:
                        emit(DVE, lambda e: e.tensor_tensor(out=accL1[:], in0=accL1[:], in1=P[:, 0:512 + DX], op=ALU.add),
                             reads=[B_p[g % NPB], B_acc], writes=[B_acc])
                    if sp:
                        emit_group(PE, [lambda e: e.matmul(L_ap[:, 0:512], lhsT=ones_f[:], rhs=accL1[:, 0:512], start=True, stop=True)],
                                   reads=[B_acc, B_c2], writes=[B_L[0]])
                        emit_group(PE, [lambda e: e.matmul(L_ap[:, 512:512 + DX], lhsT=ones_f[:], rhs=accL1[:, 512:512 + DX], start=True, stop=True)],
                                   reads=[B_acc, B_c2], writes=[B_L[1]])

            def fin_evac(u, qc):
                kind, j = units[u]
                if kind == "A":
                    emit(DVE, lambda e: e.tensor_copy(o_f[0:65, :], O_ap[0:65, :]), reads=B_O, writes=[B_of])
                    emit(DVE, lambda e: e.reciprocal(out=l_f[64:65, :], in_=o_f[64:65, :]), reads=[B_of], writes=[B_lf])
                else:
                    emit(DVE, lambda e: e.tensor_copy(o_f[:], O_ap[:]), reads=B_O, writes=[B_of])
                    emit(DVE, lambda e: e.tensor_copy(l_f[:], L_ap[:]), reads=B_L, writes=[B_lf])

            def fin_B2():
                emit(ACT, lambda e: e.activation(out=lnl[:], in_=l_f[:], func=AF.Ln), reads=[B_lf], writes=[B_lnl])
                emit(ACT, lambda e: e.activation(out=l_f[:], in_=lnl[:], func=AF.Exp, scale=-1.0), reads=[B_lnl], writes=[B_lf])

            def fin_B3():
                emit(DVE, lambda e: e.tensor_tensor(out=dd[:, 0:512], in0=o_f[:, 0:512], in1=l_f[:, 0:512], op=ALU.mult),
                     reads=[B_of, B_lf], writes=[B_dd])
                emit(DVE, lambda e: e.scalar_tensor_tensor(out=dd[:, 512:1024], in0=o_f[:, 512:1024], scalar=neglam_col, in1=l_f[:, 512:1024],
                                                           op0=ALU.mult, op1=ALU.mult), reads=[B_of, B_lf, B_lam], writes=[B_dd])
                emit(DVE, lambda e: e.tensor_tensor(out=dd[:, 0:512], in0=dd[:, 0:512], in1=dd[:, 512:1024], op=ALU.add), reads=[B_dd], writes=[B_dd])
                emit(DVE, lambda e: e.tensor_tensor(out=sq[:], in0=dd[:, 0:512], in1=dd[:, 0:512], op=ALU.mult), reads=[B_dd], writes=[B_sq])

            def fin_pe(u, qc, gslot):
                kind, j = units[u]
                if kind == "A":
                    Sg = Sbuf[gslot] if gslot is not None else L_ap
                    B_Sg = B_S[gslot] if gslot is not None else B_L
                    emit_group(PE, [
                        lambda e: e.matmul(Sg[0:64, 0:512], lhsT=ones_f[64:65, 0:64], rhs=l_f[64:65, 0:512], start=True, stop=True),
                        lambda e: e.matmul(Sg[0:64, 512:1024], lhsT=ones_f[64:65, 0:64], rhs=l_f[64:65, 512:1024], start=True, stop=True),
                    ], reads=[B_lf, B_c2], writes=B_Sg)
                    for hh, col in ((j, 0), (j + 4, 512)):
                        emit(DVE, lambda e, hh=hh, col=col: e.tensor_tensor(out=oaT[:, hh, qc * 512:(qc + 1) * 512], in0=o_f[0:64, col:col + 512],
                                                                           in1=Sg[0:64, col:col + 512], op=ALU.mult),
                             reads=[B_of] + B_Sg, writes=[B_oaT])
                else:
                    Sg = Sbuf[gslot]
                    emit_group(PE, [lambda e: e.matmul(Sg[:, 0:512], lhsT=ones_f[:], rhs=sq[:], start=True, stop=True)],
                               reads=[B_sq, B_c2], writes=B_S[gslot])
                    emit(ACT, lambda e: e.activation(out=lnv[:], in_=Sg[:, 0:512], func=AF.Ln, scale=1.0 / 128, bias=eps_t[:, 0:1]),
                         reads=B_S[gslot] + [B_c2], writes=[B_lnv])
                    emit(ACT, lambda e: e.activation(out=rs_f[:], in_=lnv[:], func=AF.Exp, scale=-0.5), reads=[B_lnv], writes=[B_rs])
                    emit(DVE, lambda e: e.scalar_tensor_tensor(out=obT[:, j, qc * 512:(qc + 1) * 512], in0=dd[:, 0:512], scalar=gcol, in1=rs_f[:],
                                                               op0=ALU.mult, op1=ALU.mult), reads=[B_dd, B_rs, B_lam], writes=[B_obT])

            load_unit(0)
            load_unit(4)
            DEFER = 14
            free_S = [0, 1]
            Sphys = {}
            pending = []
            next_S = 0

            def issue_S(g):
                sl = free_S.pop(0)
                Sphys[g] = sl
                u, qc, kt = steps[g]
                kind, j = units[u]
                s = slot_of[u]
                blk = j if kind == "A" else 4 + j
                Sg = Sbuf[sl]
                q0 = qT_all[0:64, blk, qc * 512:(qc + 1) * 512]
                q1 = qT_all[64:128, blk, qc * 512:(qc + 1) * 512]
                emit_group(PE, [
                    lambda e: e.matmul(Sg[:, 0:512], lhsT=kTs[s][0:64, kt * 128:(kt + 1) * 128], rhs=q0, start=True, stop=True),
                    lambda e: e.matmul(Sg[:, 512:1024], lhsT=kTs[s][64:128, kt * 128:(kt + 1) * 128], rhs=q1, start=True, stop=True),
                ], reads=[B_kTs[s], B_U1], writes=B_S[sl])

            def _emit_exp(g):
                sl = Sphys[g]
                Sg = Sbuf[sl]
                emit(ACT, lambda e: e.activation(out=p_sb[g % NPB][:], in_=Sg[:], func=AF.Exp, scale=0.125),
                     reads=B_S[sl], writes=[B_p[g % NPB]])
                free_S.append(sl)

            def run_pending(g):
                pending.sort(key=lambda x: x[0])
                while pending and pending[0][0] <= g:
                    _, what, pu, pqc = pending.pop(0)
                    if what == "B2":
                        fin_B2()
                    elif what == "B3":
                        fin_B3()
                    elif units[pu][0] == "A" and not (pu == 3 and pqc == 3):
                        fin_pe(pu, pqc, None)
                    else:
                        sl = free_S.pop(0)
                        fin_pe(pu, pqc, sl)
                        free_S.append(sl)

            issue_S(0)
            issue_S(1)
            next_S = 2
            for g in range(NS):
                u, qc, kt = steps[g]
                _emit_exp(g)
                run_pending(g)
                while next_S < NS and next_S <= g + 2 and free_S:
                    issue_S(next_S)
                    next_S += 1
                emit_PV(g)
                if kt == 63:
                    fin_evac(u, qc)
                    if units[u][0] == "B":
                        pending.append((g + 3, "B2", u, qc))
                        pending.append((g + 5, "B3", u, qc))
                    pending.append((g + DEFER, "PE", u, qc))
                    if qc == 3:
                        if u == 3:
                            load_unit(5)
                        elif u == 4:
                            load_unit(6)
                        elif u == 5:
                            load_unit(7)
            run_pending(NS + DEFER + 1)
            B_p2_all = B_kTs + B_Vs + B_p + [B_of, B_lf, B_dd, B_sq, B_lnv, B_rs, B_acc, B_lnl]
        barrier(B_p2_all, ALL)

        if debug:
            dq = sb("dq", [128, 2048], F32)
            for b in range(8):
                emit(DVE, lambda e, b=b: e.tensor_copy(dq[:], qT_all[:, b, :]), reads=[B_U1], writes=[B_dbg])
                emit_dma(SP, B_dbg.dsem, dbg["qT"].ap()[:, b * NOWN:(b + 1) * NOWN], dq[:], reads=[B_dbg])
                emit(DVE, lambda e, b=b: e.tensor_copy(dq[0:64, :], oaT[:, b, :]), reads=[B_oaT], writes=[B_dbg])
                emit_dma(SP, B_dbg.dsem, dbg["oaT"].ap()[:, b * NOWN:(b + 1) * NOWN], dq[0:64, :], reads=[B_dbg])
            for b in range(4):
                emit(DVE, lambda e, b=b: e.tensor_copy(dq[:], obT[:, b, :]), reads=[B_obT], writes=[B_dbg])
                emit_dma(SP, B_dbg.dsem, dbg["obT"].ap()[:, b * NOWN:(b + 1) * NOWN], dq[:], reads=[B_dbg])

        h2T = U1
        pT3_ap = PS[3][:, 0:512].bitcast(BF16)
        B_pT3 = PSB[3][0]
        with ExitStack() as es3:
            Wpa = es3.enter_context(nc.sbuf_tensor("Wpa", [64, 8, D], BF16))
            Wpb = es3.enter_context(nc.sbuf_tensor("Wpb", [128, 4, D], BF16))
            Wout = es3.enter_context(nc.sbuf_tensor("Wout", [128, 8, D], BF16))
            B_Wpa, B_Wpb, B_Wout = mkbuf("Wpa"), mkbuf("Wpb"), mkbuf("Wout")
            emit_dma(POOL, B_Wpa.dsem, Wpa[:], wview(w_pa, 64), writes=[B_Wpa])
            emit_dma(POOL, B_Wpb.dsem, Wpb[:], wview(w_pb), writes=[B_Wpb])
            emit_dma(POOL, B_Wout.dsem, Wout[:], wview(w_out), writes=[B_Wout])
            N3 = 4
            xt = [es3.enter_context(nc.sbuf_tensor(f"p3xt{i}", [128, D], F32)) for i in range(N3)]
            B_xt = [mkbuf(f"p3xt{i}") for i in range(N3)]
            gt = [es3.enter_context(nc.sbuf_tensor(f"p3gt{i}", [128, 2048], BF16)) for i in range(N3)]
            B_gt = [mkbuf(f"p3gt{i}") for i in range(N3)]
            ta = es3.enter_context(nc.sbuf_tensor("p3ta", [128, D], F32))
            tb = es3.enter_context(nc.sbuf_tensor("p3tb", [128, D], F32))
            yb16 = [es3.enter_context(nc.sbuf_tensor(f"p3yb{i}", [128, D], BF16)) for i in range(2)]
            ypT = [es3.enter_context(nc.sbuf_tensor(f"p3ypT{i}", [128, 8, 128], BF16)) for i in range(2)]
            x2t = [es3.enter_context(nc.sbuf_tensor(f"p3x2{i}", [128, D], F32)) for i in range(2)]
            B_x2t = [mkbuf(f"p3x2{i}") for i in range(2)]
            junk = es3.enter_context(nc.sbuf_tensor("p3junk", [128, D], BF16))
            st3 = [es3.enter_context(nc.sbuf_tensor(f"p3st{i}", [128, 4], F32)) for i in range(2)]
            h2b = [es3.enter_context(nc.sbuf_tensor(f"p3h2b{i}", [128, D], BF16)) for i in range(2)]
            B_ta, B_tb, B_junk = Buf(), Buf(), Buf()
            B_yb, B_ypT, B_st3, B_h2b = [Buf(), Buf()], [Buf(), Buf()], [Buf(), Buf()], [Buf(), Buf()]
            B_x2scr = Buf()

            def load3(t):
                i = t % N3
                emit_dma(SP, B_xt[i].dsem, xt[i][:], x_own.ap()[t * 128:(t + 1) * 128, :], writes=[B_xt[i]])
                emit_dma(SP, B_gt[i].dsem, gt[i][:], g_scr.ap()[t * 128:(t + 1) * 128, :], reads=[B_gscr], writes=[B_gt[i]])

            def p3_A(t):
                i, i4 = t % 2, t % N3
                tok = slice(t * 128, (t + 1) * 128)
                YA, YB = PS[0], PS[1]
                fa, fb = [], []
                for h in range(8):
                    for hf in range(2):
                        fa.append(lambda e, h=h, hf=hf: e.matmul(YA[:, hf * 512:(hf + 1) * 512], lhsT=oaT[:, h, tok], rhs=Wpa[:, h, hf * 512:(hf + 1) * 512],
                                                                 start=(h == 0), stop=(h == 7)))
                for h in range(4):
                    for hf in range(2):
                        fb.append(lambda e, h=h, hf=hf: e.matmul(YB[:, hf * 512:(hf + 1) * 512], lhsT=obT[:, h, tok], rhs=Wpb[:, h, hf * 512:(hf + 1) * 512],
                                                                 start=(h == 0), stop=(h == 3)))
                emit_group(PE, fa, reads=[B_oaT, B_Wpa], writes=PSB[0])
                emit_group(PE, fb, reads=[B_obT, B_Wpb], writes=PSB[1])
                emit(DVE, lambda e: e.tensor_tensor(out=ta[:], in0=YA[:], in1=gt[i4][:, 0:1024], op=ALU.mult), reads=PSB[0] + [B_gt[i4]], writes=[B_ta])
                emit(DVE, lambda e: e.tensor_tensor(out=tb[:], in0=YB[:], in1=gt[i4][:, 1024:2048], op=ALU.mult), reads=PSB[1] + [B_gt[i4]], writes=[B_tb])
                emit(DVE, lambda e: e.tensor_tensor(out=yb16[i][:], in0=ta[:], in1=tb[:], op=ALU.add), reads=[B_ta, B_tb], writes=[B_yb[i]])

            def p3_A2(t):
                i = t % 2
                emit_group(PE, [(lambda e, c=c: e.transpose(pT3_ap[:, c * 128:(c + 1) * 128], yb16[i][:, c * 128:(c + 1) * 128], ident[:]))
                                for c in range(8)], reads=[B_yb[i], B_c2], writes=[B_pT3])
                emit(ACT, lambda e: e.activation(out=ypT[i][:].rearrange("p c n -> p (c n)"), in_=pT3_ap, func=AF.Copy), reads=[B_pT3], writes=[B_ypT[i]])

            def p3_B(t):
                i, i4 = t % 2, t % N3
                tok = slice(t * 128, (t + 1) * 128)
                XO = PS[2]
                fo = []
                for c in range(8):
                    for hf in range(2):
                        fo.append(lambda e, c=c, hf=hf: e.matmul(XO[:, hf * 512:(hf + 1) * 512], lhsT=ypT[i][:, c, :], rhs=Wout[:, c, hf * 512:(hf + 1) * 512],
                                                                 start=(c == 0), stop=(c == 7)))
                emit_group(PE, fo, reads=[B_ypT[i], B_Wout], writes=PSB[2])
                emit(DVE, lambda e: e.tensor_tensor(out=x2t[i][:], in0=XO[:], in1=xt[i4][:], op=ALU.add), reads=PSB[2] + [B_xt[i4]], writes=[B_x2t[i]])
                emit_dma(SP, B_x2t[i].dsem, x2_scr.ap()[tok, :], x2t[i][:], reads=[B_x2t[i]], writes=[B_x2scr])
                st = st3[i]
                emit(DVE, lambda e: e.memset(st[:, 0:1], 0.0), writes=[B_st3[i]])
                emit(ACT, lambda e: e.activation(out=junk[:], in_=x2t[i][:], func=AF.Square, accum_out=st[:, 0:1]), reads=[B_x2t[i], B_st3[i]], writes=[B_junk, B_st3[i]])
                emit(ACT, lambda e: e.activation(out=st[:, 1:2], in_=st[:, 0:1], func=AF.Ln, scale=1.0 / D, bias=eps_t[:, 0:1]), reads=[B_st3[i], B_c2], writes=[B_st3[i]])
                emit(ACT, lambda e: e.activation(out=st[:, 2:3], in_=st[:, 1:2], func=AF.Exp, scale=-0.5), reads=[B_st3[i]], writes=[B_st3[i]])
                emit(DVE, lambda e: e.scalar_tensor_tensor(out=h2b[i][:], in0=x2t[i][:], scalar=st[:, 2:3], in1=gffn_b, op0=ALU.mult, op1=ALU.mult),
                     reads=[B_x2t[i], B_st3[i], B_const], writes=[B_h2b[i]])

            def p3_B2(t):
                i = t % 2
                tok = slice(t * 128, (t + 1) * 128)
                emit_group(PE, [(lambda e, c=c: e.transpose(pKT_ap[:, c * 128:(c + 1) * 128], h2b[i][:, c * 128:(c + 1) * 128], ident[:]))
                                for c in range(8)], reads=[B_h2b[i], B_c2], writes=[B_pKT])
                emit(ACT, lambda e: e.activation(out=h2T[:, :, tok], in_=pKT_ap.rearrange("p (c n) -> p c n", c=8), func=AF.Copy), reads=[B_pKT], writes=[B_U1])

            pipeline(16, [load3, p3_A, p3_B, p3_A2, p3_B2], [0, 1, 3, 2, 4])
            B_p3_all = B_xt + B_gt + B_x2t + B_yb + B_ypT + B_st3 + B_h2b + [B_ta, B_tb, B_junk, B_Wpa, B_Wpb, B_Wout, B_oaT, B_obT]
        barrier(B_p3_all, ALL)

        aT = U2[:, 0:22 * 1024].rearrange("p (k n) -> p k n", k=22)
        B_aT = Buf()
        SCW = [512, 512, 512, 512, 512, 256]
        with ExitStack() as es4:
            Wd = es4.enter_context(nc.sbuf_tensor("Wd", [128, 22, D], BF16))
            B_Wd = mkbuf("Wd")
            Wgs = [es4.enter_context(nc.sbuf_tensor(f"Wgs{i}", [128, 8, 512], BF16)) for i in range(2)]
            Wus = [es4.enter_context(nc.sbuf_tensor(f"Wus{i}", [128, 8, 512], BF16)) for i in range(2)]
            B_Wgs = [mkbuf(f"Wgs{i}") for i in range(2)]
            B_Wus = [mkbuf(f"Wus{i}") for i in range(2)]
            ef = [es4.enter_context(nc.sbuf_tensor(f"p4e{i}", [128, 512], F32)) for i in range(2)]
            B_ef = [Buf(), Buf()]
            x2t = [es4.enter_context(nc.sbuf_tensor(f"p4x2{i}", [128, D], F32)) for i in range(2)]
            B_x2t = [mkbuf(f"p4x2{i}") for i in range(2)]
            x3 = es4.enter_context(nc.sbuf_tensor("p4x3", [128, D], F32))
            outt = [es4.enter_context(nc.sbuf_tensor(f"p4o{i}", [128, D], F32)) for i in range(2)]
            B_outt = [mkbuf(f"p4o{i}") for i in range(2)]
            junk = es4.enter_context(nc.sbuf_tensor("p4junk", [128, D], BF16))
            st4 = es4.enter_context(nc.sbuf_tensor("p4st", [128, 4], F32))
            B_x3, B_junk, B_st4 = Buf(), Buf(), Buf()
            B_out = Buf()

            sc_list = [(grp, sc) for grp in range(2) for sc in range(6)]

            def load_sc(idx):
                grp, sc = sc_list[idx]
                i = idx % 2
                w = SCW[sc]
                c0 = sc * 512
                emit_dma(POOL, B_Wgs[i].dsem, Wgs[i][:, :, 0:w], wview(w_gate)[:, :, c0:c0 + w], writes=[B_Wgs[i]])
                emit_dma(POOL, B_Wus[i].dsem, Wus[i][:, :, 0:w], wview(w_up)[:, :, c0:c0 + w], writes=[B_Wus[i]])

            load_sc(0)
            emit_dma(POOL, B_Wd.dsem, Wd[:], wview(w_down), writes=[B_Wd])
            pp = 0
            tile_ctr = 0
            for idx, (grp, sc) in enumerate(sc_list):
                i = idx % 2
                if idx + 1 < len(sc_list):
                    load_sc(idx + 1)
                for kk in range(SCW[sc] // 128):
                    k = sc * 4 + kk
                    for hf in range(2):
                        GU = PS[pp % 2]
                        B_gu = PSB[pp % 2]
                        e_t, B_e = ef[pp % 2], B_ef[pp % 2]
                        pp += 1
                        tk = slice(grp * 1024 + hf * 512, grp * 1024 + (hf + 1) * 512)
                        fg = []
                        for c in range(8):
                            fg.append(lambda e, c=c, GU=GU, kk=kk, i=i, tk=tk: e.matmul(GU[:, 0:512], lhsT=Wgs[i][:, c, kk * 128:(kk + 1) * 128], rhs=h2T[:, c, tk],
                                                                                      start=(c == 0), stop=(c == 7)))
                        for c in range(8):
                            fg.append(lambda e, c=c, GU=GU, kk=kk, i=i, tk=tk: e.matmul(GU[:, 512:1024], lhsT=Wus[i][:, c, kk * 128:(kk + 1) * 128], rhs=h2T[:, c, tk],
                                                                                      start=(c == 0), stop=(c == 7)))
                        emit_group(PE, fg, reads=[B_Wgs[i], B_Wus[i], B_U1], writes=B_gu)
                        emit(ACT, lambda e, GU=GU, e_t=e_t: e.activation(out=e_t[:], in_=GU[:, 0:512], func=AF.Silu), reads=B_gu, writes=[B_e])
                        emit(DVE, lambda e, e_t=e_t, GU=GU, k=k, hf=hf: e.tensor_tensor(out=aT[:, k, hf * 512:(hf + 1) * 512], in0=e_t[:], in1=GU[:, 512:1024], op=ALU.mult),
                             reads=[B_e] + B_gu, writes=[B_aT])
                if sc == 5:
                    for tt in range(8):
                        t = grp * 8 + tt
                        i2 = tile_ctr % 2
                        tile_ctr += 1
                        tok = slice(t * 128, (t + 1) * 128)
                        emit_dma(SP, B_x2t[i2].dsem, x2t[i2][:], x2_scr.ap()[tok, :], reads=[B_x2scr], writes=[B_x2t[i2]])
                        Y = PS[2 + tt % 2]
                        B_y = PSB[2 + tt % 2]
                        fd = []
                        for k in range(22):
                            for hf in range(2):
                                fd.append(lambda e, k=k, hf=hf, Y=Y, tt=tt: e.matmul(Y[:, hf * 512:(hf + 1) * 512], lhsT=aT[:, k, tt * 128:(tt + 1) * 128],
                                                                                   rhs=Wd[:, k, hf * 512:(hf + 1) * 512], start=(k == 0), stop=(k == 21)))
                        emit_group(PE, fd, reads=[B_aT, B_Wd], writes=B_y)
                        emit(DVE, lambda e, Y=Y, i2=i2: e.tensor_tensor(out=x3[:], in0=Y[:], in1=x2t[i2][:], op=ALU.add), reads=B_y + [B_x2t[i2]], writes=[B_x3])
                        emit(DVE, lambda e: e.memset(st4[:, 0:1], 0.0), writes=[B_st4])
                        emit(ACT, lambda e: e.activation(out=junk[:], in_=x3[:], func=AF.Square, accum_out=st4[:, 0:1]), reads=[B_x3, B_st4], writes=[B_junk, B_st4])
                        emit(ACT, lambda e: e.activation(out=st4[:, 1:2], in_=st4[:, 0:1], func=AF.Ln, scale=1.0 / D, bias=eps_t[:, 0:1]), reads=[B_st4, B_c2], writes=[B_st4])
                        emit(ACT, lambda e: e.activation(out=st4[:, 2:3], in_=st4[:, 1:2], func=AF.Exp, scale=-0.5), reads=[B_st4], writes=[B_st4])
                        emit(DVE, lambda e, i2=i2: e.scalar_tensor_tensor(out=outt[i2][:], in0=x3[:], scalar=st4[:, 2:3], in1=gfin_b, op0=ALU.mult, op1=ALU.mult),
                             reads=[B_x3, B_st4, B_const], writes=[B_outt[i2]])
                        emit_dma(SP, B_outt[i2].dsem, out.ap()[tok, :], outt[i2][:], reads=[B_outt[i2]], writes=[B_out])
            for sc_, v in B_out.w.items():
                SP.wait(sc_, v)
            if debug:
                for sc_, v in B_dbg.r.items():
                    SP.wait(sc_, v)
    return nc


_NC_CACHE = {}


def _rope_tab(ang_cs):
    c, s = ang_cs
    return np.concatenate([c, c], axis=1), np.concatenate([-s, s], axis=1)


def _rope_angles_like_reference():
    try:
        import jax
        import jax.numpy as jnp
        with jax.default_device(jax.devices("cpu")[0]):
            def rope_angles(pos, dim, theta):
                inv_freq = theta ** (-jnp.arange(0, dim, 2, dtype=jnp.float32) / dim)
                return pos[:, None] * inv_freq[None, :]
            rows = S // 64
            row = jnp.broadcast_to(jnp.arange(rows, dtype=jnp.float32)[:, None], (rows, 64)).reshape(-1)
            col = jnp.broadcast_to(jnp.arange(64, dtype=jnp.float32)[None, :], (rows, 64)).reshape(-1)
            ang_a = jnp.concatenate([rope_angles(row, 32, 10000.0), rope_angles(col, 32, 10000.0)], axis=-1)
            ang_1 = rope_angles(jnp.arange(S, dtype=jnp.float32), 64, 10000.0)
            out = [(np.asarray(jnp.cos(a), dtype=np.float32), np.asarray(jnp.sin(a), dtype=np.float32)) for a in (ang_a, ang_1)]
        return out[0], out[1]
    except Exception:
        pos = np.arange(S, dtype=np.float32)
        inv16 = (np.float32(10000.0) ** (-np.arange(0, 32, 2, dtype=np.float32) / np.float32(32))).astype(np.float32)
        inv32 = (np.float32(10000.0) ** (-np.arange(0, 64, 2, dtype=np.float32) / np.float32(64))).astype(np.float32)
        row = np.floor(pos / 64).astype(np.float32)
        col = (pos - row * 64).astype(np.float32)
        ang_a = np.concatenate([row[:, None] * inv16[None, :], col[:, None] * inv16[None, :]], axis=1).astype(np.float32)
        ang_1 = (pos[:, None] * inv32[None, :]).astype(np.float32)
        return (np.cos(ang_a), np.sin(ang_a)), (np.cos(ang_1), np.sin(ang_1))


def _host_inputs(inputs):
    f = lambda a: np.ascontiguousarray(np.asarray(a, dtype=np.float32))
    x = f(inputs["x"])
    w_in = f(inputs["w_in"])[0]
    w_kv = np.concatenate([w_in[:, 512:640], w_in[:, 640:768], w_in[:, 1280:1792], w_in[:, 1792:2304]], axis=1)
    order = [0, 4, 1, 5, 2, 6, 3, 7]
    aq = np.concatenate([w_in[:, h * 64:(h + 1) * 64] for h in order], axis=1)
    w_q = np.concatenate([aq, w_in[:, 768:1280]], axis=1)
    w_g = w_in[:, 2304:4352]
    vecs = np.concatenate([f(inputs["norm_mix"])[0], f(inputs["norm_ffn"])[0], f(inputs["norm_final"]),
                           f(inputs["q_norm_a"])[0], f(inputs["k_norm_a"])[0],
                           f(inputs["lambda_q1"])[0], f(inputs["lambda_k1"])[0], f(inputs["lambda_q2"])[0], f(inputs["lambda_k2"])[0]])[None, :]
    angA, ang1 = _rope_angles_like_reference()
    tcA, tsA = _rope_tab(angA)
    tc1, ts1 = _rope_tab(ang1)
    tab = np.ascontiguousarray(np.concatenate([tcA, tsA, tc1, ts1], axis=1).astype(np.float32))
    common = {
        "w_kv": np.ascontiguousarray(w_kv), "w_q": np.ascontiguousarray(w_q), "w_g": np.ascontiguousarray(w_g),
        "w_pa": f(inputs["w_proj_a"])[0], "w_pb": f(inputs["w_proj_b"])[0], "w_out": f(inputs["w_out"])[0],
        "w_gate": f(inputs["w_gate_ffn"])[0], "w_up": f(inputs["w_up_ffn"])[0], "w_down": f(inputs["w_down_ffn"])[0],
        "vecs": np.ascontiguousarray(vecs), "subln_col": np.ascontiguousarray(f(inputs["subln_b"])[0][:, None]),
        "ident": np.eye(128, dtype=np.float32), "tab_all": tab,
    }
    in_maps = []
    for c in range(8):
        b, qb = c // 4, c % 4
        m = dict(common)
        m["x_all"] = x[b]
        m["x_own"] = np.ascontiguousarray(x[b, qb * NOWN:(qb + 1) * NOWN])
        m["tab_own"] = np.ascontiguousarray(tab[qb * NOWN:(qb + 1) * NOWN])
        in_maps.append(m)
    return in_maps


def kernel(**inputs):
    in_maps = _host_inputs(inputs)
    if "nc" not in _NC_CACHE:
        _NC_CACHE["nc"] = build_program()
    nc = _NC_CACHE["nc"]
    res = run_bass_kernel_spmd(nc, in_maps, core_ids=list(range(8)))
    outp = np.empty((2, S, D), dtype=np.float32)
    for c in range(8):
        b, qb = c // 4, c % 4
        outp[b, qb * NOWN:(qb + 1) * NOWN] = res.results[c]["out"]
    return outp
```

```python
import math
from contextlib import ExitStack

import numpy as np
import concourse.bass as bass
import concourse.mybir as mybir
from concourse.bass_utils import run_bass_kernel_spmd

F32 = mybir.dt.float32
BF16 = mybir.dt.bfloat16
AF = mybir.ActivationFunctionType
ALU = mybir.AluOpType
AX = mybir.AxisListType

S = 8192
D = 1024
NOWN = 2048
DFF = 2816
EPS = 1e-6
LAMBDA_INIT = 0.8 - 0.6 * math.exp(-0.3 * 0)
NVEC = 3 * 1024 + 2 * 64 + 256


class SemC:
    def __init__(self, nc, es, name):
        self.sem = es.enter_context(nc.semaphore(name))
        self.cnt = 0


class Eng(SemC):
    def __init__(self, nc, es, name, eng):
        super().__init__(nc, es, name)
        self.e = eng
        self.seen = {}

    def wait(self, sc, v):
        if self.seen.get(sc, 0) >= v:
            return
        self.e.wait_ge(sc.sem, v)
        self.seen[sc] = v


class Buf:
    def __init__(self, nc=None, es=None, name=None):
        self.w = {}
        self.r = {}
        self.dsem = SemC(nc, es, "d_" + name) if nc is not None else None


STRICT = True


def _pre(E, reads, writes):
    for b in reads:
        for sc, v in b.w.items():
            E.wait(sc, v)
    for b in writes:
        for sc, v in b.w.items():
            if STRICT or sc is not E:
                E.wait(sc, v)
        for sc, v in b.r.items():
            if STRICT or sc is not E:
                E.wait(sc, v)


def emit(E, fn, reads=(), writes=()):
    _pre(E, reads, writes)
    ins = fn(E.e)
    E.cnt += 1
    ins.then_inc(E.sem, 1)
    for b in reads:
        b.r[E] = E.cnt
    for b in writes:
        b.w[E] = E.cnt


def emit_group(E, fns, reads=(), writes=()):
    _pre(E, reads, writes)
    ins = None
    for fn in fns:
        ins = fn(E.e)
    E.cnt += 1
    ins.then_inc(E.sem, 1)
    for b in reads:
        b.r[E] = E.cnt
    for b in writes:
        b.w[E] = E.cnt


def emit_dma(Q, dsem, out, in_, reads=(), writes=()):
    _pre(Q, reads, writes)
    Q.e.dma_start(out=out, in_=in_).then_inc(dsem.sem, 16)
    dsem.cnt += 16
    for b in reads:
        b.r[dsem] = dsem.cnt
    for b in writes:
        b.w[dsem] = dsem.cnt


def emit_dma_batch(Q, dsem, pairs, reads=(), writes=()):
    _pre(Q, reads, writes)
    for out, in_ in pairs:
        Q.e.dma_start(out=out, in_=in_).then_inc(dsem.sem, 16)
        dsem.cnt += 16
    for b in reads:
        b.r[dsem] = dsem.cnt
    for b in writes:
        b.w[dsem] = dsem.cnt


def bc_mid(ap2d, h):
    (ps, pn), (s, n) = ap2d.ap
    return bass.AP(ap2d.tensor, ap2d.offset, [[ps, pn], [0, h], [s, n]])


def bc_last(ap2d, n):
    (ps, pn), (s, h) = ap2d.ap
    return bass.AP(ap2d.tensor, ap2d.offset, [[ps, pn], [s, h], [0, n]])


def build_program(debug=False):
    nc = bass.Bass("TRN2", target_bir_lowering=False)

    def din(name, shape, dt=F32):
        return nc.dram_tensor(name, shape, dt, kind="ExternalInput")

    x_all = din("x_all", [S, D])
    x_own = din("x_own", [NOWN, D])
    tab_all = din("tab_all", [S, 256])
    tab_own = din("tab_own", [NOWN, 256])
    w_kv = din("w_kv", [D, 1280])
    w_q = din("w_q", [D, 1024])
    w_g = din("w_g", [D, 2048])
    w_pa = din("w_pa", [512, D])
    w_pb = din("w_pb", [512, D])
    w_out = din("w_out", [D, D])
    w_gate = din("w_gate", [D, DFF])
    w_up = din("w_up", [D, DFF])
    w_down = din("w_down", [DFF, D])
    vecs = din("vecs", [1, NVEC])
    subln_col = din("subln_col", [128, 1])
    ident_in = din("ident", [128, 128])
    out = nc.dram_tensor("out", [NOWN, D], F32, kind="ExternalOutput")
    kT_scr = [nc.dram_tensor(f"kT_scr{i}", [128, S], BF16, kind="Internal") for i in range(5)]
    vA_scr = nc.dram_tensor("vA_scr", [128, 64 * 130], BF16, kind="Internal")
    vB_scr = [nc.dram_tensor(f"vB_scr{i}", [128, 64 * 128], BF16, kind="Internal") for i in range(4)]
    g_scr = nc.dram_tensor("g_scr", [NOWN, 2048], BF16, kind="Internal")
    x2_scr = nc.dram_tensor("x2_scr", [NOWN, D], F32, kind="Internal")
    dbg = {}
    if debug:
        dbg["qT"] = nc.dram_tensor("dbg_qT", [128, 8 * NOWN], F32, kind="ExternalOutput")
        dbg["oaT"] = nc.dram_tensor("dbg_oaT", [64, 8 * NOWN], F32, kind="ExternalOutput")
        dbg["obT"] = nc.dram_tensor("dbg_obT", [128, 4 * NOWN], F32, kind="ExternalOutput")

    with ExitStack() as es:
        es.enter_context(nc.allow_low_precision("bf16 matmul operands, fp32 accumulation"))
        PE = Eng(nc, es, "s_pe", nc.tensor)
        ACT = Eng(nc, es, "s_act", nc.scalar)
        DVE = Eng(nc, es, "s_dve", nc.vector)
        POOL = Eng(nc, es, "s_pool", nc.gpsimd)
        SP = Eng(nc, es, "s_sp", nc.sync)

        def sb(name, shape, dt):
            return es.enter_context(nc.sbuf_tensor(name, shape, dt))

        def mkbuf(name):
            return Buf(nc, es, name)

        PS = [es.enter_context(nc.psum_tensor(f"ps{i}", [128, 1024], F32)) for i in range(4)]
        PSB = [[Buf(), Buf()] for _ in range(4)]

        vec_b = sb("vec_b", [128, NVEC], F32)
        ident_f = sb("ident_f", [128, 128], F32)
        ident = sb("ident_b", [128, 128], BF16)
        ones_f = sb("ones_f", [128, 128], F32)
        ones_b = sb("ones_b", [128, 128], BF16)
        eps_t = sb("eps_t", [128, 1], F32)
        cst = sb("cst", [128, 16], F32)
        U1 = sb("U1", [128, 8, NOWN], BF16)
        U2 = sb("U2", [128, 12 * NOWN], BF16)
        B_const = mkbuf("const")
        B_U1 = Buf()
        B_oaT = Buf()
        B_obT = Buf()
        gmix_b = vec_b[:, 0:1024]
        gffn_b = vec_b[:, 1024:2048]
        gfin_b = vec_b[:, 2048:3072]
        gq_b = vec_b[:, 3072:3136]
        gk_b = vec_b[:, 3136:3200]
        oaT = U2[0:64, 0:8 * NOWN].rearrange("p (h n) -> p h n", h=8)
        obT = U2[:, 8 * NOWN:12 * NOWN].rearrange("p (h n) -> p h n", h=4)

        vb_src = bass.AP(vecs, 0, [[0, 128], [1, NVEC]])
        emit_dma(SP, B_const.dsem, vec_b[:], vb_src, writes=[B_const])
        emit_dma(SP, B_const.dsem, ident_f[:], ident_in.ap(), writes=[B_const])
        emit_dma(SP, B_const.dsem, cst[:, 5:6], subln_col.ap(), writes=[B_const])
        B_c2 = Buf()
        emit(DVE, lambda e: e.memset(ones_f[:], 1.0), writes=[B_c2])
        emit(DVE, lambda e: e.memset(ones_b[:], 1.0), writes=[B_c2])
        emit(DVE, lambda e: e.memset(eps_t[:], EPS), writes=[B_c2])
        emit(DVE, lambda e: e.tensor_copy(ident[:], ident_f[:]), reads=[B_const], writes=[B_c2])
        lam_w = sb("lam_w", [128, 128], F32)
        B_lam = Buf()
        emit(DVE, lambda e: e.tensor_tensor(out=lam_w[:, 0:64], in0=vec_b[:, 3200:3264], in1=vec_b[:, 3264:3328], op=ALU.mult), reads=[B_const], writes=[B_lam])
        emit(DVE, lambda e: e.tensor_tensor(out=lam_w[:, 64:128], in0=vec_b[:, 3328:3392], in1=vec_b[:, 3392:3456], op=ALU.mult), reads=[B_const], writes=[B_lam])
        emit(DVE, lambda e: e.tensor_reduce(out=cst[:, 0:2], in_=lam_w[:].rearrange("p (a n) -> p a n", a=2), axis=AX.X, op=ALU.add), reads=[B_lam], writes=[B_lam])
        emit(ACT, lambda e: e.activation(out=cst[:, 2:4], in_=cst[:, 0:2], func=AF.Exp), reads=[B_lam], writes=[B_lam])
        emit(DVE, lambda e: e.tensor_tensor(out=cst[:, 4:5], in0=cst[:, 3:4], in1=cst[:, 2:3], op=ALU.subtract), reads=[B_lam], writes=[B_lam])
        emit(DVE, lambda e: e.tensor_scalar(out=cst[:, 4:5], in0=cst[:, 4:5], scalar1=-LAMBDA_INIT, scalar2=None, op0=ALU.add), reads=[B_lam], writes=[B_lam])
        emit(DVE, lambda e: e.tensor_scalar(out=cst[:, 5:6], in0=cst[:, 5:6], scalar1=1.0 - LAMBDA_INIT, scalar2=None, op0=ALU.mult), reads=[B_const, B_lam], writes=[B_lam])
        neglam_col = cst[:, 4:5]
        gcol = cst[:, 5:6]
        B_cst_all = [B_const, B_c2, B_lam]

        def wview(w, p=128):
            return w.ap().rearrange("(c p) n -> p c n", p=p)

        def pipeline(n, stages, skews):
            for s_ in range(n + max(skews)):
                for fn, sk in zip(stages, skews):
                    t_ = s_ - sk
                    if 0 <= t_ < n:
                        fn(t_)

        NXT, NXNT, NTAB = 4, 4, 8

        def phase_front(es2, pfx):
            r = {}
            r["xt"] = [es2.enter_context(nc.sbuf_tensor(pfx + f"xt{i}", [128, D], F32)) for i in range(NXT)]
            r["B_xt"] = [mkbuf(pfx + f"xt{i}") for i in range(NXT)]
            r["junk"] = es2.enter_context(nc.sbuf_tensor(pfx + "junk", [128, D], BF16))
            r["B_junk"] = Buf()
            r["st"] = [es2.enter_context(nc.sbuf_tensor(pfx + f"st{i}", [128, 4], F32)) for i in range(2)]
            r["B_st"] = [Buf(), Buf()]
            r["xn"] = [es2.enter_context(nc.sbuf_tensor(pfx + f"xn{i}", [128, D], BF16)) for i in range(2)]
            r["B_xn"] = [Buf(), Buf()]
            r["xnT"] = [es2.enter_context(nc.sbuf_tensor(pfx + f"xnT{i}", [128, 8, 128], BF16)) for i in range(NXNT)]
            r["B_xnT"] = [Buf() for _ in range(NXNT)]
            return r

        def rmsnorm_1(F, t, gain_b):
            xt, st = F["xt"][t % NXT], F["st"][t % 2]
            B_xt, B_st = F["B_xt"][t % NXT], F["B_st"][t % 2]
            xn, B_xn = F["xn"][t % 2], F["B_xn"][t % 2]
            emit(DVE, lambda e: e.memset(st[:, 0:1], 0.0), writes=[B_st])
            emit(ACT, lambda e: e.activation(out=F["junk"][:], in_=xt[:], func=AF.Square, accum_out=st[:, 0:1]),
                 reads=[B_xt, B_st], writes=[F["B_junk"], B_st])
            emit(ACT, lambda e: e.activation(out=st[:, 1:2], in_=st[:, 0:1], func=AF.Ln, scale=1.0 / D, bias=eps_t[:, 0:1]),
                 reads=[B_st, B_c2], writes=[B_st])
            emit(ACT, lambda e: e.activation(out=st[:, 2:3], in_=st[:, 1:2], func=AF.Exp, scale=-0.5),
                 reads=[B_st], writes=[B_st])
            emit(DVE, lambda e: e.scalar_tensor_tensor(out=xn[:], in0=xt[:], scalar=st[:, 2:3], in1=gain_b,
                                                       op0=ALU.mult, op1=ALU.mult),
                 reads=[B_xt, B_st, B_const], writes=[B_xn])

        def rmsnorm_2(F, t, pT_ap, B_pT):
            xn, B_xn = F["xn"][t % 2], F["B_xn"][t % 2]
            xnT, B_xnT = F["xnT"][t % NXNT], F["B_xnT"][t % NXNT]
            emit_group(PE, [(lambda e, c=c: e.transpose(pT_ap[:, c * 128:(c + 1) * 128], xn[:, c * 128:(c + 1) * 128], ident[:]))
                            for c in range(8)], reads=[B_xn, B_c2], writes=[B_pT])
            emit(ACT, lambda e: e.activation(out=xnT[:].rearrange("p c n -> p (c n)"), in_=pT_ap, func=AF.Copy),
                 reads=[B_pT], writes=[B_xnT])

        def rope(z_ap, H, tc, ts, t1, t2, B_z, B_tab, B_t, out_ap, B_out, rs=None, B_rs=None):
            z3 = z_ap.rearrange("p (h d) -> p h d", h=H)
            t13 = t1[:, 0:H * 64].rearrange("p (h d) -> p h d", h=H)
            t23 = t2[:, 0:H * 64].rearrange("p (h d) -> p h d", h=H)
            o3 = out_ap.rearrange("p (h d) -> p h d", h=H)
            B_t1, B_t2 = B_t
            emit(DVE, lambda e: e.tensor_tensor(out=t13, in0=z3, in1=bc_mid(tc, H), op=ALU.mult),
                 reads=B_z + [B_tab], writes=[B_t1])
            emit(DVE, lambda e: e.tensor_tensor(out=t23[:, :, 0:32], in0=z3[:, :, 32:64], in1=bc_mid(ts[:, 0:32], H), op=ALU.mult),
                 reads=B_z + [B_tab], writes=[B_t2[0]])
            emit(DVE, lambda e: e.tensor_tensor(out=t23[:, :, 32:64], in0=z3[:, :, 0:32], in1=bc_mid(ts[:, 32:64], H), op=ALU.mult),
                 reads=B_z + [B_tab], writes=[B_t2[1]])
            if rs is None:
                emit(DVE, lambda e: e.tensor_tensor(out=o3, in0=t13, in1=t23, op=ALU.add), reads=[B_t1] + B_t2, writes=[B_out])
            else:
                emit(DVE, lambda e: e.tensor_tensor(out=t23, in0=t13, in1=t23, op=ALU.add), reads=[B_t1] + B_t2, writes=B_t2)
                emit(DVE, lambda e: e.tensor_tensor(out=o3, in0=t23, in1=bc_last(rs, 64), op=ALU.mult),
                     reads=B_t2 + [B_rs], writes=[B_out])

        def head_rs(z_ap, H, sq, hs, B_z, B_sq, B_hs):
            emit(ACT, lambda e: e.activation(out=sq[:, 0:H * 64], in_=z_ap, func=AF.Square), reads=B_z, writes=[B_sq])
            emit(DVE, lambda e: e.tensor_reduce(out=hs[:, 0:H], in_=sq[:, 0:H * 64].rearrange("p (h d) -> p h d", h=H), axis=AX.X, op=ALU.add),
                 reads=[B_sq], writes=[B_hs])
            emit(ACT, lambda e: e.activation(out=hs[:, H:2 * H], in_=hs[:, 0:H], func=AF.Ln, scale=1.0 / 64, bias=eps_t[:, 0:1]),
                 reads=[B_hs, B_c2], writes=[B_hs])
            emit(ACT, lambda e: e.activation(out=hs[:, 2 * H:3 * H], in_=hs[:, H:2 * H], func=AF.Exp, scale=-0.5),
                 reads=[B_hs], writes=[B_hs])
            return hs[:, 2 * H:3 * H]

        pT_ap = PS[1][:, 512:1024].bitcast(BF16)
        B_pT = PSB[1][1]
        pKT_ap = PS[3][:, 512:1024].bitcast(BF16)
        B_pKT = PSB[3][1]

        def barrier(bufs, engines):
            for E in engines:
                for b in bufs:
                    for sc, v in list(b.w.items()) + list(b.r.items()):
                        if sc is not E:
                            E.wait(sc, v)

        ALL = [PE, ACT, DVE, POOL, SP]
        Wq = U2[:, 0:8192].rearrange("p (c n) -> p c n", c=8)
        Wg = U2[:, 8192:24576].rearrange("p (c n) -> p c n", c=8)
        B_Wq, B_Wg = mkbuf("Wq"), mkbuf("Wg")

        with ExitStack() as es1:
            F = phase_front(es1, "p1")
            Wkv = es1.enter_context(nc.sbuf_tensor("Wkv", [128, 8, 1280], BF16))
            B_Wkv = mkbuf("Wkv")
            emit_dma(POOL, B_Wkv.dsem, Wkv[:], wview(w_kv), writes=[B_Wkv])
            emit_dma(POOL, B_Wq.dsem, Wq, wview(w_q), writes=[B_Wq])
            emit_dma(POOL, B_Wg.dsem, Wg, wview(w_g), writes=[B_Wg])
            tabt = [es1.enter_context(nc.sbuf_tensor(f"tabt{i}", [128, 256], F32)) for i in range(NTAB)]
            B_tab = [mkbuf(f"tabt{i}") for i in range(NTAB)]
            t1 = es1.enter_context(nc.sbuf_tensor("t1", [128, 512], F32))
            t2 = es1.enter_context(nc.sbuf_tensor("t2", [128, 512], F32))
            sqs = es1.enter_context(nc.sbuf_tensor("sqs", [128, 512], F32))
            kg = es1.enter_context(nc.sbuf_tensor("kg", [128, 128], F32))
            hs = es1.enter_context(nc.sbuf_tensor("hs", [128, 24], F32))
            B_t = (Buf(), [Buf(), Buf()])
            B_kg, B_hs, B_sqs = Buf(), Buf(), Buf()
            kpost = [es1.enter_context(nc.sbuf_tensor(f"kpost{i}", [128, 640], BF16)) for i in range(2)]
            B_kpost = [[Buf(), Buf()] for _ in range(2)]
            kst = [es1.enter_context(nc.sbuf_tensor(f"kst{i}", [128, 5, 512], BF16)) for i in range(2)]
            B_kst = [mkbuf(f"kst{i}") for i in range(2)]
            vst = [es1.enter_context(nc.sbuf_tensor(f"vst{i}", [128, 4, 642], BF16)) for i in range(2)]
            B_vst = [mkbuf(f"vst{i}") for i in range(2)]
            B_kscr = [Buf(), Buf()]
            B_vscr = [Buf(), Buf()]
            for i in range(2):
                emit(DVE, lambda e, i=i: e.memset(vst[i][:], 1.0), writes=[B_vst[i]])

            def load1(t):
                emit_dma(SP, F["B_xt"][t % NXT].dsem, F["xt"][t % NXT][:], x_all.ap()[t * 128:(t + 1) * 128, :], writes=[F["B_xt"][t % NXT]])
                emit_dma(SP, B_tab[t % NTAB].dsem, tabt[t % NTAB][:], tab_all.ap()[t * 128:(t + 1) * 128, :], writes=[B_tab[t % NTAB]])

            def p1_A1(t):
                rmsnorm_1(F, t, gmix_b)

            def p1_A2(t):
                rmsnorm_2(F, t, pT_ap, B_pT)

            def p1_zs(t):
                zs = t % 2
                return PS[2 * zs], PS[2 * zs + 1], [PSB[2 * zs][0], PSB[2 * zs][1], PSB[2 * zs + 1][0]]

            def p1_B(t):
                Z0, Z1, B_z = p1_zs(t)
                xnT = F["xnT"][t % NXNT]
                fns = []
                for c in range(8):
                    fns.append(lambda e, c=c: e.matmul(Z0[:, 0:512], lhsT=xnT[:, c, :], rhs=Wkv[:, c, 0:512], start=(c == 0), stop=(c == 7)))
                    fns.append(lambda e, c=c: e.matmul(Z0[:, 512:1024], lhsT=xnT[:, c, :], rhs=Wkv[:, c, 512:1024], start=(c == 0), stop=(c == 7)))
                    fns.append(lambda e, c=c: e.matmul(Z1[:, 0:256], lhsT=xnT[:, c, :], rhs=Wkv[:, c, 1024:1280], start=(c == 0), stop=(c == 7)))
                emit_group(PE, fns, reads=[F["B_xnT"][t % NXNT], B_Wkv], writes=B_z)

            def p1_C1(t):
                Z0, Z1, B_z = p1_zs(t)
                i = t % 2
                tb_, B_tb_ = tabt[t % NTAB], B_tab[t % NTAB]
                g4, s4 = t // 4, (t // 4) % 2
                j4 = t % 4
                vrow = vst[s4][:, j4, :]
                emit(ACT, lambda e: e.activation(out=vrow[:, 0:130].rearrange("p (h d) -> p h d", h=2)[:, :, 0:64],
                                                 in_=Z0[:, 128:256].rearrange("p (h d) -> p h d", h=2), func=AF.Copy),
                     reads=[B_z[0]], writes=[B_vst[s4]])
                emit(ACT, lambda e: e.activation(out=vrow[:, 130:386], in_=Z0[:, 768:1024], func=AF.Copy), reads=[B_z[1]], writes=[B_vst[s4]])
                emit(ACT, lambda e: e.activation(out=vrow[:, 386:642], in_=Z1[:, 0:256], func=AF.Copy), reads=[B_z[2]], writes=[B_vst[s4]])
                rk = head_rs(Z0[:, 0:128], 2, sqs, hs, [B_z[0]], B_sqs, B_hs)
                emit(DVE, lambda e: e.tensor_tensor(out=kg[:].rearrange("p (h d) -> p h d", h=2), in0=Z0[:, 0:128].rearrange("p (h d) -> p h d", h=2),
                                                    in1=bc_mid(gk_b, 2), op=ALU.mult), reads=[B_z[0], B_const], writes=[B_kg])
                rope(Z0[:, 256:768], 8, tb_[:, 128:192], tb_[:, 192:256], t1, t2, [B_z[0], B_z[1]], B_tb_, B_t, kpost[i][:, 128:640], B_kpost[i][1])
                rope(kg[:], 2, tb_[:, 0:64], tb_[:, 64:128], t1, t2, [B_kg], B_tb_, B_t, kpost[i][:, 0:128], B_kpost[i][0], rs=rk, B_rs=B_hs)

            def p1_C2(t):
                i = t % 2
                g4, s4 = t // 4, (t // 4) % 2
                j4 = t % 4
                emit_group(PE, [(lambda e, b=b: e.transpose(pKT_ap[:, b * 128:(b + 1) * 128], kpost[i][:, b * 128:(b + 1) * 128], ident[:]))
                                for b in range(5)], reads=B_kpost[i] + [B_c2], writes=[B_pKT])
                emit(ACT, lambda e: e.activation(out=kst[s4][:, :, j4 * 128:(j4 + 1) * 128], in_=pKT_ap[:, 0:640].rearrange("p (b n) -> p b n", b=5), func=AF.Copy),
                     reads=[B_pKT], writes=[B_kst[s4]])
                if j4 == 3:
                    emit_dma_batch(SP, B_kst[s4].dsem,
                                   [(kT_scr[b].ap()[:, g4 * 512:(g4 + 1) * 512], kst[s4][:, b, :]) for b in range(5)],
                                   reads=[B_kst[s4]], writes=[B_kscr[g4 % 2]])
                    emit_dma_batch(SP, B_vst[s4].dsem,
                                   [(vA_scr.ap().rearrange("p (t c) -> p t c", c=130)[:, g4 * 4:(g4 + 1) * 4, :], vst[s4][:, :, 0:130])] +
                                   [(vB_scr[h].ap().rearrange("p (t c) -> p t c", c=128)[:, g4 * 4:(g4 + 1) * 4, :],
                                     vst[s4][:, :, 130 + h * 128:130 + (h + 1) * 128]) for h in range(4)],
                                   reads=[B_vst[s4]], writes=[B_vscr[g4 % 2]])

            qT_all = U1
            qg = es1.enter_context(nc.sbuf_tensor("qg", [128, 512], F32))
            ee = [es1.enter_context(nc.sbuf_tensor(f"ee{i}", [128, 512], F32)) for i in range(2)]
            B_ee = [Buf(), Buf()]
            B_qg = Buf()
            qpost = [es1.enter_context(nc.sbuf_tensor(f"qpost{i}", [128, 1024], BF16)) for i in range(2)]
            B_qpost = [[Buf(), Buf()] for _ in range(2)]
            gst = [es1.enter_context(nc.sbuf_tensor(f"gst{i}", [128, 2048], BF16)) for i in range(2)]
            B_gst = [mkbuf(f"gst{i}") for i in range(2)]
            B_gscr = Buf()
            Zs = [PS[0], PS[2]]
            B_Zs = [PSB[0], PSB[2]]
            Gs = [PS[1][:, 0:512], PS[3][:, 0:512]]
            B_Gs = [PSB[1][0], PSB[3][0]]

            def load1b(T):
                t = T - 64
                emit_dma(SP, F["B_xt"][T % NXT].dsem, F["xt"][T % NXT][:], x_own.ap()[t * 128:(t + 1) * 128, :], writes=[F["B_xt"][T % NXT]])
                emit_dma(SP, B_tab[T % NTAB].dsem, tabt[T % NTAB][:], tab_own.ap()[t * 128:(t + 1) * 128, :], writes=[B_tab[T % NTAB]])

            def p1b_A1(T):
                t = T - 64
                rmsnorm_1(F, T, gmix_b)

            def p1b_A2(T):
                t = T - 64
                rmsnorm_2(F, T, pT_ap, B_pT)

            def p1b_B(T):
                t = T - 64
                i = T % 2
                xnT = F["xnT"][T % NXNT]
                B_xnT = F["B_xnT"][T % NXNT]
                Z, B_z = Zs[i], B_Zs[i]
                fns = []
                for c in range(8):
                    fns.append(lambda e, c=c: e.matmul(Z[:, 0:512], lhsT=xnT[:, c, :], rhs=Wq[:, c, 0:512], start=(c == 0), stop=(c == 7)))
                    fns.append(lambda e, c=c: e.matmul(Z[:, 512:1024], lhsT=xnT[:, c, :], rhs=Wq[:, c, 512:1024], start=(c == 0), stop=(c == 7)))
                emit_group(PE, fns, reads=[B_xnT, B_Wq], writes=B_z)

            def p1b_Bg(T):
                t = T - 64
                i = T % 2
                xnT = F["xnT"][T % NXNT]
                B_xnT = F["B_xnT"][T % NXNT]
                for q in range(4):
                    G, B_g = Gs[q % 2], B_Gs[q % 2]
                    emit_group(PE, [(lambda e, c=c, q=q, G=G: e.matmul(G, lhsT=xnT[:, c, :], rhs=Wg[:, c, q * 512:(q + 1) * 512],
                                                                      start=(c == 0), stop=(c == 7))) for c in range(8)],
                               reads=[B_xnT, B_Wg], writes=[B_g])
                    e_t, B_e = ee[q % 2], B_ee[q % 2]
                    emit(ACT, lambda e, G=G, e_t=e_t: e.activation(out=e_t[:], in_=G, func=AF.Exp, scale=-1.0), reads=[B_g], writes=[B_e])
                    emit(ACT, lambda e, e_t=e_t: e.activation(out=e_t[:], in_=e_t[:], func=AF.Ln, bias=ones_f[:, 0:1]), reads=[B_e, B_c2], writes=[B_e])
                    emit(ACT, lambda e, e_t=e_t, q=q: e.activation(out=gst[i][:, q * 512:(q + 1) * 512], in_=e_t[:], func=AF.Exp, scale=-1.0),
                         reads=[B_e], writes=[B_gst[i]])
                emit_dma(SP, B_gst[i].dsem, g_scr.ap()[t * 128:(t + 1) * 128, :], gst[i][:], reads=[B_gst[i]], writes=[B_gscr])

            def p1b_C1(T):
                t = T - 64
                i = T % 2
                Z, B_z = Zs[i], B_Zs[i]
                tb_, B_tb_ = tabt[T % NTAB], B_tab[T % NTAB]
                rq = head_rs(Z[:, 0:512], 8, sqs, hs, [B_z[0]], B_sqs, B_hs)
                emit(DVE, lambda e: e.tensor_tensor(out=qg[:].rearrange("p (h d) -> p h d", h=8), in0=Z[:, 0:512].rearrange("p (h d) -> p h d", h=8),
                                                    in1=bc_mid(gq_b, 8), op=ALU.mult), reads=[B_z[0], B_const], writes=[B_qg])
                rope(Z[:, 512:1024], 8, tb_[:, 128:192], tb_[:, 192:256], t1, t2, [B_z[1]], B_tb_, B_t, qpost[i][:, 512:1024], B_qpost[i][1])
                rope(qg[:], 8, tb_[:, 0:64], tb_[:, 64:128], t1, t2, [B_qg], B_tb_, B_t, qpost[i][:, 0:512], B_qpost[i][0], rs=rq, B_rs=B_hs)

            def p1b_C2(T):
                t = T - 64
                i = T % 2
                emit_group(PE, [(lambda e, b=b: e.transpose(pKT_ap[:, b * 128:(b + 1) * 128], qpost[i][:, b * 128:(b + 1) * 128], ident[:]))
                                for b in range(8)], reads=B_qpost[i] + [B_c2], writes=[B_pKT])
                emit(DVE, lambda e: e.tensor_copy(qT_all[:, :, t * 128:(t + 1) * 128], pKT_ap.rearrange("p (b n) -> p b n", b=8)),
                     reads=[B_pKT], writes=[B_U1])

            def both(f_kv, f_q):
                return lambda T: (f_kv(T) if T < 64 else (f_q(T) if f_q is not None else None)) if (f_kv is not None or T >= 64) else None

            pipeline(80, [both(load1, load1b), both(p1_A1, p1b_A1), both(p1_C1, p1b_C1), both(p1_A2, p1b_A2), both(p1_C2, p1b_C2),
                          both(p1_B, p1b_B), both(None, p1b_Bg)], [0, 2, 6, 3, 7, 5, 5])
            B_p1_all = ([F["B_junk"], B_t[0], B_kg, B_hs, B_sqs, B_Wkv, B_qg, B_Wq, B_Wg] + B_t[1] + F["B_xt"] + F["B_st"] + F["B_xn"] + F["B_xnT"] + B_tab
                        + B_kpost[0] + B_kpost[1] + B_kst + B_vst + B_qpost[0] + B_qpost[1] + B_gst + B_ee)
        barrier(B_p1_all, ALL)
        if debug:
            B_dbg = mkbuf("dbg")

        with ExitStack() as es2:
            kTs = [es2.enter_context(nc.sbuf_tensor(f"kTs{i}", [128, S], BF16)) for i in range(2)]
            Vs = [es2.enter_context(nc.sbuf_tensor(f"Vs{i}", [128, 64 * 130], BF16)) for i in range(2)]
            B_kTs = [mkbuf(f"kTs{i}") for i in range(2)]
            B_Vs = [mkbuf(f"Vs{i}") for i in range(2)]
            NPB = 4
            p_sb = [es2.enter_context(nc.sbuf_tensor(f"p_sb{i}", [128, 1024], BF16)) for i in range(NPB)]
            B_p = [Buf() for _ in range(NPB)]
            o_f = es2.enter_context(nc.sbuf_tensor("o_f", [128, 1024], F32))
            l_f = es2.enter_context(nc.sbuf_tensor("l_f", [128, 1024], F32))
            dd = es2.enter_context(nc.sbuf_tensor("dd", [128, 1024], F32))
            sq = es2.enter_context(nc.sbuf_tensor("sq", [128, 512], F32))
            lnv = es2.enter_context(nc.sbuf_tensor("lnv", [128, 512], F32))
            rs_f = es2.enter_context(nc.sbuf_tensor("rs_f", [128, 512], F32))
            B_of, B_lf, B_dd, B_sq, B_lnv, B_rs = Buf(), Buf(), Buf(), Buf(), Buf(), Buf()
            DX = 128
            accL1 = es2.enter_context(nc.sbuf_tensor("accL1", [128, 512 + DX], F32))
            lnl = es2.enter_context(nc.sbuf_tensor("lnl", [128, 1024], F32))
            B_lnl = Buf()
            B_acc = Buf()

            units = [("A", j) for j in range(4)] + [("B", h) for h in range(4)]
            slot_of = {0: 0, 1: 0, 2: 0, 3: 0, 4: 1, 5: 0, 6: 1, 7: 0}

            def load_unit(u):
                kind, j = units[u]
                s = slot_of[u]
                if kind == "A":
                    if j != 0:
                        return
                    emit_dma(SP, B_kTs[s].dsem, kTs[s][:], kT_scr[0].ap(), reads=B_kscr, writes=[B_kTs[s]])
                    emit_dma(SP, B_Vs[s].dsem, Vs[s][:], vA_scr.ap(), reads=B_vscr, writes=[B_Vs[s]])
                else:
                    emit_dma(SP, B_kTs[s].dsem, kTs[s][:], kT_scr[1 + j].ap(), reads=B_kscr, writes=[B_kTs[s]])
                    emit_dma(SP, B_Vs[s].dsem, Vs[s][:, 0:64 * 128], vB_scr[j].ap(), reads=B_vscr, writes=[B_Vs[s]])

            steps = [(u, qc, kt) for u in range(8) for qc in range(4) for kt in range(64)]
            NS = len(steps)
            Sbuf = [PS[0], PS[1]]
            B_S = [PSB[0], PSB[1]]
            O_ap = PS[2]
            B_O = PSB[2]
            L_ap = PS[3]
            B_L = PSB[3]

            def emit_PV(g):
                u, qc, kt = steps[g]
                kind, j = units[u]
                s = slot_of[u]
                P = p_sb[g % NPB]
                st, sp = (kt == 0), (kt == 63)
                if kind == "A":
                    V3 = Vs[s][:].rearrange("p (t c) -> p t c", c=130)
                    emit_group(PE, [
                        lambda e: e.matmul(O_ap[0:65, 0:512], lhsT=V3[:, kt, 0:65], rhs=P[:, 0:512], start=st, stop=sp),
                        lambda e: e.matmul(O_ap[0:65, 512:1024], lhsT=V3[:, kt, 65:130], rhs=P[:, 512:1024], start=st, stop=sp),
                    ], reads=[B_Vs[s], B_p[g % NPB]], writes=B_O)
                else:
                    V3 = Vs[s][:, 0:64 * 128].rearrange("p (t c) -> p t c", c=128)
                    mm_o = [
                        lambda e: e.matmul(O_ap[:, 0:512], lhsT=V3[:, kt, :], rhs=P[:, 0:512], start=st, stop=sp),
                        lambda e: e.matmul(O_ap[:, 512:1024], lhsT=V3[:, kt, :], rhs=P[:, 512:1024], start=st, stop=sp),
                    ]
                    mm_l = [lambda e: e.matmul(L_ap[:, 512 + DX:1024], lhsT=ones_b[:], rhs=P[:, 512 + DX:1024], start=st, stop=sp)]
                    if st:
                        emit_group(PE, mm_o, reads=[B_Vs[s], B_p[g % NPB]], writes=B_O)
                        emit_group(PE, mm_l, reads=[B_p[g % NPB], B_c2], writes=[B_L[1]])
                    else:
                        emit_group(PE, mm_o + mm_l, reads=[B_Vs[s], B_p[g % NPB], B_c2], writes=B_O + [B_L[1]])
                    if st:
                        emit(DVE, lambda e: e.tensor_copy(accL1[:], P[:, 0:512 + DX]), reads=[B_p[g % NPB]], writes=[B_acc])
                    else:
                        emit(DVE, lambda e: e.tensor_tensor(out=accL1[:], in0=accL1[:], in1=P[:, 0:512 + DX], op=ALU.add),
                             reads=[B_p[g % NPB], B_acc], writes=[B_acc])
                    if sp:
                        emit_group(PE, [lambda e: e.matmul(L_ap[:, 0:512], lhsT=ones_f[:], rhs=accL1[:, 0:512], start=True, stop=True)],
                                   reads=[B_acc, B_c2], writes=[B_L[0]])
                        emit_group(PE, [lambda e: e.matmul(L_ap[:, 512:512 + DX], lhsT=ones_f[:], rhs=accL1[:, 512:512 + DX], start=True, stop=True)],
                                   reads=[B_acc, B_c2], writes=[B_L[1]])

            def fin_evac(u, qc):
                kind, j = units[u]
                if kind == "A":
                    emit(DVE, lambda e: e.tensor_copy(o_f[0:65, :], O_ap[0:65, :]), reads=B_O, writes=[B_of])
                    emit(DVE, lambda e: e.reciprocal(out=l_f[64:65, :], in_=o_f[64:65, :]), reads=[B_of], writes=[B_lf])
                else:
                    emit(DVE, lambda e: e.tensor_copy(o_f[:], O_ap[:]), reads=B_O, writes=[B_of])
                    emit(DVE, lambda e: e.tensor_copy(l_f[:], L_ap[:]), reads=B_L, writes=[B_lf])

            def fin_B2():
                emit(ACT, lambda e: e.activation(out=lnl[:], in_=l_f[:], func=AF.Ln), reads=[B_lf], writes=[B_lnl])
                emit(ACT, lambda e: e.activation(out=l_f[:], in_=lnl[:], func=AF.Exp, scale=-1.0), reads=[B_lnl], writes=[B_lf])

            def fin_B3():
                emit(DVE, lambda e: e.tensor_tensor(out=dd[:, 0:512], in0=o_f[:, 0:512], in1=l_f[:, 0:512], op=ALU.mult),
                     reads=[B_of, B_lf], writes=[B_dd])
                emit(DVE, lambda e: e.scalar_tensor_tensor(out=dd[:, 512:1024], in0=o_f[:, 512:1024], scalar=neglam_col, in1=l_f[:, 512:1024],
                                                           op0=ALU.mult, op1=ALU.mult), reads=[B_of, B_lf, B_lam], writes=[B_dd])
                emit(DVE, lambda e: e.tensor_tensor(out=dd[:, 0:512], in0=dd[:, 0:512], in1=dd[:, 512:1024], op=ALU.add), reads=[B_dd], writes=[B_dd])
                emit(DVE, lambda e: e.tensor_tensor(out=sq[:], in0=dd[:, 0:512], in1=dd[:, 0:512], op=ALU.mult), reads=[B_dd], writes=[B_sq])

            def fin_pe(u, qc, gslot):
                kind, j = units[u]
                if kind == "A":
                    Sg = Sbuf[gslot] if gslot is not None else L_ap
                    B_Sg = B_S[gslot] if gslot is not None else B_L
                    emit_group(PE, [
                        lambda e: e.matmul(Sg[0:64, 0:512], lhsT=ones_f[64:65, 0:64], rhs=l_f[64:65, 0:512], start=True, stop=True),
                        lambda e: e.matmul(Sg[0:64, 512:1024], lhsT=ones_f[64:65, 0:64], rhs=l_f[64:65, 512:1024], start=True, stop=True),
                    ], reads=[B_lf, B_c2], writes=B_Sg)
                    for hh, col in ((j, 0), (j + 4, 512)):
                        emit(DVE, lambda e, hh=hh, col=col: e.tensor_tensor(out=oaT[:, hh, qc * 512:(qc + 1) * 512], in0=o_f[0:64, col:col + 512],
                                                                           in1=Sg[0:64, col:col + 512], op=ALU.mult),
                             reads=[B_of] + B_Sg, writes=[B_oaT])
                else:
                    Sg = Sbuf[gslot]
                    emit_group(PE, [lambda e: e.matmul(Sg[:, 0:512], lhsT=ones_f[:], rhs=sq[:], start=True, stop=True)],
                               reads=[B_sq, B_c2], writes=B_S[gslot])
                    emit(ACT, lambda e: e.activation(out=lnv[:], in_=Sg[:, 0:512], func=AF.Ln, scale=1.0 / 128, bias=eps_t[:, 0:1]),
                         reads=B_S[gslot] + [B_c2], writes=[B_lnv])
                    emit(ACT, lambda e: e.activation(out=rs_f[:], in_=lnv[:], func=AF.Exp, scale=-0.5), reads=[B_lnv], writes=[B_rs])
                    emit(DVE, lambda e: e.scalar_tensor_tensor(out=obT[:, j, qc * 512:(qc + 1) * 512], in0=dd[:, 0:512], scalar=gcol, in1=rs_f[:],
                                                               op0=ALU.mult, op1=ALU.mult), reads=[B_dd, B_rs, B_lam], writes=[B_obT])

            load_unit(0)
            load_unit(4)
            DEFER = 14
            free_S = [0, 1]
            Sphys = {}
            pending = []
            next_S = 0

            def issue_S(g):
                sl = free_S.pop(0)
                Sphys[g] = sl
                u, qc, kt = steps[g]
                kind, j = units[u]
                s = slot_of[u]
                blk = j if kind == "A" else 4 + j
                Sg = Sbuf[sl]
                q0 = qT_all[0:64, blk, qc * 512:(qc + 1) * 512]
                q1 = qT_all[64:128, blk, qc * 512:(qc + 1) * 512]
                emit_group(PE, [
                    lambda e: e.matmul(Sg[:, 0:512], lhsT=kTs[s][0:64, kt * 128:(kt + 1) * 128], rhs=q0, start=True, stop=True),
                    lambda e: e.matmul(Sg[:, 512:1024], lhsT=kTs[s][64:128, kt * 128:(kt + 1) * 128], rhs=q1, start=True, stop=True),
                ], reads=[B_kTs[s], B_U1], writes=B_S[sl])

            def _emit_exp(g):
                sl = Sphys[g]
                Sg = Sbuf[sl]
                emit(ACT, lambda e: e.activation(out=p_sb[g % NPB][:], in_=Sg[:], func=AF.Exp, scale=0.125),
                     reads=B_S[sl], writes=[B_p[g % NPB]])
                free_S.append(sl)

            def run_pending(g):
                pending.sort(key=lambda x: x[0])
                while pending and pending[0][0] <= g:
                    _, what, pu, pqc = pending.pop(0)
                    if what == "B2":
                        fin_B2()
                    elif what == "B3":
                        fin_B3()
                    elif units[pu][0] == "A" and not (pu == 3 and pqc == 3):
                        fin_pe(pu, pqc, None)
                    else:
                        sl = free_S.pop(0)
                        fin_pe(pu, pqc, sl)
                        free_S.append(sl)

            issue_S(0)
            issue_S(1)
            next_S = 2
            for g in range(NS):
                u, qc, kt = steps[g]
                _emit_exp(g)
                run_pending(g)
                while next_S < NS and next_S <= g + 2 and free_S:
                    issue_S(next_S)
                    next_S += 1
                emit_PV(g)
                if kt == 63:
                    fin_evac(u, qc)
                    if units[u][0] == "B":
                        pending.append((g + 3, "B2", u, qc))
                        pending.append((g + 5, "B3", u, qc))
                    pending.append((g + DEFER, "PE", u, qc))
                    if qc == 3:
                        if u == 3:
                            load_unit(5)
                        elif u == 4:
                            load_unit(6)
                        elif u == 5:
                            load_unit(7)
            run_pending(NS + DEFER + 1)
            B_p2_all = B_kTs + B_Vs + B_p + [B_of, B_lf, B_dd, B_sq, B_lnv, B_rs, B_acc, B_lnl]
        barrier(B_p2_all, ALL)

        if debug:
            dq = sb("dq", [128, 2048], F32)
            for b in range(8):
                emit(DVE, lambda e, b=b: e.tensor_copy(dq[:], qT_all[:, b, :]), reads=[B_U1], writes=[B_dbg])
                emit_dma(SP, B_dbg.dsem, dbg["qT"].ap()[:, b * NOWN:(b + 1) * NOWN], dq[:], reads=[B_dbg])
                emit(DVE, lambda e, b=b: e.tensor_copy(dq[0:64, :], oaT[:, b, :]), reads=[B_oaT], writes=[B_dbg])
                emit_dma(SP, B_dbg.dsem, dbg["oaT"].ap()[:, b * NOWN:(b + 1) * NOWN], dq[0:64, :], reads=[B_dbg])
            for b in range(4):
                emit(DVE, lambda e, b=b: e.tensor_copy(dq[:], obT[:, b, :]), reads=[B_obT], writes=[B_dbg])
                emit_dma(SP, B_dbg.dsem, dbg["obT"].ap()[:, b * NOWN:(b + 1) * NOWN], dq[:], reads=[B_dbg])

        h2T = U1
        pT3_ap = PS[3][:, 0:512].bitcast(BF16)
        B_pT3 = PSB[3][0]
        with ExitStack() as es3:
            Wpa = es3.enter_context(nc.sbuf_tensor("Wpa", [64, 8, D], BF16))
            Wpb = es3.enter_context(nc.sbuf_tensor("Wpb", [128, 4, D], BF16))
            Wout = es3.enter_context(nc.sbuf_tensor("Wout", [128, 8, D], BF16))
            B_Wpa, B_Wpb, B_Wout = mkbuf("Wpa"), mkbuf("Wpb"), mkbuf("Wout")
            emit_dma(POOL, B_Wpa.dsem, Wpa[:], wview(w_pa, 64), writes=[B_Wpa])
            emit_dma(POOL, B_Wpb.dsem, Wpb[:], wview(w_pb), writes=[B_Wpb])
            emit_dma(POOL, B_Wout.dsem, Wout[:], wview(w_out), writes=[B_Wout])
            N3 = 4
            xt = [es3.enter_context(nc.sbuf_tensor(f"p3xt{i}", [128, D], F32)) for i in range(N3)]
            B_xt = [mkbuf(f"p3xt{i}") for i in range(N3)]
            gt = [es3.enter_context(nc.sbuf_tensor(f"p3gt{i}", [128, 2048], BF16)) for i in range(N3)]
            B_gt = [mkbuf(f"p3gt{i}") for i in range(N3)]
            ta = es3.enter_context(nc.sbuf_tensor("p3ta", [128, D], F32))
            tb = es3.enter_context(nc.sbuf_tensor("p3tb", [128, D], F32))
            yb16 = [es3.enter_context(nc.sbuf_tensor(f"p3yb{i}", [128, D], BF16)) for i in range(2)]
            ypT = [es3.enter_context(nc.sbuf_tensor(f"p3ypT{i}", [128, 8, 128], BF16)) for i in range(2)]
            x2t = [es3.enter_context(nc.sbuf_tensor(f"p3x2{i}", [128, D], F32)) for i in range(2)]
            B_x2t = [mkbuf(f"p3x2{i}") for i in range(2)]
            junk = es3.enter_context(nc.sbuf_tensor("p3junk", [128, D], BF16))
            st3 = [es3.enter_context(nc.sbuf_tensor(f"p3st{i}", [128, 4], F32)) for i in range(2)]
            h2b = [es3.enter_context(nc.sbuf_tensor(f"p3h2b{i}", [128, D], BF16)) for i in range(2)]
            B_ta, B_tb, B_junk = Buf(), Buf(), Buf()
            B_yb, B_ypT, B_st3, B_h2b = [Buf(), Buf()], [Buf(), Buf()], [Buf(), Buf()], [Buf(), Buf()]
            B_x2scr = Buf()

            def load3(t):
                i = t % N3
                emit_dma(SP, B_xt[i].dsem, xt[i][:], x_own.ap()[t * 128:(t + 1) * 128, :], writes=[B_xt[i]])
                emit_dma(SP, B_gt[i].dsem, gt[i][:], g_scr.ap()[t * 128:(t + 1) * 128, :], reads=[B_gscr], writes=[B_gt[i]])

            def p3_A(t):
                i, i4 = t % 2, t % N3
                tok = slice(t * 128, (t + 1) * 128)
                YA, YB = PS[0], PS[1]
                fa, fb = [], []
                for h in range(8):
                    for hf in range(2):
                        fa.append(lambda e, h=h, hf=hf: e.matmul(YA[:, hf * 512:(hf + 1) * 512], lhsT=oaT[:, h, tok], rhs=Wpa[:, h, hf * 512:(hf + 1) * 512],
                                                                 start=(h == 0), stop=(h == 7)))
                for h in range(4):
                    for hf in range(2):
                        fb.append(lambda e, h=h, hf=hf: e.matmul(YB[:, hf * 512:(hf + 1) * 512], lhsT=obT[:, h, tok], rhs=Wpb[:, h, hf * 512:(hf + 1) * 512],
                                                                 start=(h == 0), stop=(h == 3)))
                emit_group(PE, fa, reads=[B_oaT, B_Wpa], writes=PSB[0])
                emit_group(PE, fb, reads=[B_obT, B_Wpb], writes=PSB[1])
                emit(DVE, lambda e: e.tensor_tensor(out=ta[:], in0=YA[:], in1=gt[i4][:, 0:1024], op=ALU.mult), reads=PSB[0] + [B_gt[i4]], writes=[B_ta])
                emit(DVE, lambda e: e.tensor_tensor(out=tb[:], in0=YB[:], in1=gt[i4][:, 1024:2048], op=ALU.mult), reads=PSB[1] + [B_gt[i4]], writes=[B_tb])
                emit(DVE, lambda e: e.tensor_tensor(out=yb16[i][:], in0=ta[:], in1=tb[:], op=ALU.add), reads=[B_ta, B_tb], writes=[B_yb[i]])

            def p3_A2(t):
                i = t % 2
                emit_group(PE, [(lambda e, c=c: e.transpose(pT3_ap[:, c * 128:(c + 1) * 128], yb16[i][:, c * 128:(c + 1) * 128], ident[:]))
                                for c in range(8)], reads=[B_yb[i], B_c2], writes=[B_pT3])
                emit(ACT, lambda e: e.activation(out=ypT[i][:].rearrange("p c n -> p (c n)"), in_=pT3_ap, func=AF.Copy), reads=[B_pT3], writes=[B_ypT[i]])

            def p3_B(t):
                i, i4 = t % 2, t % N3
                tok = slice(t * 128, (t + 1) * 128)
                XO = PS[2]
                fo = []
                for c in range(8):
                    for hf in range(2):
                        fo.append(lambda e, c=c, hf=hf: e.matmul(XO[:, hf * 512:(hf + 1) * 512], lhsT=ypT[i][:, c, :], rhs=Wout[:, c, hf * 512:(hf + 1) * 512],
                                                                 start=(c == 0), stop=(c == 7)))
                emit_group(PE, fo, reads=[B_ypT[i], B_Wout], writes=PSB[2])
                emit(DVE, lambda e: e.tensor_tensor(out=x2t[i][:], in0=XO[:], in1=xt[i4][:], op=ALU.add), reads=PSB[2] + [B_xt[i4]], writes=[B_x2t[i]])
                emit_dma(SP, B_x2t[i].dsem, x2_scr.ap()[tok, :], x2t[i][:], reads=[B_x2t[i]], writes=[B_x2scr])
                st = st3[i]
                emit(DVE, lambda e: e.memset(st[:, 0:1], 0.0), writes=[B_st3[i]])
                emit(ACT, lambda e: e.activation(out=junk[:], in_=x2t[i][:], func=AF.Square, accum_out=st[:, 0:1]), reads=[B_x2t[i], B_st3[i]], writes=[B_junk, B_st3[i]])
                emit(ACT, lambda e: e.activation(out=st[:, 1:2], in_=st[:, 0:1], func=AF.Ln, scale=1.0 / D, bias=eps_t[:, 0:1]), reads=[B_st3[i], B_c2], writes=[B_st3[i]])
                emit(ACT, lambda e: e.activation(out=st[:, 2:3], in_=st[:, 1:2], func=AF.Exp, scale=-0.5), reads=[B_st3[i]], writes=[B_st3[i]])
                emit(DVE, lambda e: e.scalar_tensor_tensor(out=h2b[i][:], in0=x2t[i][:], scalar=st[:, 2:3], in1=gffn_b, op0=ALU.mult, op1=ALU.mult),
                     reads=[B_x2t[i], B_st3[i], B_const], writes=[B_h2b[i]])

            def p3_B2(t):
                i = t % 2
                tok = slice(t * 128, (t + 1) * 128)
                emit_group(PE, [(lambda e, c=c: e.transpose(pKT_ap[:, c * 128:(c + 1) * 128], h2b[i][:, c * 128:(c + 1) * 128], ident[:]))
                                for c in range(8)], reads=[B_h2b[i], B_c2], writes=[B_pKT])
                emit(ACT, lambda e: e.activation(out=h2T[:, :, tok], in_=pKT_ap.rearrange("p (c n) -> p c n", c=8), func=AF.Copy), reads=[B_pKT], writes=[B_U1])

            pipeline(16, [load3, p3_A, p3_B, p3_A2, p3_B2], [0, 1, 3, 2, 4])
            B_p3_all = B_xt + B_gt + B_x2t + B_yb + B_ypT + B_st3 + B_h2b + [B_ta, B_tb, B_junk, B_Wpa, B_Wpb, B_Wout, B_oaT, B_obT]
        barrier(B_p3_all, ALL)

        aT = U2[:, 0:22 * 1024].rearrange("p (k n) -> p k n", k=22)
        B_aT = Buf()
        SCW = [512, 512, 512, 512, 512, 256]
        with ExitStack() as es4:
            Wd = es4.enter_context(nc.sbuf_tensor("Wd", [128, 22, D], BF16))
            B_Wd = mkbuf("Wd")
            Wgs = [es4.enter_context(nc.sbuf_tensor(f"Wgs{i}", [128, 8, 512], BF16)) for i in range(2)]
            Wus = [es4.enter_context(nc.sbuf_tensor(f"Wus{i}", [128, 8, 512], BF16)) for i in range(2)]
            B_Wgs = [mkbuf(f"Wgs{i}") for i in range(2)]
            B_Wus = [mkbuf(f"Wus{i}") for i in range(2)]
            ef = [es4.enter_context(nc.sbuf_tensor(f"p4e{i}", [128, 512], F32)) for i in range(2)]
            B_ef = [Buf(), Buf()]
            x2t = [es4.enter_context(nc.sbuf_tensor(f"p4x2{i}", [128, D], F32)) for i in range(2)]
            B_x2t = [mkbuf(f"p4x2{i}") for i in range(2)]
            x3 = es4.enter_context(nc.sbuf_tensor("p4x3", [128, D], F32))
            outt = [es4.enter_context(nc.sbuf_tensor(f"p4o{i}", [128, D], F32)) for i in range(2)]
            B_outt = [mkbuf(f"p4o{i}") for i in range(2)]
            junk = es4.enter_context(nc.sbuf_tensor("p4junk", [128, D], BF16))
            st4 = es4.enter_context(nc.sbuf_tensor("p4st", [128, 4], F32))
            B_x3, B_junk, B_st4 = Buf(), Buf(), Buf()
            B_out = Buf()

            sc_list = [(grp, sc) for grp in range(2) for sc in range(6)]

            def load_sc(idx):
                grp, sc = sc_list[idx]
                i = idx % 2
                w = SCW[sc]
                c0 = sc * 512
                emit_dma(POOL, B_Wgs[i].dsem, Wgs[i][:, :, 0:w], wview(w_gate)[:, :, c0:c0 + w], writes=[B_Wgs[i]])
                emit_dma(POOL, B_Wus[i].dsem, Wus[i][:, :, 0:w], wview(w_up)[:, :, c0:c0 + w], writes=[B_Wus[i]])

            load_sc(0)
            emit_dma(POOL, B_Wd.dsem, Wd[:], wview(w_down), writes=[B_Wd])
            pp = 0
            tile_ctr = 0
            for idx, (grp, sc) in enumerate(sc_list):
                i = idx % 2
                if idx + 1 < len(sc_list):
                    load_sc(idx + 1)
                for kk in range(SCW[sc] // 128):
                    k = sc * 4 + kk
                    for hf in range(2):
                        GU = PS[pp % 2]
                        B_gu = PSB[pp % 2]
                        e_t, B_e = ef[pp % 2], B_ef[pp % 2]
                        pp += 1
                        tk = slice(grp * 1024 + hf * 512, grp * 1024 + (hf + 1) * 512)
                        fg = []
                        for c in range(8):
                            fg.append(lambda e, c=c, GU=GU, kk=kk, i=i, tk=tk: e.matmul(GU[:, 0:512], lhsT=Wgs[i][:, c, kk * 128:(kk + 1) * 128], rhs=h2T[:, c, tk],
                                                                                      start=(c == 0), stop=(c == 7)))
                        for c in range(8):
                            fg.append(lambda e, c=c, GU=GU, kk=kk, i=i, tk=tk: e.matmul(GU[:, 512:1024], lhsT=Wus[i][:, c, kk * 128:(kk + 1) * 128], rhs=h2T[:, c, tk],
                                                                                      start=(c == 0), stop=(c == 7)))
                        emit_group(PE, fg, reads=[B_Wgs[i], B_Wus[i], B_U1], writes=B_gu)
                        emit(ACT, lambda e, GU=GU, e_t=e_t: e.activation(out=e_t[:], in_=GU[:, 0:512], func=AF.Silu), reads=B_gu, writes=[B_e])
                        emit(DVE, lambda e, e_t=e_t, GU=GU, k=k, hf=hf: e.tensor_tensor(out=aT[:, k, hf * 512:(hf + 1) * 512], in0=e_t[:], in1=GU[:, 512:1024], op=ALU.mult),
                             reads=[B_e] + B_gu, writes=[B_aT])
                if sc == 5:
                    for tt in range(8):
                        t = grp * 8 + tt
                        i2 = tile_ctr % 2
                        tile_ctr += 1
                        tok = slice(t * 128, (t + 1) * 128)
                        emit_dma(SP, B_x2t[i2].dsem, x2t[i2][:], x2_scr.ap()[tok, :], reads=[B_x2scr], writes=[B_x2t[i2]])
                        Y = PS[2 + tt % 2]
                        B_y = PSB[2 + tt % 2]
                        fd = []
                        for k in range(22):
                            for hf in range(2):
                                fd.append(lambda e, k=k, hf=hf, Y=Y, tt=tt: e.matmul(Y[:, hf * 512:(hf + 1) * 512], lhsT=aT[:, k, tt * 128:(tt + 1) * 128],
                                                                                   rhs=Wd[:, k, hf * 512:(hf + 1) * 512], start=(k == 0), stop=(k == 21)))
                        emit_group(PE, fd, reads=[B_aT, B_Wd], writes=B_y)
                        emit(DVE, lambda e, Y=Y, i2=i2: e.tensor_tensor(out=x3[:], in0=Y[:], in1=x2t[i2][:], op=ALU.add), reads=B_y + [B_x2t[i2]], writes=[B_x3])
                        emit(DVE, lambda e: e.memset(st4[:, 0:1], 0.0), writes=[B_st4])
                        emit(ACT, lambda e: e.activation(out=junk[:], in_=x3[:], func=AF.Square, accum_out=st4[:, 0:1]), reads=[B_x3, B_st4], writes=[B_junk, B_st4])
                        emit(ACT, lambda e: e.activation(out=st4[:, 1:2], in_=st4[:, 0:1], func=AF.Ln, scale=1.0 / D, bias=eps_t[:, 0:1]), reads=[B_st4, B_c2], writes=[B_st4])
                        emit(ACT, lambda e: e.activation(out=st4[:, 2:3], in_=st4[:, 1:2], func=AF.Exp, scale=-0.5), reads=[B_st4], writes=[B_st4])
                        emit(DVE, lambda e, i2=i2: e.scalar_tensor_tensor(out=outt[i2][:], in0=x3[:], scalar=st4[:, 2:3], in1=gfin_b, op0=ALU.mult, op1=ALU.mult),
                             reads=[B_x3, B_st4, B_const], writes=[B_outt[i2]])
                        emit_dma(SP, B_outt[i2].dsem, out.ap()[tok, :], outt[i2][:], reads=[B_outt[i2]], writes=[B_out])
            for sc_, v in B_out.w.items():
                SP.wait(sc_, v)
            if debug:
                for sc_, v in B_dbg.r.items():
                    SP.wait(sc_, v)
    return nc


_NC_CACHE = {}


def _rope_tab(ang_cs):
    c, s = ang_cs
    return np.concatenate([c, c], axis=1), np.concatenate([-s, s], axis=1)


def _rope_angles_like_reference():
    try:
        import jax
        import jax.numpy as jnp
        with jax.default_device(jax.devices("cpu")[0]):
            def rope_angles(pos, dim, theta):
                inv_freq = theta ** (-jnp.arange(0, dim, 2, dtype=jnp.float32) / dim)
                return pos[:, None] * inv_freq[None, :]
            rows = S // 64
            row = jnp.broadcast_to(jnp.arange(rows, dtype=jnp.float32)[:, None], (rows, 64)).reshape(-1)
            col = jnp.broadcast_to(jnp.arange(64, dtype=jnp.float32)[None, :], (rows, 64)).reshape(-1)
            ang_a = jnp.concatenate([rope_angles(row, 32, 10000.0), rope_angles(col, 32, 10000.0)], axis=-1)
            ang_1 = rope_angles(jnp.arange(S, dtype=jnp.float32), 64, 10000.0)
            out = [(np.asarray(jnp.cos(a), dtype=np.float32), np.asarray(jnp.sin(a), dtype=np.float32)) for a in (ang_a, ang_1)]
        return out[0], out[1]
    except Exception:
        pos = np.arange(S, dtype=np.float32)
        inv16 = (np.float32(10000.0) ** (-np.arange(0, 32, 2, dtype=np.float32) / np.float32(32))).astype(np.float32)
        inv32 = (np.float32(10000.0) ** (-np.arange(0, 64, 2, dtype=np.float32) / np.float32(64))).astype(np.float32)
        row = np.floor(pos / 64).astype(np.float32)
        col = (pos - row * 64).astype(np.float32)
        ang_a = np.concatenate([row[:, None] * inv16[None, :], col[:, None] * inv16[None, :]], axis=1).astype(np.float32)
        ang_1 = (pos[:, None] * inv32[None, :]).astype(np.float32)
        return (np.cos(ang_a), np.sin(ang_a)), (np.cos(ang_1), np.sin(ang_1))


def _host_inputs(inputs):
    f = lambda a: np.ascontiguousarray(np.asarray(a, dtype=np.float32))
    x = f(inputs["x"])
    w_in = f(inputs["w_in"])[0]
    w_kv = np.concatenate([w_in[:, 512:640], w_in[:, 640:768], w_in[:, 1280:1792], w_in[:, 1792:2304]], axis=1)
    order = [0, 4, 1, 5, 2, 6, 3, 7]
    aq = np.concatenate([w_in[:, h * 64:(h + 1) * 64] for h in order], axis=1)
    w_q = np.concatenate([aq, w_in[:, 768:1280]], axis=1)
    w_g = w_in[:, 2304:4352]
    vecs = np.concatenate([f(inputs["norm_mix"])[0], f(inputs["norm_ffn"])[0], f(inputs["norm_final"]),
                           f(inputs["q_norm_a"])[0], f(inputs["k_norm_a"])[0],
                           f(inputs["lambda_q1"])[0], f(inputs["lambda_k1"])[0], f(inputs["lambda_q2"])[0], f(inputs["lambda_k2"])[0]])[None, :]
    angA, ang1 = _rope_angles_like_reference()
    tcA, tsA = _rope_tab(angA)
    tc1, ts1 = _rope_tab(ang1)
    tab = np.ascontiguousarray(np.concatenate([tcA, tsA, tc1, ts1], axis=1).astype(np.float32))
    common = {
        "w_kv": np.ascontiguousarray(w_kv), "w_q": np.ascontiguousarray(w_q), "w_g": np.ascontiguousarray(w_g),
        "w_pa": f(inputs["w_proj_a"])[0], "w_pb": f(inputs["w_proj_b"])[0], "w_out": f(inputs["w_out"])[0],
        "w_gate": f(inputs["w_gate_ffn"])[0], "w_up": f(inputs["w_up_ffn"])[0], "w_down": f(inputs["w_down_ffn"])[0],
        "vecs": np.ascontiguousarray(vecs), "subln_col": np.ascontiguousarray(f(inputs["subln_b"])[0][:, None]),
        "ident": np.eye(128, dtype=np.float32), "tab_all": tab,
    }
    in_maps = []
    for c in range(8):
        b, qb = c // 4, c % 4
        m = dict(common)
        m["x_all"] = x[b]
        m["x_own"] = np.ascontiguousarray(x[b, qb * NOWN:(qb + 1) * NOWN])
        m["tab_own"] = np.ascontiguousarray(tab[qb * NOWN:(qb + 1) * NOWN])
        in_maps.append(m)
    return in_maps


def kernel(**inputs):
    in_maps = _host_inputs(inputs)
    if "nc" not in _NC_CACHE:
        _NC_CACHE["nc"] = build_program()
    nc = _NC_CACHE["nc"]
    res = run_bass_kernel_spmd(nc, in_maps, core_ids=list(range(8)))
    outp = np.empty((2, S, D), dtype=np.float32)
    for c in range(8):
        b, qb = c // 4, c % 4
        outp[b, qb * NOWN:(qb + 1) * NOWN] = res.results[c]["out"]
    return outp
```

```python
import math
from contextlib import ExitStack

import numpy as np
import concourse.bass as bass
import concourse.mybir as mybir
from concourse.bass_utils import run_bass_kernel_spmd

F32 = mybir.dt.float32
BF16 = mybir.dt.bfloat16
AF = mybir.ActivationFunctionType
ALU = mybir.AluOpType
AX = mybir.AxisListType

S = 8192
D = 1024
NOWN = 2048
DFF = 2816
EPS = 1e-6
LAMBDA_INIT = 0.8 - 0.6 * math.exp(-0.3 * 0)
NVEC = 3 * 1024 + 2 * 64 + 256


class SemC:
    def __init__(self, nc, es, name):
        self.sem = es.enter_context(nc.semaphore(name))
        self.cnt = 0


class Eng(SemC):
    def __init__(self, nc, es, name, eng):
        super().__init__(nc, es, name)
        self.e = eng
        self.seen = {}

    def wait(self, sc, v):
        if self.seen.get(sc, 0) >= v:
            return
        self.e.wait_ge(sc.sem, v)
        self.seen[sc] = v


class Buf:
    def __init__(self, nc=None, es=None, name=None):
        self.w = {}
        self.r = {}
        self.dsem = SemC(nc, es, "d_" + name) if nc is not None else None


STRICT = True


def _pre(E, reads, writes):
    for b in reads:
        for sc, v in b.w.items():
            E.wait(sc, v)
    for b in writes:
        for sc, v in b.w.items():
            if STRICT or sc is not E:
                E.wait(sc, v)
        for sc, v in b.r.items():
            if STRICT or sc is not E:
                E.wait(sc, v)


def emit(E, fn, reads=(), writes=()):
    _pre(E, reads, writes)
    ins = fn(E.e)
    E.cnt += 1
    ins.then_inc(E.sem, 1)
    for b in reads:
        b.r[E] = E.cnt
    for b in writes:
        b.w[E] = E.cnt


def emit_group(E, fns, reads=(), writes=()):
    _pre(E, reads, writes)
    ins = None
    for fn in fns:
        ins = fn(E.e)
    E.cnt += 1
    ins.then_inc(E.sem, 1)
    for b in reads:
        b.r[E] = E.cnt
    for b in writes:
        b.w[E] = E.cnt


def emit_dma(Q, dsem, out, in_, reads=(), writes=()):
    _pre(Q, reads, writes)
    Q.e.dma_start(out=out, in_=in_).then_inc(dsem.sem, 16)
    dsem.cnt += 16
    for b in reads:
        b.r[dsem] = dsem.cnt
    for b in writes:
        b.w[dsem] = dsem.cnt


def emit_dma_batch(Q, dsem, pairs, reads=(), writes=()):
    _pre(Q, reads, writes)
    for out, in_ in pairs:
        Q.e.dma_start(out=out, in_=in_).then_inc(dsem.sem, 16)
        dsem.cnt += 16
    for b in reads:
        b.r[dsem] = dsem.cnt
    for b in writes:
        b.w[dsem] = dsem.cnt


def bc_mid(ap2d, h):
    (ps, pn), (s, n) = ap2d.ap
    return bass.AP(ap2d.tensor, ap2d.offset, [[ps, pn], [0, h], [s, n]])


def bc_last(ap2d, n):
    (ps, pn), (s, h) = ap2d.ap
    return bass.AP(ap2d.tensor, ap2d.offset, [[ps, pn], [s, h], [0, n]])


def build_program(debug=False):
    nc = bass.Bass("TRN2", target_bir_lowering=False)

    def din(name, shape, dt=F32):
        return nc.dram_tensor(name, shape, dt, kind="ExternalInput")

    x_all = din("x_all", [S, D])
    x_own = din("x_own", [NOWN, D])
    tab_all = din("tab_all", [S, 256])
    tab_own = din("tab_own", [NOWN, 256])
    w_kv = din("w_kv", [D, 1280])
    w_q = din("w_q", [D, 1024])
    w_g = din("w_g", [D, 2048])
    w_pa = din("w_pa", [512, D])
    w_pb = din("w_pb", [512, D])
    w_out = din("w_out", [D, D])
    w_gate = din("w_gate", [D, DFF])
    w_up = din("w_up", [D, DFF])
    w_down = din("w_down", [DFF, D])
    vecs = din("vecs", [1, NVEC])
    subln_col = din("subln_col", [128, 1])
    ident_in = din("ident", [128, 128])
    out = nc.dram_tensor("out", [NOWN, D], F32, kind="ExternalOutput")
    kT_scr = [nc.dram_tensor(f"kT_scr{i}", [128, S], BF16, kind="Internal") for i in range(5)]
    vA_scr = nc.dram_tensor("vA_scr", [128, 64 * 130], BF16, kind="Internal")
    vB_scr = [nc.dram_tensor(f"vB_scr{i}", [128, 64 * 128], BF16, kind="Internal") for i in range(4)]
    g_scr = nc.dram_tensor("g_scr", [NOWN, 2048], BF16, kind="Internal")
    x2_scr = nc.dram_tensor("x2_scr", [NOWN, D], F32, kind="Internal")
    dbg = {}
    if debug:
        dbg["qT"] = nc.dram_tensor("dbg_qT", [128, 8 * NOWN], F32, kind="ExternalOutput")
        dbg["oaT"] = nc.dram_tensor("dbg_oaT", [64, 8 * NOWN], F32, kind="ExternalOutput")
        dbg["obT"] = nc.dram_tensor("dbg_obT", [128, 4 * NOWN], F32, kind="ExternalOutput")

    with ExitStack() as es:
        es.enter_context(nc.allow_low_precision("bf16 matmul operands, fp32 accumulation"))
        PE = Eng(nc, es, "s_pe", nc.tensor)
        ACT = Eng(nc, es, "s_act", nc.scalar)
        DVE = Eng(nc, es, "s_dve", nc.vector)
        POOL = Eng(nc, es, "s_pool", nc.gpsimd)
        SP = Eng(nc, es, "s_sp", nc.sync)

        def sb(name, shape, dt):
            return es.enter_context(nc.sbuf_tensor(name, shape, dt))

        def mkbuf(name):
            return Buf(nc, es, name)

        PS = [es.enter_context(nc.psum_tensor(f"ps{i}", [128, 1024], F32)) for i in range(4)]
        PSB = [[Buf(), Buf()] for _ in range(4)]

        vec_b = sb("vec_b", [128, NVEC], F32)
        ident_f = sb("ident_f", [128, 128], F32)
        ident = sb("ident_b", [128, 128], BF16)
        ones_f = sb("ones_f", [128, 128], F32)
        ones_b = sb("ones_b", [128, 128], BF16)
        eps_t = sb("eps_t", [128, 1], F32)
        cst = sb("cst", [128, 16], F32)
        U1 = sb("U1", [128, 8, NOWN], BF16)
        U2 = sb("U2", [128, 12 * NOWN], BF16)
        B_const = mkbuf("const")
        B_U1 = Buf()
        B_oaT = Buf()
        B_obT = Buf()
        gmix_b = vec_b[:, 0:1024]
        gffn_b = vec_b[:, 1024:2048]
        gfin_b = vec_b[:, 2048:3072]
        gq_b = vec_b[:, 3072:3136]
        gk_b = vec_b[:, 3136:3200]
        oaT = U2[0:64, 0:8 * NOWN].rearrange("p (h n) -> p h n", h=8)
        obT = U2[:, 8 * NOWN:12 * NOWN].rearrange("p (h n) -> p h n", h=4)

        vb_src = bass.AP(vecs, 0, [[0, 128], [1, NVEC]])
        emit_dma(SP, B_const.dsem, vec_b[:], vb_src, writes=[B_const])
        emit_dma(SP, B_const.dsem, ident_f[:], ident_in.ap(), writes=[B_const])
        emit_dma(SP, B_const.dsem, cst[:, 5:6], subln_col.ap(), writes=[B_const])
        B_c2 = Buf()
        emit(DVE, lambda e: e.memset(ones_f[:], 1.0), writes=[B_c2])
        emit(DVE, lambda e: e.memset(ones_b[:], 1.0), writes=[B_c2])
        emit(DVE, lambda e: e.memset(eps_t[:], EPS), writes=[B_c2])
        emit(DVE, lambda e: e.tensor_copy(ident[:], ident_f[:]), reads=[B_const], writes=[B_c2])
        lam_w = sb("lam_w", [128, 128], F32)
        B_lam = Buf()
        emit(DVE, lambda e: e.tensor_tensor(out=lam_w[:, 0:64], in0=vec_b[:, 3200:3264], in1=vec_b[:, 3264:3328], op=ALU.mult), reads=[B_const], writes=[B_lam])
        emit(DVE, lambda e: e.tensor_tensor(out=lam_w[:, 64:128], in0=vec_b[:, 3328:3392], in1=vec_b[:, 3392:3456], op=ALU.mult), reads=[B_const], writes=[B_lam])
        emit(DVE, lambda e: e.tensor_reduce(out=cst[:, 0:2], in_=lam_w[:].rearrange("p (a n) -> p a n", a=2), axis=AX.X, op=ALU.add), reads=[B_lam], writes=[B_lam])
        emit(ACT, lambda e: e.activation(out=cst[:, 2:4], in_=cst[:, 0:2], func=AF.Exp), reads=[B_lam], writes=[B_lam])
        emit(DVE, lambda e: e.tensor_tensor(out=cst[:, 4:5], in0=cst[:, 3:4], in1=cst[:, 2:3], op=ALU.subtract), reads=[B_lam], writes=[B_lam])
        emit(DVE, lambda e: e.tensor_scalar(out=cst[:, 4:5], in0=cst[:, 4:5], scalar1=-LAMBDA_INIT, scalar2=None, op0=ALU.add), reads=[B_lam], writes=[B_lam])
        emit(DVE, lambda e: e.tensor_scalar(out=cst[:, 5:6], in0=cst[:, 5:6], scalar1=1.0 - LAMBDA_INIT, scalar2=None, op0=ALU.mult), reads=[B_const, B_lam], writes=[B_lam])
        neglam_col = cst[:, 4:5]
        gcol = cst[:, 5:6]
        B_cst_all = [B_const, B_c2, B_lam]

        def wview(w, p=128):
            return w.ap().rearrange("(c p) n -> p c n", p=p)

        def pipeline(n, stages, skews):
            for s_ in range(n + max(skews)):
                for fn, sk in zip(stages, skews):
                    t_ = s_ - sk
                    if 0 <= t_ < n:
                        fn(t_)

        NXT, NXNT, NTAB = 4, 4, 8

        def phase_front(es2, pfx):
            r = {}
            r["xt"] = [es2.enter_context(nc.sbuf_tensor(pfx + f"xt{i}", [128, D], F32)) for i in range(NXT)]
            r["B_xt"] = [mkbuf(pfx + f"xt{i}") for i in range(NXT)]
            r["junk"] = es2.enter_context(nc.sbuf_tensor(pfx + "junk", [128, D], BF16))
            r["B_junk"] = Buf()
            r["st"] = [es2.enter_context(nc.sbuf_tensor(pfx + f"st{i}", [128, 4], F32)) for i in range(2)]
            r["B_st"] = [Buf(), Buf()]
            r["xn"] = [es2.enter_context(nc.sbuf_tensor(pfx + f"xn{i}", [128, D], BF16)) for i in range(2)]
            r["B_xn"] = [Buf(), Buf()]
            r["xnT"] = [es2.enter_context(nc.sbuf_tensor(pfx + f"xnT{i}", [128, 8, 128], BF16)) for i in range(NXNT)]
            r["B_xnT"] = [Buf() for _ in range(NXNT)]
            return r

        def rmsnorm_1(F, t, gain_b):
            xt, st = F["xt"][t % NXT], F["st"][t % 2]
            B_xt, B_st = F["B_xt"][t % NXT], F["B_st"][t % 2]
            xn, B_xn = F["xn"][t % 2], F["B_xn"][t % 2]
            emit(DVE, lambda e: e.memset(st[:, 0:1], 0.0), writes=[B_st])
            emit(ACT, lambda e: e.activation(out=F["junk"][:], in_=xt[:], func=AF.Square, accum_out=st[:, 0:1]),
                 reads=[B_xt, B_st], writes=[F["B_junk"], B_st])
            emit(ACT, lambda e: e.activation(out=st[:, 1:2], in_=st[:, 0:1], func=AF.Ln, scale=1.0 / D, bias=eps_t[:, 0:1]),
                 reads=[B_st, B_c2], writes=[B_st])
            emit(ACT, lambda e: e.activation(out=st[:, 2:3], in_=st[:, 1:2], func=AF.Exp, scale=-0.5),
                 reads=[B_st], writes=[B_st])
            emit(DVE, lambda e: e.scalar_tensor_tensor(out=xn[:], in0=xt[:], scalar=st[:, 2:3], in1=gain_b,
                                                       op0=ALU.mult, op1=ALU.mult),
                 reads=[B_xt, B_st, B_const], writes=[B_xn])

        def rmsnorm_2(F, t, pT_ap, B_pT):
            xn, B_xn = F["xn"][t % 2], F["B_xn"][t % 2]
            xnT, B_xnT = F["xnT"][t % NXNT], F["B_xnT"][t % NXNT]
            emit_group(PE, [(lambda e, c=c: e.transpose(pT_ap[:, c * 128:(c + 1) * 128], xn[:, c * 128:(c + 1) * 128], ident[:]))
                            for c in range(8)], reads=[B_xn, B_c2], writes=[B_pT])
            emit(ACT, lambda e: e.activation(out=xnT[:].rearrange("p c n -> p (c n)"), in_=pT_ap, func=AF.Copy),
                 reads=[B_pT], writes=[B_xnT])

        def rope(z_ap, H, tc, ts, t1, t2, B_z, B_tab, B_t, out_ap, B_out, rs=None, B_rs=None):
            z3 = z_ap.rearrange("p (h d) -> p h d", h=H)
            t13 = t1[:, 0:H * 64].rearrange("p (h d) -> p h d", h=H)
            t23 = t2[:, 0:H * 64].rearrange("p (h d) -> p h d", h=H)
            o3 = out_ap.rearrange("p (h d) -> p h d", h=H)
            B_t1, B_t2 = B_t
            emit(DVE, lambda e: e.tensor_tensor(out=t13, in0=z3, in1=bc_mid(tc, H), op=ALU.mult),
                 reads=B_z + [B_tab], writes=[B_t1])
            emit(DVE, lambda e: e.tensor_tensor(out=t23[:, :, 0:32], in0=z3[:, :, 32:64], in1=bc_mid(ts[:, 0:32], H), op=ALU.mult),
                 reads=B_z + [B_tab], writes=[B_t2[0]])
            emit(DVE, lambda e: e.tensor_tensor(out=t23[:, :, 32:64], in0=z3[:, :, 0:32], in1=bc_mid(ts[:, 32:64], H), op=ALU.mult),
                 reads=B_z + [B_tab], writes=[B_t2[1]])
            if rs is None:
                emit(DVE, lambda e: e.tensor_tensor(out=o3, in0=t13, in1=t23, op=ALU.add), reads=[B_t1] + B_t2, writes=[B_out])
            else:
                emit(DVE, lambda e: e.tensor_tensor(out=t23, in0=t13, in1=t23, op=ALU.add), reads=[B_t1] + B_t2, writes=B_t2)
                emit(DVE, lambda e: e.tensor_tensor(out=o3, in0=t23, in1=bc_last(rs, 64), op=ALU.mult),
                     reads=B_t2 + [B_rs], writes=[B_out])

        def head_rs(z_ap, H, sq, hs, B_z, B_sq, B_hs):
            emit(ACT, lambda e: e.activation(out=sq[:, 0:H * 64], in_=z_ap, func=AF.Square), reads=B_z, writes=[B_sq])
            emit(DVE, lambda e: e.tensor_reduce(out=hs[:, 0:H], in_=sq[:, 0:H * 64].rearrange("p (h d) -> p h d", h=H), axis=AX.X, op=ALU.add),
                 reads=[B_sq], writes=[B_hs])
            emit(ACT, lambda e: e.activation(out=hs[:, H:2 * H], in_=hs[:, 0:H], func=AF.Ln, scale=1.0 / 64, bias=eps_t[:, 0:1]),
                 reads=[B_hs, B_c2], writes=[B_hs])
            emit(ACT, lambda e: e.activation(out=hs[:, 2 * H:3 * H], in_=hs[:, H:2 * H], func=AF.Exp, scale=-0.5),
                 reads=[B_hs], writes=[B_hs])
            return hs[:, 2 * H:3 * H]

        pT_ap = PS[1][:, 512:1024].bitcast(BF16)
        B_pT = PSB[1][1]
        pKT_ap = PS[3][:, 512:1024].bitcast(BF16)
        B_pKT = PSB[3][1]

        def barrier(bufs, engines):
            for E in engines:
                for b in bufs:
                    for sc, v in list(b.w.items()) + list(b.r.items()):
                        if sc is not E:
                            E.wait(sc, v)

        ALL = [PE, ACT, DVE, POOL, SP]
        Wq = U2[:, 0:8192].rearrange("p (c n) -> p c n", c=8)
        Wg = U2[:, 8192:24576].rearrange("p (c n) -> p c n", c=8)
        B_Wq, B_Wg = mkbuf("Wq"), mkbuf("Wg")

        with ExitStack() as es1:
            F = phase_front(es1, "p1")
            Wkv = es1.enter_context(nc.sbuf_tensor("Wkv", [128, 8, 1280], BF16))
            B_Wkv = mkbuf("Wkv")
            emit_dma(POOL, B_Wkv.dsem, Wkv[:], wview(w_kv), writes=[B_Wkv])
            emit_dma(POOL, B_Wq.dsem, Wq, wview(w_q), writes=[B_Wq])
            emit_dma(POOL, B_Wg.dsem, Wg, wview(w_g), writes=[B_Wg])
            tabt = [es1.enter_context(nc.sbuf_tensor(f"tabt{i}", [128, 256], F32)) for i in range(NTAB)]
            B_tab = [mkbuf(f"tabt{i}") for i in range(NTAB)]
            t1 = es1.enter_context(nc.sbuf_tensor("t1", [128, 512], F32))
            t2 = es1.enter_context(nc.sbuf_tensor("t2", [128, 512], F32))
            sqs = es1.enter_context(nc.sbuf_tensor("sqs", [128, 512], F32))
            kg = es1.enter_context(nc.sbuf_tensor("kg", [128, 128], F32))
            hs = es1.enter_context(nc.sbuf_tensor("hs", [128, 24], F32))
            B_t = (Buf(), [Buf(), Buf()])
            B_kg, B_hs, B_sqs = Buf(), Buf(), Buf()
            kpost = [es1.enter_context(nc.sbuf_tensor(f"kpost{i}", [128, 640], BF16)) for i in range(2)]
            B_kpost = [[Buf(), Buf()] for _ in range(2)]
            kst = [es1.enter_context(nc.sbuf_tensor(f"kst{i}", [128, 5, 512], BF16)) for i in range(2)]
            B_kst = [mkbuf(f"kst{i}") for i in range(2)]
            vst = [es1.enter_context(nc.sbuf_tensor(f"vst{i}", [128, 4, 642], BF16)) for i in range(2)]
            B_vst = [mkbuf(f"vst{i}") for i in range(2)]
            B_kscr = [Buf(), Buf()]
            B_vscr = [Buf(), Buf()]
            for i in range(2):
                emit(DVE, lambda e, i=i: e.memset(vst[i][:], 1.0), writes=[B_vst[i]])

            def load1(t):
                emit_dma(SP, F["B_xt"][t % NXT].dsem, F["xt"][t % NXT][:], x_all.ap()[t * 128:(t + 1) * 128, :], writes=[F["B_xt"][t % NXT]])
                emit_dma(SP, B_tab[t % NTAB].dsem, tabt[t % NTAB][:], tab_all.ap()[t * 128:(t + 1) * 128, :], writes=[B_tab[t % NTAB]])

            def p1_A1(t):
                rmsnorm_1(F, t, gmix_b)

            def p1_A2(t):
                rmsnorm_2(F, t, pT_ap, B_pT)

            def p1_zs(t):
                zs = t % 2
                return PS[2 * zs], PS[2 * zs + 1], [PSB[2 * zs][0], PSB[2 * zs][1], PSB[2 * zs + 1][0]]

            def p1_B(t):
                Z0, Z1, B_z = p1_zs(t)
                xnT = F["xnT"][t % NXNT]
                fns = []
                for c in range(8):
                    fns.append(lambda e, c=c: e.matmul(Z0[:, 0:512], lhsT=xnT[:, c, :], rhs=Wkv[:, c, 0:512], start=(c == 0), stop=(c == 7)))
                    fns.append(lambda e, c=c: e.matmul(Z0[:, 512:1024], lhsT=xnT[:, c, :], rhs=Wkv[:, c, 512:1024], start=(c == 0), stop=(c == 7)))
                    fns.append(lambda e, c=c: e.matmul(Z1[:, 0:256], lhsT=xnT[:, c, :], rhs=Wkv[:, c, 1024:1280], start=(c == 0), stop=(c == 7)))
                emit_group(PE, fns, reads=[F["B_xnT"][t % NXNT], B_Wkv], writes=B_z)

            def p1_C1(t):
                Z0, Z1, B_z = p1_zs(t)
                i = t % 2
                tb_, B_tb_ = tabt[t % NTAB], B_tab[t % NTAB]
                g4, s4 = t // 4, (t // 4) % 2
                j4 = t % 4
                vrow = vst[s4][:, j4, :]
                emit(ACT, lambda e: e.activation(out=vrow[:, 0:130].rearrange("p (h d) -> p h d", h=2)[:, :, 0:64],
                                                 in_=Z0[:, 128:256].rearrange("p (h d) -> p h d", h=2), func=AF.Copy),
                     reads=[B_z[0]], writes=[B_vst[s4]])
                emit(ACT, lambda e: e.activation(out=vrow[:, 130:386], in_=Z0[:, 768:1024], func=AF.Copy), reads=[B_z[1]], writes=[B_vst[s4]])
                emit(ACT, lambda e: e.activation(out=vrow[:, 386:642], in_=Z1[:, 0:256], func=AF.Copy), reads=[B_z[2]], writes=[B_vst[s4]])
                rk = head_rs(Z0[:, 0:128], 2, sqs, hs, [B_z[0]], B_sqs, B_hs)
                emit(DVE, lambda e: e.tensor_tensor(out=kg[:].rearrange("p (h d) -> p h d", h=2), in0=Z0[:, 0:128].rearrange("p (h d) -> p h d", h=2),
                                                    in1=bc_mid(gk_b, 2), op=ALU.mult), reads=[B_z[0], B_const], writes=[B_kg])
                rope(Z0[:, 256:768], 8, tb_[:, 128:192], tb_[:, 192:256], t1, t2, [B_z[0], B_z[1]], B_tb_, B_t, kpost[i][:, 128:640], B_kpost[i][1])
                rope(kg[:], 2, tb_[:, 0:64], tb_[:, 64:128], t1, t2, [B_kg], B_tb_, B_t, kpost[i][:, 0:128], B_kpost[i][0], rs=rk, B_rs=B_hs)

            def p1_C2(t):
                i = t % 2
                g4, s4 = t // 4, (t // 4) % 2
                j4 = t % 4
                emit_group(PE, [(lambda e, b=b: e.transpose(pKT_ap[:, b * 128:(b + 1) * 128], kpost[i][:, b * 128:(b + 1) * 128], ident[:]))
                                for b in range(5)], reads=B_kpost[i] + [B_c2], writes=[B_pKT])
                emit(ACT, lambda e: e.activation(out=kst[s4][:, :, j4 * 128:(j4 + 1) * 128], in_=pKT_ap[:, 0:640].rearrange("p (b n) -> p b n", b=5), func=AF.Copy),
                     reads=[B_pKT], writes=[B_kst[s4]])
                if j4 == 3:
                    emit_dma_batch(SP, B_kst[s4].dsem,
                                   [(kT_scr[b].ap()[:, g4 * 512:(g4 + 1) * 512], kst[s4][:, b, :]) for b in range(5)],
                                   reads=[B_kst[s4]], writes=[B_kscr[g4 % 2]])
                    emit_dma_batch(SP, B_vst[s4].dsem,
                                   [(vA_scr.ap().rearrange("p (t c) -> p t c", c=130)[:, g4 * 4:(g4 + 1) * 4, :], vst[s4][:, :, 0:130])] +
                                   [(vB_scr[h].ap().rearrange("p (t c) -> p t c", c=128)[:, g4 * 4:(g4 + 1) * 4, :],
                                     vst[s4][:, :, 130 + h * 128:130 + (h + 1) * 128]) for h in range(4)],
                                   reads=[B_vst[s4]], writes=[B_vscr[g4 % 2]])

            qT_all = U1
            qg = es1.enter_context(nc.sbuf_tensor("qg", [128, 512], F32))
            ee = [es1.enter_context(nc.sbuf_tensor(f"ee{i}", [128, 512], F32)) for i in range(2)]
            B_ee = [Buf(), Buf()]
            B_qg = Buf()
            qpost = [es1.enter_context(nc.sbuf_tensor(f"qpost{i}", [128, 1024], BF16)) for i in range(2)]
            B_qpost = [[Buf(), Buf()] for _ in range(2)]
            gst = [es1.enter_context(nc.sbuf_tensor(f"gst{i}", [128, 2048], BF16)) for i in range(2)]
            B_gst = [mkbuf(f"gst{i}") for i in range(2)]
            B_gscr = Buf()
            Zs = [PS[0], PS[2]]
            B_Zs = [PSB[0], PSB[2]]
            Gs = [PS[1][:, 0:512], PS[3][:, 0:512]]
            B_Gs = [PSB[1][0], PSB[3][0]]

            def load1b(T):
                t = T - 64
                emit_dma(SP, F["B_xt"][T % NXT].dsem, F["xt"][T % NXT][:], x_own.ap()[t * 128:(t + 1) * 128, :], writes=[F["B_xt"][T % NXT]])
                emit_dma(SP, B_tab[T % NTAB].dsem, tabt[T % NTAB][:], tab_own.ap()[t * 128:(t + 1) * 128, :], writes=[B_tab[T % NTAB]])

            def p1b_A1(T):
                t = T - 64
                rmsnorm_1(F, T, gmix_b)

            def p1b_A2(T):
                t = T - 64
                rmsnorm_2(F, T, pT_ap, B_pT)

            def p1b_B(T):
                t = T - 64
                i = T % 2
                xnT = F["xnT"][T % NXNT]
                B_xnT = F["B_xnT"][T % NXNT]
                Z, B_z = Zs[i], B_Zs[i]
                fns = []
                for c in range(8):
                    fns.append(lambda e, c=c: e.matmul(Z[:, 0:512], lhsT=xnT[:, c, :], rhs=Wq[:, c, 0:512], start=(c == 0), stop=(c == 7)))
                    fns.append(lambda e, c=c: e.matmul(Z[:, 512:1024], lhsT=xnT[:, c, :], rhs=Wq[:, c, 512:1024], start=(c == 0), stop=(c == 7)))
                emit_group(PE, fns, reads=[B_xnT, B_Wq], writes=B_z)

            def p1b_Bg(T):
                t = T - 64
                i = T % 2
                xnT = F["xnT"][T % NXNT]
                B_xnT = F["B_xnT"][T % NXNT]
                for q in range(4):
                    G, B_g = Gs[q % 2], B_Gs[q % 2]
                    emit_group(PE, [(lambda e, c=c, q=q, G=G: e.matmul(G, lhsT=xnT[:, c, :], rhs=Wg[:, c, q * 512:(q + 1) * 512],
                                                                      start=(c == 0), stop=(c == 7))) for c in range(8)],
                               reads=[B_xnT, B_Wg], writes=[B_g])
                    e_t, B_e = ee[q % 2], B_ee[q % 2]
                    emit(ACT, lambda e, G=G, e_t=e_t: e.activation(out=e_t[:], in_=G, func=AF.Exp, scale=-1.0), reads=[B_g], writes=[B_e])
                    emit(ACT, lambda e, e_t=e_t: e.activation(out=e_t[:], in_=e_t[:], func=AF.Ln, bias=ones_f[:, 0:1]), reads=[B_e, B_c2], writes=[B_e])
                    emit(ACT, lambda e, e_t=e_t, q=q: e.activation(out=gst[i][:, q * 512:(q + 1) * 512], in_=e_t[:], func=AF.Exp, scale=-1.0),
                         reads=[B_e], writes=[B_gst[i]])
                emit_dma(SP, B_gst[i].dsem, g_scr.ap()[t * 128:(t + 1) * 128, :], gst[i][:], reads=[B_gst[i]], writes=[B_gscr])

            def p1b_C1(T):
                t = T - 64
                i = T % 2
                Z, B_z = Zs[i], B_Zs[i]
                tb_, B_tb_ = tabt[T % NTAB], B_tab[T % NTAB]
                rq = head_rs(Z[:, 0:512], 8, sqs, hs, [B_z[0]], B_sqs, B_hs)
                emit(DVE, lambda e: e.tensor_tensor(out=qg[:].rearrange("p (h d) -> p h d", h=8), in0=Z[:, 0:512].rearrange("p (h d) -> p h d", h=8),
                                                    in1=bc_mid(gq_b, 8), op=ALU.mult), reads=[B_z[0], B_const], writes=[B_qg])
                rope(Z[:, 512:1024], 8, tb_[:, 128:192], tb_[:, 192:256], t1, t2, [B_z[1]], B_tb_, B_t, qpost[i][:, 512:1024], B_qpost[i][1])
                rope(qg[:], 8, tb_[:, 0:64], tb_[:, 64:128], t1, t2, [B_qg], B_tb_, B_t, qpost[i][:, 0:512], B_qpost[i][0], rs=rq, B_rs=B_hs)

            def p1b_C2(T):
                t = T - 64
                i = T % 2
                emit_group(PE, [(lambda e, b=b: e.transpose(pKT_ap[:, b * 128:(b + 1) * 128], qpost[i][:, b * 128:(b + 1) * 128], ident[:]))
                                for b in range(8)], reads=B_qpost[i] + [B_c2], writes=[B_pKT])
                emit(DVE, lambda e: e.tensor_copy(qT_all[:, :, t * 128:(t + 1) * 128], pKT_ap.rearrange("p (b n) -> p b n", b=8)),
                     reads=[B_pKT], writes=[B_U1])

            def both(f_kv, f_q):
                return lambda T: (f_kv(T) if T < 64 else (f_q(T) if f_q is not None else None)) if (f_kv is not None or T >= 64) else None

            pipeline(80, [both(load1, load1b), both(p1_A1, p1b_A1), both(p1_C1, p1b_C1), both(p1_A2, p1b_A2), both(p1_C2, p1b_C2),
                          both(p1_B, p1b_B), both(None, p1b_Bg)], [0, 2, 6, 3, 7, 5, 5])
            B_p1_all = ([F["B_junk"], B_t[0], B_kg, B_hs, B_sqs, B_Wkv, B_qg, B_Wq, B_Wg] + B_t[1] + F["B_xt"] + F["B_st"] + F["B_xn"] + F["B_xnT"] + B_tab
                        + B_kpost[0] + B_kpost[1] + B_kst + B_vst + B_qpost[0] + B_qpost[1] + B_gst + B_ee)
        barrier(B_p1_all, ALL)
        if debug:
            B_dbg = mkbuf("dbg")

        with ExitStack() as es2:
            kTs = [es2.enter_context(nc.sbuf_tensor(f"kTs{i}", [128, S], BF16)) for i in range(2)]
            Vs = [es2.enter_context(nc.sbuf_tensor(f"Vs{i}", [128, 64 * 130], BF16)) for i in range(2)]
            B_kTs = [mkbuf(f"kTs{i}") for i in range(2)]
            B_Vs = [mkbuf(f"Vs{i}") for i in range(2)]
            NPB = 4
            p_sb = [es2.enter_context(nc.sbuf_tensor(f"p_sb{i}", [128, 1024], BF16)) for i in range(NPB)]
            B_p = [Buf() for _ in range(NPB)]
            o_f = es2.enter_context(nc.sbuf_tensor("o_f", [128, 1024], F32))
            l_f = es2.enter_context(nc.sbuf_tensor("l_f", [128, 1024], F32))
            dd = es2.enter_context(nc.sbuf_tensor("dd", [128, 1024], F32))
            sq = es2.enter_context(nc.sbuf_tensor("sq", [128, 512], F32))
            lnv = es2.enter_context(nc.sbuf_tensor("lnv", [128, 512], F32))
            rs_f = es2.enter_context(nc.sbuf_tensor("rs_f", [128, 512], F32))
            B_of, B_lf, B_dd, B_sq, B_lnv, B_rs = Buf(), Buf(), Buf(), Buf(), Buf(), Buf()
            DX = 128
            accL1 = es2.enter_context(nc.sbuf_tensor("accL1", [128, 512 + DX], F32))
            lnl = es2.enter_context(nc.sbuf_tensor("lnl", [128, 1024], F32))
            B_lnl = Buf()
            B_acc = Buf()

            units = [("A", j) for j in range(4)] + [("B", h) for h in range(4)]
            slot_of = {0: 0, 1: 0, 2: 0, 3: 0, 4: 1, 5: 0, 6: 1, 7: 0}

            def load_unit(u):
                kind, j = units[u]
                s = slot_of[u]
                if kind == "A":
                    if j != 0:
                        return
                    emit_dma(SP, B_kTs[s].dsem, kTs[s][:], kT_scr[0].ap(), reads=B_kscr, writes=[B_kTs[s]])
                    emit_dma(SP, B_Vs[s].dsem, Vs[s][:], vA_scr.ap(), reads=B_vscr, writes=[B_Vs[s]])
                else:
                    emit_dma(SP, B_kTs[s].dsem, kTs[s][:], kT_scr[1 + j].ap(), reads=B_kscr, writes=[B_kTs[s]])
                    emit_dma(SP, B_Vs[s].dsem, Vs[s][:, 0:64 * 128], vB_scr[j].ap(), reads=B_vscr, writes=[B_Vs[s]])

            steps = [(u, qc, kt) for u in range(8) for qc in range(4) for kt in range(64)]
            NS = len(steps)
            Sbuf = [PS[0], PS[1]]
            B_S = [PSB[0], PSB[1]]
            O_ap = PS[2]
            B_O = PSB[2]
            L_ap = PS[3]
            B_L = PSB[3]

            def emit_PV(g):
                u, qc, kt = steps[g]
                kind, j = units[u]
                s = slot_of[u]
                P = p_sb[g % NPB]
                st, sp = (kt == 0), (kt == 63)
                if kind == "A":
                    V3 = Vs[s][:].rearrange("p (t c) -> p t c", c=130)
                    emit_group(PE, [
                        lambda e: e.matmul(O_ap[0:65, 0:512], lhsT=V3[:, kt, 0:65], rhs=P[:, 0:512], start=st, stop=sp),
                        lambda e: e.matmul(O_ap[0:65, 512:1024], lhsT=V3[:, kt, 65:130], rhs=P[:, 512:1024], start=st, stop=sp),
                    ], reads=[B_Vs[s], B_p[g % NPB]], writes=B_O)
                else:
                    V3 = Vs[s][:, 0:64 * 128].rearrange("p (t c) -> p t c", c=128)
                    emit_group(PE, [
                        lambda e: e.matmul(O_ap[:, 0:512], lhsT=V3[:, kt, :], rhs=P[:, 0:512], start=st, stop=sp),
                        lambda e: e.matmul(O_ap[:, 512:1024], lhsT=V3[:, kt, :], rhs=P[:, 512:1024], start=st, stop=sp),
                        lambda e: e.matmul(L_ap[:, 512 + DX:1024], lhsT=ones_b[:], rhs=P[:, 512 + DX:1024], start=st, stop=sp),
                    ], reads=[B_Vs[s], B_p[g % NPB], B_c2], writes=B_O + [B_L[1]])
                    if st:
                        emit(DVE, lambda e: e.tensor_copy(accL1[:], P[:, 0:512 + DX]), reads=[B_p[g % NPB]], writes=[B_acc])
                    else:
                        emit(DVE, lambda e: e.tensor_tensor(out=accL1[:], in0=accL1[:], in1=P[:, 0:512 + DX], op=ALU.add),
                             reads=[B_p[g % NPB], B_acc], writes=[B_acc])
                    if sp:
                        emit_group(PE, [lambda e: e.matmul(L_ap[:, 0:512], lhsT=ones_f[:], rhs=accL1[:, 0:512], start=True, stop=True)],
                                   reads=[B_acc, B_c2], writes=[B_L[0]])
                        emit_group(PE, [lambda e: e.matmul(L_ap[:, 512:512 + DX], lhsT=ones_f[:], rhs=accL1[:, 512:512 + DX], start=True, stop=True)],
                                   reads=[B_acc, B_c2], writes=[B_L[1]])

            def fin_evac(u, qc):
                kind, j = units[u]
                if kind == "A":
                    emit(DVE, lambda e: e.tensor_copy(o_f[0:65, :], O_ap[0:65, :]), reads=B_O, writes=[B_of])
                    emit(DVE, lambda e: e.reciprocal(out=l_f[64:65, :], in_=o_f[64:65, :]), reads=[B_of], writes=[B_lf])
                else:
                    emit(DVE, lambda e: e.tensor_copy(o_f[:], O_ap[:]), reads=B_O, writes=[B_of])
                    emit(DVE, lambda e: e.tensor_copy(l_f[:], L_ap[:]), reads=B_L, writes=[B_lf])

            def fin_B2():
                emit(ACT, lambda e: e.activation(out=lnl[:], in_=l_f[:], func=AF.Ln), reads=[B_lf], writes=[B_lnl])
                emit(ACT, lambda e: e.activation(out=l_f[:], in_=lnl[:], func=AF.Exp, scale=-1.0), reads=[B_lnl], writes=[B_lf])

            def fin_B3():
                emit(DVE, lambda e: e.tensor_tensor(out=dd[:, 0:512], in0=o_f[:, 0:512], in1=l_f[:, 0:512], op=ALU.mult),
                     reads=[B_of, B_lf], writes=[B_dd])
                emit(DVE, lambda e: e.scalar_tensor_tensor(out=dd[:, 512:1024], in0=o_f[:, 512:1024], scalar=neglam_col, in1=l_f[:, 512:1024],
                                                           op0=ALU.mult, op1=ALU.mult), reads=[B_of, B_lf, B_lam], writes=[B_dd])
                emit(DVE, lambda e: e.tensor_tensor(out=dd[:, 0:512], in0=dd[:, 0:512], in1=dd[:, 512:1024], op=ALU.add), reads=[B_dd], writes=[B_dd])
                emit(DVE, lambda e: e.tensor_tensor(out=sq[:], in0=dd[:, 0:512], in1=dd[:, 0:512], op=ALU.mult), reads=[B_dd], writes=[B_sq])

            def fin_pe(u, qc, gslot):
                kind, j = units[u]
                if kind == "A":
                    Sg = Sbuf[gslot] if gslot is not None else L_ap
                    B_Sg = B_S[gslot] if gslot is not None else B_L
                    emit_group(PE, [
                        lambda e: e.matmul(Sg[0:64, 0:512], lhsT=ones_f[64:65, 0:64], rhs=l_f[64:65, 0:512], start=True, stop=True),
                        lambda e: e.matmul(Sg[0:64, 512:1024], lhsT=ones_f[64:65, 0:64], rhs=l_f[64:65, 512:1024], start=True, stop=True),
                    ], reads=[B_lf, B_c2], writes=B_Sg)
                    for hh, col in ((j, 0), (j + 4, 512)):
                        emit(DVE, lambda e, hh=hh, col=col: e.tensor_tensor(out=oaT[:, hh, qc * 512:(qc + 1) * 512], in0=o_f[0:64, col:col + 512],
                                                                           in1=Sg[0:64, col:col + 512], op=ALU.mult),
                             reads=[B_of] + B_Sg, writes=[B_oaT])
                else:
                    Sg = Sbuf[gslot]
                    emit_group(PE, [lambda e: e.matmul(Sg[:, 0:512], lhsT=ones_f[:], rhs=sq[:], start=True, stop=True)],
                               reads=[B_sq, B_c2], writes=B_S[gslot])
                    emit(ACT, lambda e: e.activation(out=lnv[:], in_=Sg[:, 0:512], func=AF.Ln, scale=1.0 / 128, bias=eps_t[:, 0:1]),
                         reads=B_S[gslot] + [B_c2], writes=[B_lnv])
                    emit(ACT, lambda e: e.activation(out=rs_f[:], in_=lnv[:], func=AF.Exp, scale=-0.5), reads=[B_lnv], writes=[B_rs])
                    emit(DVE, lambda e: e.scalar_tensor_tensor(out=obT[:, j, qc * 512:(qc + 1) * 512], in0=dd[:, 0:512], scalar=gcol, in1=rs_f[:],
                                                               op0=ALU.mult, op1=ALU.mult), reads=[B_dd, B_rs, B_lam], writes=[B_obT])

            load_unit(0)
            load_unit(4)
            DEFER = 14
            free_S = [0, 1]
            Sphys = {}
            pending = []
            next_S = 0

            def issue_S(g):
                sl = free_S.pop(0)
                Sphys[g] = sl
                u, qc, kt = steps[g]
                kind, j = units[u]
                s = slot_of[u]
                blk = j if kind == "A" else 4 + j
                Sg = Sbuf[sl]
                q0 = qT_all[0:64, blk, qc * 512:(qc + 1) * 512]
                q1 = qT_all[64:128, blk, qc * 512:(qc + 1) * 512]
                emit_group(PE, [
                    lambda e: e.matmul(Sg[:, 0:512], lhsT=kTs[s][0:64, kt * 128:(kt + 1) * 128], rhs=q0, start=True, stop=True),
                    lambda e: e.matmul(Sg[:, 512:1024], lhsT=kTs[s][64:128, kt * 128:(kt + 1) * 128], rhs=q1, start=True, stop=True),
                ], reads=[B_kTs[s], B_U1], writes=B_S[sl])

            def _emit_exp(g):
                sl = Sphys[g]
                Sg = Sbuf[sl]
                emit(ACT, lambda e: e.activation(out=p_sb[g % NPB][:], in_=Sg[:], func=AF.Exp, scale=0.125),
                     reads=B_S[sl], writes=[B_p[g % NPB]])
                free_S.append(sl)

            def run_pending(g):
                pending.sort(key=lambda x: x[0])
                while pending and pending[0][0] <= g:
                    _, what, pu, pqc = pending.pop(0)
                    if what == "B2":
                        fin_B2()
                    elif what == "B3":
                        fin_B3()
                    elif units[pu][0] == "A" and not (pu == 3 and pqc == 3):
                        fin_pe(pu, pqc, None)
                    else:
                        sl = free_S.pop(0)
                        fin_pe(pu, pqc, sl)
                        free_S.append(sl)

            issue_S(0)
            issue_S(1)
            next_S = 2
            for g in range(NS):
                u, qc, kt = steps[g]
                _emit_exp(g)
                run_pending(g)
                while next_S < NS and next_S <= g + 2 and free_S:
                    issue_S(next_S)
                    next_S += 1
                emit_PV(g)
                if kt == 63:
                    fin_evac(u, qc)
                    if units[u][0] == "B":
                        pending.append((g + 3, "B2", u, qc))
                        pending.append((g + 5, "B3", u, qc))
                    pending.append((g + DEFER, "PE", u, qc))
                    if qc == 3:
                        if u == 3:
                            load_unit(5)
                        elif u == 4:
                            load_unit(6)
                        elif u == 5:
                            load_unit(7)
            run_pending(NS + DEFER + 1)
            B_p2_all = B_kTs + B_Vs + B_p + [B_of, B_lf, B_dd, B_sq, B_lnv, B_rs, B_acc, B_lnl]
        barrier(B_p2_all, ALL)

        if debug:
            dq = sb("dq", [128, 2048], F32)
            for b in range(8):
                emit(DVE, lambda e, b=b: e.tensor_copy(dq[:], qT_all[:, b, :]), reads=[B_U1], writes=[B_dbg])
                emit_dma(SP, B_dbg.dsem, dbg["qT"].ap()[:, b * NOWN:(b + 1) * NOWN], dq[:], reads=[B_dbg])
                emit(DVE, lambda e, b=b: e.tensor_copy(dq[0:64, :], oaT[:, b, :]), reads=[B_oaT], writes=[B_dbg])
                emit_dma(SP, B_dbg.dsem, dbg["oaT"].ap()[:, b * NOWN:(b + 1) * NOWN], dq[0:64, :], reads=[B_dbg])
            for b in range(4):
                emit(DVE, lambda e, b=b: e.tensor_copy(dq[:], obT[:, b, :]), reads=[B_obT], writes=[B_dbg])
                emit_dma(SP, B_dbg.dsem, dbg["obT"].ap()[:, b * NOWN:(b + 1) * NOWN], dq[:], reads=[B_dbg])

        h2T = U1
        pT3_ap = PS[3][:, 0:512].bitcast(BF16)
        B_pT3 = PSB[3][0]
        with ExitStack() as es3:
            Wpa = es3.enter_context(nc.sbuf_tensor("Wpa", [64, 8, D], BF16))
            Wpb = es3.enter_context(nc.sbuf_tensor("Wpb", [128, 4, D], BF16))
            Wout = es3.enter_context(nc.sbuf_tensor("Wout", [128, 8, D], BF16))
            B_Wpa, B_Wpb, B_Wout = mkbuf("Wpa"), mkbuf("Wpb"), mkbuf("Wout")
            emit_dma(POOL, B_Wpa.dsem, Wpa[:], wview(w_pa, 64), writes=[B_Wpa])
            emit_dma(POOL, B_Wpb.dsem, Wpb[:], wview(w_pb), writes=[B_Wpb])
            emit_dma(POOL, B_Wout.dsem, Wout[:], wview(w_out), writes=[B_Wout])
            N3 = 4
            xt = [es3.enter_context(nc.sbuf_tensor(f"p3xt{i}", [128, D], F32)) for i in range(N3)]
            B_xt = [mkbuf(f"p3xt{i}") for i in range(N3)]
            gt = [es3.enter_context(nc.sbuf_tensor(f"p3gt{i}", [128, 2048], BF16)) for i in range(N3)]
            B_gt = [mkbuf(f"p3gt{i}") for i in range(N3)]
            ta = es3.enter_context(nc.sbuf_tensor("p3ta", [128, D], F32))
            tb = es3.enter_context(nc.sbuf_tensor("p3tb", [128, D], F32))
            yb16 = [es3.enter_context(nc.sbuf_tensor(f"p3yb{i}", [128, D], BF16)) for i in range(2)]
            ypT = [es3.enter_context(nc.sbuf_tensor(f"p3ypT{i}", [128, 8, 128], BF16)) for i in range(2)]
            x2t = [es3.enter_context(nc.sbuf_tensor(f"p3x2{i}", [128, D], F32)) for i in range(2)]
            B_x2t = [mkbuf(f"p3x2{i}") for i in range(2)]
            junk = es3.enter_context(nc.sbuf_tensor("p3junk", [128, D], BF16))
            st3 = [es3.enter_context(nc.sbuf_tensor(f"p3st{i}", [128, 4], F32)) for i in range(2)]
            h2b = [es3.enter_context(nc.sbuf_tensor(f"p3h2b{i}", [128, D], BF16)) for i in range(2)]
            B_ta, B_tb, B_junk = Buf(), Buf(), Buf()
            B_yb, B_ypT, B_st3, B_h2b = [Buf(), Buf()], [Buf(), Buf()], [Buf(), Buf()], [Buf(), Buf()]
            B_x2scr = Buf()

            def load3(t):
                i = t % N3
                emit_dma(SP, B_xt[i].dsem, xt[i][:], x_own.ap()[t * 128:(t + 1) * 128, :], writes=[B_xt[i]])
                emit_dma(SP, B_gt[i].dsem, gt[i][:], g_scr.ap()[t * 128:(t + 1) * 128, :], reads=[B_gscr], writes=[B_gt[i]])

            def p3_A(t):
                i, i4 = t % 2, t % N3
                tok = slice(t * 128, (t + 1) * 128)
                YA, YB = PS[0], PS[1]
                fa, fb = [], []
                for h in range(8):
                    for hf in range(2):
                        fa.append(lambda e, h=h, hf=hf: e.matmul(YA[:, hf * 512:(hf + 1) * 512], lhsT=oaT[:, h, tok], rhs=Wpa[:, h, hf * 512:(hf + 1) * 512],
                                                                 start=(h == 0), stop=(h == 7)))
                for h in range(4):
                    for hf in range(2):
                        fb.append(lambda e, h=h, hf=hf: e.matmul(YB[:, hf * 512:(hf + 1) * 512], lhsT=obT[:, h, tok], rhs=Wpb[:, h, hf * 512:(hf + 1) * 512],
                                                                 start=(h == 0), stop=(h == 3)))
                emit_group(PE, fa, reads=[B_oaT, B_Wpa], writes=PSB[0])
                emit_group(PE, fb, reads=[B_obT, B_Wpb], writes=PSB[1])
                emit(DVE, lambda e: e.tensor_tensor(out=ta[:], in0=YA[:], in1=gt[i4][:, 0:1024], op=ALU.mult), reads=PSB[0] + [B_gt[i4]], writes=[B_ta])
                emit(DVE, lambda e: e.tensor_tensor(out=tb[:], in0=YB[:], in1=gt[i4][:, 1024:2048], op=ALU.mult), reads=PSB[1] + [B_gt[i4]], writes=[B_tb])
                emit(DVE, lambda e: e.tensor_tensor(out=yb16[i][:], in0=ta[:], in1=tb[:], op=ALU.add), reads=[B_ta, B_tb], writes=[B_yb[i]])

            def p3_A2(t):
                i = t % 2
                emit_group(PE, [(lambda e, c=c: e.transpose(pT3_ap[:, c * 128:(c + 1) * 128], yb16[i][:, c * 128:(c + 1) * 128], ident[:]))
                                for c in range(8)], reads=[B_yb[i], B_c2], writes=[B_pT3])
                emit(ACT, lambda e: e.activation(out=ypT[i][:].rearrange("p c n -> p (c n)"), in_=pT3_ap, func=AF.Copy), reads=[B_pT3], writes=[B_ypT[i]])

            def p3_B(t):
                i, i4 = t % 2, t % N3
                tok = slice(t * 128, (t + 1) * 128)
                XO = PS[2]
                fo = []
                for c in range(8):
                    for hf in range(2):
                        fo.append(lambda e, c=c, hf=hf: e.matmul(XO[:, hf * 512:(hf + 1) * 512], lhsT=ypT[i][:, c, :], rhs=Wout[:, c, hf * 512:(hf + 1) * 512],
                                                                 start=(c == 0), stop=(c == 7)))
                emit_group(PE, fo, reads=[B_ypT[i], B_Wout], writes=PSB[2])
                emit(DVE, lambda e: e.tensor_tensor(out=x2t[i][:], in0=XO[:], in1=xt[i4][:], op=ALU.add), reads=PSB[2] + [B_xt[i4]], writes=[B_x2t[i]])
                emit_dma(SP, B_x2t[i].dsem, x2_scr.ap()[tok, :], x2t[i][:], reads=[B_x2t[i]], writes=[B_x2scr])
                st = st3[i]
                emit(DVE, lambda e: e.memset(st[:, 0:1], 0.0), writes=[B_st3[i]])
                emit(ACT, lambda e: e.activation(out=junk[:], in_=x2t[i][:], func=AF.Square, accum_out=st[:, 0:1]), reads=[B_x2t[i], B_st3[i]], writes=[B_junk, B_st3[i]])
                emit(ACT, lambda e: e.activation(out=st[:, 1:2], in_=st[:, 0:1], func=AF.Ln, scale=1.0 / D, bias=eps_t[:, 0:1]), reads=[B_st3[i], B_c2], writes=[B_st3[i]])
                emit(ACT, lambda e: e.activation(out=st[:, 2:3], in_=st[:, 1:2], func=AF.Exp, scale=-0.5), reads=[B_st3[i]], writes=[B_st3[i]])
                emit(DVE, lambda e: e.scalar_tensor_tensor(out=h2b[i][:], in0=x2t[i][:], scalar=st[:, 2:3], in1=gffn_b, op0=ALU.mult, op1=ALU.mult),
                     reads=[B_x2t[i], B_st3[i], B_const], writes=[B_h2b[i]])

            def p3_B2(t):
                i = t % 2
                tok = slice(t * 128, (t + 1) * 128)
                emit_group(PE, [(lambda e, c=c: e.transpose(pKT_ap[:, c * 128:(c + 1) * 128], h2b[i][:, c * 128:(c + 1) * 128], ident[:]))
                                for c in range(8)], reads=[B_h2b[i], B_c2], writes=[B_pKT])
                emit(ACT, lambda e: e.activation(out=h2T[:, :, tok], in_=pKT_ap.rearrange("p (c n) -> p c n", c=8), func=AF.Copy), reads=[B_pKT], writes=[B_U1])

            pipeline(16, [load3, p3_A, p3_B, p3_A2, p3_B2], [0, 1, 3, 2, 4])
            B_p3_all = B_xt + B_gt + B_x2t + B_yb + B_ypT + B_st3 + B_h2b + [B_ta, B_tb, B_junk, B_Wpa, B_Wpb, B_Wout, B_oaT, B_obT]
        barrier(B_p3_all, ALL)

        aT = U2[:, 0:22 * 1024].rearrange("p (k n) -> p k n", k=22)
        B_aT = Buf()
        SCW = [512, 512, 512, 512, 512, 256]
        with ExitStack() as es4:
            Wd = es4.enter_context(nc.sbuf_tensor("Wd", [128, 22, D], BF16))
            B_Wd = mkbuf("Wd")
            Wgs = [es4.enter_context(nc.sbuf_tensor(f"Wgs{i}", [128, 8, 512], BF16)) for i in range(2)]
            Wus = [es4.enter_context(nc.sbuf_tensor(f"Wus{i}", [128, 8, 512], BF16)) for i in range(2)]
            B_Wgs = [mkbuf(f"Wgs{i}") for i in range(2)]
            B_Wus = [mkbuf(f"Wus{i}") for i in range(2)]
            ef = [es4.enter_context(nc.sbuf_tensor(f"p4e{i}", [128, 512], F32)) for i in range(2)]
            B_ef = [Buf(), Buf()]
            x2t = [es4.enter_context(nc.sbuf_tensor(f"p4x2{i}", [128, D], F32)) for i in range(2)]
            B_x2t = [mkbuf(f"p4x2{i}") for i in range(2)]
            x3 = es4.enter_context(nc.sbuf_tensor("p4x3", [128, D], F32))
            outt = [es4.enter_context(nc.sbuf_tensor(f"p4o{i}", [128, D], F32)) for i in range(2)]
            B_outt = [mkbuf(f"p4o{i}") for i in range(2)]
            junk = es4.enter_context(nc.sbuf_tensor("p4junk", [128, D], BF16))
            st4 = es4.enter_context(nc.sbuf_tensor("p4st", [128, 4], F32))
            B_x3, B_junk, B_st4 = Buf(), Buf(), Buf()
            B_out = Buf()

            sc_list = [(grp, sc) for grp in range(2) for sc in range(6)]

            def load_sc(idx):
                grp, sc = sc_list[idx]
                i = idx % 2
                w = SCW[sc]
                c0 = sc * 512
                emit_dma(POOL, B_Wgs[i].dsem, Wgs[i][:, :, 0:w], wview(w_gate)[:, :, c0:c0 + w], writes=[B_Wgs[i]])
                emit_dma(POOL, B_Wus[i].dsem, Wus[i][:, :, 0:w], wview(w_up)[:, :, c0:c0 + w], writes=[B_Wus[i]])

            load_sc(0)
            WD_PIECES = [(0, 4), (4, 8), (8, 12), (12, 16), (16, 20), (20, 22)]
            pp = 0
            tile_ctr = 0
            for idx, (grp, sc) in enumerate(sc_list):
                i = idx % 2
                if idx + 1 < len(sc_list):
                    load_sc(idx + 1)
                if idx < len(WD_PIECES):
                    k0, k1 = WD_PIECES[idx]
                    emit_dma(POOL, B_Wd.dsem, Wd[:, k0:k1, :], wview(w_down)[:, k0:k1, :], writes=[B_Wd])
                for kk in range(SCW[sc] // 128):
                    k = sc * 4 + kk
                    for hf in range(2):
                        GU = PS[pp % 2]
                        B_gu = PSB[pp % 2]
                        e_t, B_e = ef[pp % 2], B_ef[pp % 2]
                        pp += 1
                        tk = slice(grp * 1024 + hf * 512, grp * 1024 + (hf + 1) * 512)
                        fg = []
                        for c in range(8):
                            fg.append(lambda e, c=c, GU=GU, kk=kk, i=i, tk=tk: e.matmul(GU[:, 0:512], lhsT=Wgs[i][:, c, kk * 128:(kk + 1) * 128], rhs=h2T[:, c, tk],
                                                                                      start=(c == 0), stop=(c == 7)))
                        for c in range(8):
                            fg.append(lambda e, c=c, GU=GU, kk=kk, i=i, tk=tk: e.matmul(GU[:, 512:1024], lhsT=Wus[i][:, c, kk * 128:(kk + 1) * 128], rhs=h2T[:, c, tk],
                                                                                      start=(c == 0), stop=(c == 7)))
                        emit_group(PE, fg, reads=[B_Wgs[i], B_Wus[i], B_U1], writes=B_gu)
                        emit(ACT, lambda e, GU=GU, e_t=e_t: e.activation(out=e_t[:], in_=GU[:, 0:512], func=AF.Silu), reads=B_gu, writes=[B_e])
                        emit(DVE, lambda e, e_t=e_t, GU=GU, k=k, hf=hf: e.tensor_tensor(out=aT[:, k, hf * 512:(hf + 1) * 512], in0=e_t[:], in1=GU[:, 512:1024], op=ALU.mult),
                             reads=[B_e] + B_gu, writes=[B_aT])
                if sc == 5:
                    for tt in range(8):
                        t = grp * 8 + tt
                        i2 = tile_ctr % 2
                        tile_ctr += 1
                        tok = slice(t * 128, (t + 1) * 128)
                        emit_dma(SP, B_x2t[i2].dsem, x2t[i2][:], x2_scr.ap()[tok, :], reads=[B_x2scr], writes=[B_x2t[i2]])
                        Y = PS[2 + tt % 2]
                        B_y = PSB[2 + tt % 2]
                        fd = []
                        for k in range(22):
                            for hf in range(2):
                                fd.append(lambda e, k=k, hf=hf, Y=Y, tt=tt: e.matmul(Y[:, hf * 512:(hf + 1) * 512], lhsT=aT[:, k, tt * 128:(tt + 1) * 128],
                                                                                   rhs=Wd[:, k, hf * 512:(hf + 1) * 512], start=(k == 0), stop=(k == 21)))
                        emit_group(PE, fd, reads=[B_aT, B_Wd], writes=B_y)
                        emit(DVE, lambda e, Y=Y, i2=i2: e.tensor_tensor(out=x3[:], in0=Y[:], in1=x2t[i2][:], op=ALU.add), reads=B_y + [B_x2t[i2]], writes=[B_x3])
                        emit(DVE, lambda e: e.memset(st4[:, 0:1], 0.0), writes=[B_st4])
                        emit(ACT, lambda e: e.activation(out=junk[:], in_=x3[:], func=AF.Square, accum_out=st4[:, 0:1]), reads=[B_x3, B_st4], writes=[B_junk, B_st4])
                        emit(ACT, lambda e: e.activation(out=st4[:, 1:2], in_=st4[:, 0:1], func=AF.Ln, scale=1.0 / D, bias=eps_t[:, 0:1]), reads=[B_st4, B_c2], writes=[B_st4])
                        emit(ACT, lambda e: e.activation(out=st4[:, 2:3], in_=st4[:, 1:2], func=AF.Exp, scale=-0.5), reads=[B_st4], writes=[B_st4])
                        emit(DVE, lambda e, i2=i2: e.scalar_tensor_tensor(out=outt[i2][:], in0=x3[:], scalar=st4[:, 2:3], in1=gfin_b, op0=ALU.mult, op1=ALU.mult),
                             reads=[B_x3, B_st4, B_const], writes=[B_outt[i2]])
                        emit_dma(SP, B_outt[i2].dsem, out.ap()[tok, :], outt[i2][:], reads=[B_outt[i2]], writes=[B_out])
            for sc_, v in B_out.w.items():
                SP.wait(sc_, v)
            if debug:
                for sc_, v in B_dbg.r.items():
                    SP.wait(sc_, v)
    return nc


_NC_CACHE = {}


def _rope_tab(ang_cs):
    c, s = ang_cs
    return np.concatenate([c, c], axis=1), np.concatenate([-s, s], axis=1)


def _rope_angles_like_reference():
    try:
        import jax
        import jax.numpy as jnp
        with jax.default_device(jax.devices("cpu")[0]):
            def rope_angles(pos, dim, theta):
                inv_freq = theta ** (-jnp.arange(0, dim, 2, dtype=jnp.float32) / dim)
                return pos[:, None] * inv_freq[None, :]
            rows = S // 64
            row = jnp.broadcast_to(jnp.arange(rows, dtype=jnp.float32)[:, None], (rows, 64)).reshape(-1)
            col = jnp.broadcast_to(jnp.arange(64, dtype=jnp.float32)[None, :], (rows, 64)).reshape(-1)
            ang_a = jnp.concatenate([rope_angles(row, 32, 10000.0), rope_angles(col, 32, 10000.0)], axis=-1)
            ang_1 = rope_angles(jnp.arange(S, dtype=jnp.float32), 64, 10000.0)
            out = [(np.asarray(jnp.cos(a), dtype=np.float32), np.asarray(jnp.sin(a), dtype=np.float32)) for a in (ang_a, ang_1)]
        return out[0], out[1]
    except Exception:
        pos = np.arange(S, dtype=np.float32)
        inv16 = (np.float32(10000.0) ** (-np.arange(0, 32, 2, dtype=np.float32) / np.float32(32))).astype(np.float32)
        inv32 = (np.float32(10000.0) ** (-np.arange(0, 64, 2, dtype=np.float32) / np.float32(64))).astype(np.float32)
        row = np.floor(pos / 64).astype(np.float32)
        col = (pos - row * 64).astype(np.float32)
        ang_a = np.concatenate([row[:, None] * inv16[None, :], col[:, None] * inv16[None, :]], axis=1).astype(np.float32)
        ang_1 = (pos[:, None] * inv32[None, :]).astype(np.float32)
        return (np.cos(ang_a), np.sin(ang_a)), (np.cos(ang_1), np.sin(ang_1))


def _host_inputs(inputs):
    f = lambda a: np.ascontiguousarray(np.asarray(a, dtype=np.float32))
    x = f(inputs["x"])
    w_in = f(inputs["w_in"])[0]
    w_kv = np.concatenate([w_in[:, 512:640], w_in[:, 640:768], w_in[:, 1280:1792], w_in[:, 1792:2304]], axis=1)
    order = [0, 4, 1, 5, 2, 6, 3, 7]
    aq = np.concatenate([w_in[:, h * 64:(h + 1) * 64] for h in order], axis=1)
    w_q = np.concatenate([aq, w_in[:, 768:1280]], axis=1)
    w_g = w_in[:, 2304:4352]
    vecs = np.concatenate([f(inputs["norm_mix"])[0], f(inputs["norm_ffn"])[0], f(inputs["norm_final"]),
                           f(inputs["q_norm_a"])[0], f(inputs["k_norm_a"])[0],
                           f(inputs["lambda_q1"])[0], f(inputs["lambda_k1"])[0], f(inputs["lambda_q2"])[0], f(inputs["lambda_k2"])[0]])[None, :]
    angA, ang1 = _rope_angles_like_reference()
    tcA, tsA = _rope_tab(angA)
    tc1, ts1 = _rope_tab(ang1)
    tab = np.ascontiguousarray(np.concatenate([tcA, tsA, tc1, ts1], axis=1).astype(np.float32))
    common = {
        "w_kv": np.ascontiguousarray(w_kv), "w_q": np.ascontiguousarray(w_q), "w_g": np.ascontiguousarray(w_g),
        "w_pa": f(inputs["w_proj_a"])[0], "w_pb": f(inputs["w_proj_b"])[0], "w_out": f(inputs["w_out"])[0],
        "w_gate": f(inputs["w_gate_ffn"])[0], "w_up": f(inputs["w_up_ffn"])[0], "w_down": f(inputs["w_down_ffn"])[0],
        "vecs": np.ascontiguousarray(vecs), "subln_col": np.ascontiguousarray(f(inputs["subln_b"])[0][:, None]),
        "ident": np.eye(128, dtype=np.float32), "tab_all": tab,
    }
    in_maps = []
    for c in range(8):
        b, qb = c // 4, c % 4
        m = dict(common)
        m["x_all"] = x[b]
        m["x_own"] = np.ascontiguousarray(x[b, qb * NOWN:(qb + 1) * NOWN])
        m["tab_own"] = np.ascontiguousarray(tab[qb * NOWN:(qb + 1) * NOWN])
        in_maps.append(m)
    return in_maps


def kernel(**inputs):
    in_maps = _host_inputs(inputs)
    if "nc" not in _NC_CACHE:
        _NC_CACHE["nc"] = build_program()
    nc = _NC_CACHE["nc"]
    res = run_bass_kernel_spmd(nc, in_maps, core_ids=list(range(8)))
    outp = np.empty((2, S, D), dtype=np.float32)
    for c in range(8):
        b, qb = c // 4, c % 4
        outp[b, qb * NOWN:(qb + 1) * NOWN] = res.results[c]["out"]
    return outp
```
